# Optimizing a Trainium2 kernel written in Bass

```python
import jax, jax.numpy as jnp
from jax import lax
import numpy as np

D_MODEL = 1024
BATCH = 4
SEQ = 4096
DEPTH = 1

CHUNK = 64
D_A = 1024
SGU_BLOCK = 128
SGU_GROUPS = 8
SGU_GROUP_DIM = D_A // SGU_GROUPS
D_B = 1024
CM_KERNEL = 31
D_FF = 2816
FFN_KERNEL = 3
EPS = 1e-6
SPLITS = (D_A, 2 * D_A, 2 * D_A + D_B, 2 * D_A + 2 * D_B, 2 * D_A + 2 * D_B + D_MODEL)
D_IN = 2 * D_A + 2 * D_B + 2 * D_MODEL

kernel_name = "hybrid_gmlp_conformer_gated_block"


def rmsnorm(x, g):
    xf = x.astype(jnp.float32)
    y = xf * lax.rsqrt(jnp.mean(xf * xf, axis=-1, keepdims=True) + EPS)
    return (y * g.astype(jnp.float32)).astype(x.dtype)


def layernorm(x, g, b):
    xf = x.astype(jnp.float32)
    mu = jnp.mean(xf, axis=-1, keepdims=True)
    xc = xf - mu
    y = xc * lax.rsqrt(jnp.mean(xc * xc, axis=-1, keepdims=True) + EPS)
    return (y * g.astype(jnp.float32) + b.astype(jnp.float32)).astype(x.dtype)


def causal_dwconv(x, w, b):
    k, c = w.shape
    y = lax.conv_general_dilated(
        x, w[:, None, :].astype(x.dtype), window_strides=(1,), padding=[(k - 1, 0)],
        dimension_numbers=("NWC", "WIO", "NWC"), feature_group_count=c)
    return y + b


def setup_inputs(seed: int = 0) -> dict:
    key = jax.random.key(seed)
    ks = jax.random.split(key, 24)
    f32 = jnp.float32
    nrm = lambda k, shape, scale: jax.random.normal(k, shape, f32) * scale
    L = DEPTH
    return {
        "x": jax.random.normal(ks[0], (BATCH, SEQ, D_MODEL), f32),
        "norm1_g": 1.0 + nrm(ks[1], (L, D_MODEL), 0.02),
        "w_in": nrm(ks[2], (L, D_MODEL, D_IN), D_MODEL ** -0.5),
        "sgu_ln_g": 1.0 + nrm(ks[3], (L, D_A), 0.02),
        "sgu_ln_b": nrm(ks[4], (L, D_A), 0.02),
        "sgu_w": nrm(ks[5], (L, SGU_GROUPS, SGU_BLOCK, SGU_BLOCK), SGU_BLOCK ** -0.5),
        "sgu_b": 1.0 + nrm(ks[6], (L, SGU_GROUPS, SGU_BLOCK), 0.02),
        "w_a_out": nrm(ks[7], (L, D_A, D_MODEL), D_A ** -0.5),
        "cm_dw_w": nrm(ks[8], (L, CM_KERNEL, D_B), CM_KERNEL ** -0.5),
        "cm_dw_b": nrm(ks[9], (L, D_B), 0.02),
        "cm_ln_g": 1.0 + nrm(ks[10], (L, D_B), 0.02),
        "cm_ln_b": nrm(ks[11], (L, D_B), 0.02),
        "w_b_out": nrm(ks[12], (L, D_B, D_MODEL), D_B ** -0.5),
        "w_o": nrm(ks[13], (L, D_MODEL, D_MODEL), D_MODEL ** -0.5),
        "norm2_g": 1.0 + nrm(ks[14], (L, D_MODEL), 0.02),
        "w_up": nrm(ks[15], (L, D_MODEL, 2 * D_FF), D_MODEL ** -0.5),
        "ffn_dw_w": nrm(ks[16], (L, FFN_KERNEL, 2 * D_FF), FFN_KERNEL ** -0.5),
        "ffn_dw_b": nrm(ks[17], (L, 2 * D_FF), 0.02),
        "w_down": nrm(ks[18], (L, D_FF, D_MODEL), D_FF ** -0.5),
        "normf_g": 1.0 + nrm(ks[19], (D_MODEL,), 0.02),
    }


def reference(x, norm1_g, w_in, sgu_ln_g, sgu_ln_b, sgu_w, sgu_b, w_a_out,
              cm_dw_w, cm_dw_b, cm_ln_g, cm_ln_b, w_b_out, w_o,
              norm2_g, w_up, ffn_dw_w, ffn_dw_b, w_down, normf_g):
    bsz, seq, _ = x.shape
    nblk = seq // SGU_BLOCK
    pos = jnp.arange(SGU_BLOCK) // CHUNK
    sgu_mask = pos[None, :] <= pos[:, None]

    for l in range(DEPTH):
        h = rmsnorm(x, norm1_g[l])
        p = h @ w_in[l]
        u_a, v_a, ga_b, gb_b, gate_a, gate_b = jnp.split(p, SPLITS, axis=-1)

        u_a = jax.nn.gelu(u_a, approximate=False)
        v_a = layernorm(jax.nn.gelu(v_a, approximate=False), sgu_ln_g[l], sgu_ln_b[l])
        vb = v_a.reshape(bsz, nblk, SGU_BLOCK, SGU_GROUPS, SGU_GROUP_DIM)
        ws = jnp.where(sgu_mask[None], sgu_w[l], 0.0).astype(vb.dtype)
        sv = jnp.einsum("gij,bnjgc->bnigc", ws, vb) + sgu_b[l].T[None, None, :, :, None]
        y_a = (u_a * sv.reshape(bsz, seq, D_A)) @ w_a_out[l]

        z = ga_b * jax.nn.sigmoid(gb_b)
        z = causal_dwconv(z, cm_dw_w[l], cm_dw_b[l])
        z = jax.nn.silu(layernorm(z, cm_ln_g[l], cm_ln_b[l]))
        y_b = z @ w_b_out[l]

        merged = jax.nn.sigmoid(gate_a) * y_a + jax.nn.sigmoid(gate_b) * y_b
        x = x + merged @ w_o[l]

        h2 = rmsnorm(x, norm2_g[l])
        up = causal_dwconv(h2 @ w_up[l], ffn_dw_w[l], ffn_dw_b[l])
        g_ff, v_ff = jnp.split(up, 2, axis=-1)
        x = x + (jax.nn.silu(g_ff) * v_ff) @ w_down[l]

    return rmsnorm(x, normf_g)
```

```python
from contextlib import ExitStack
import numpy as np
import concourse.bass as bass
import concourse.mybir as mybir
from concourse.bass_utils import run_bass_kernel_spmd

F32 = mybir.dt.float32
BF16 = mybir.dt.bfloat16
AF = mybir.ActivationFunctionType
ALU = mybir.AluOpType

NCORES = 8
D = 1024
SEQ_CORE = 2048
HALO = 128
NTOK = HALO + SEQ_CORE
NB = 1152
DFF = 2816
EPS = 1e-6
RING = 4

GRAN = 256
ENGS = ("pe", "act", "dve", "pool", "sp")


class Buf:
    def __init__(self, name, ap, off_bytes, es, K, C):
        self.name, self.ap, self.off, self.es, self.K, self.C = name, ap, off_bytes, es, K, C

    def rng(self, k0=0, k1=None, c0=0, c1=None):
        k1 = self.K if k1 is None else k1
        c1 = self.C if c1 is None else c1
        return [(self.off + (k * self.C + c0) * self.es, self.off + (k * self.C + c1) * self.es)
                for k in range(k0, k1)]


def gran_cells(ranges):
    cells = set()
    for a, b in ranges:
        cells.update(range(a // GRAN, (b - 1) // GRAN + 1))
    return cells


class Op:
    __slots__ = ("idx", "eng", "fn", "deps", "is_dma", "sem_key", "sem_val", "seqpos",
                 "marked", "count", "waits")

    def __init__(self):
        self.deps = {}
        self.is_dma = False
        self.marked = False
        self.count = 0
        self.waits = []
        self.sem_key = None
        self.sem_val = 0


class Sched:
    def __init__(self):
        self.ops = []
        self.last_w = {}
        self.readers = {}
        self.dma_counts = {}
        self.eng_len = {e: 0 for e in ENGS}

    @staticmethod
    def _cells(spec):
        cells = set()
        for item in spec:
            if isinstance(item, tuple):
                cells.add(item)
            else:
                cells |= gran_cells(item)
        return cells

    def add(self, eng, fn, reads=(), writes=(), dma_key=None):
        op = Op()
        op.idx = len(self.ops)
        op.eng = eng
        op.fn = fn
        op.seqpos = self.eng_len[eng]
        self.eng_len[eng] += 1
        if dma_key is not None:
            op.is_dma = True
            op.sem_key = dma_key
            self.dma_counts[dma_key] = self.dma_counts.get(dma_key, 0) + 16
            op.sem_val = self.dma_counts[dma_key]
        for c in self._cells(reads):
            w = self.last_w.get(c)
            if w is not None:
                op.deps[w] = "RAW"
            self.readers.setdefault(c, []).append(op.idx)
        for c in self._cells(writes):
            w = self.last_w.get(c)
            if w is not None and w != op.idx:
                op.deps.setdefault(w, "WAW")
            for r in self.readers.get(c, ()):
                if r != op.idx:
                    op.deps.setdefault(r, "WAR")
            self.readers[c] = []
            self.last_w[c] = op.idx
        self.ops.append(op)
        return op

    def finalize(self):
        known = {e: {f: -1 for f in ENGS} for e in ENGS}
        known_dma = {e: {} for e in ENGS}
        for op in self.ops:
            E = op.eng
            need = []
            best = {}
            for d, kind in op.deps.items():
                Dp = self.ops[d]
                if Dp.is_dma:
                    if known_dma[E].get(Dp.sem_key, 0) < Dp.sem_val:
                        need.append(("dma", Dp.sem_key, Dp.sem_val))
                        known_dma[E][Dp.sem_key] = Dp.sem_val
                    continue
                Fe = Dp.eng
                if Fe == E and not op.is_dma:
                    if kind != "RAW" or E == "pe":
                        continue
                if Dp.seqpos <= known[E][Fe]:
                    continue
                if Fe not in best or Dp.seqpos > best[Fe].seqpos:
                    best[Fe] = Dp
            for Fe, Dp in best.items():
                known[E][Fe] = Dp.seqpos
                Dp.marked = True
                need.append(("eng", Fe, Dp))
            op.waits = need
        cnt = {e: 0 for e in ENGS}
        for op in self.ops:
            if op.is_dma:
                continue
            if op.marked:
                cnt[op.eng] += 1
            op.count = cnt[op.eng]

    def emit(self, block, eng_sems, dma_sems):
        per = {e: [] for e in ENGS}
        for op in self.ops:
            per[op.eng].append(op)

        def run(eng_name, handle):
            for op in per[eng_name]:
                for w in op.waits:
                    if w[0] == "dma":
                        handle.wait_ge(dma_sems[w[1]], w[2])
                    else:
                        handle.wait_ge(eng_sems[w[1]], w[2].count)
                if op.fn is None:
                    continue
                ins = op.fn(handle)
                if op.is_dma:
                    ins.then_inc(dma_sems[op.sem_key], 16)
                elif op.marked:
                    ins.then_inc(eng_sems[op.eng], 1)

        @block.tensor
        def _(h):
            run("pe", h)

        @block.scalar
        def _(h):
            run("act", h)

        @block.vector
        def _(h):
            run("dve", h)

        @block.gpsimd
        def _(h):
            run("pool", h)

        @block.sync
        def _(h):
            run("sp", h)


C_G1, C_G2, C_GF, C_GAMA, C_CMB, C_CMG, C_CMBT = 0, 8, 16, 24, 32, 40, 48
C_CMW = 56
C_FDB = C_CMW + 8 * 31
C_FDW = C_FDB + 44
C_FLAG = C_FDW + 144
C_I32 = 496
C_ID = C_I32 + 32
NCA = C_ID + 128
B_SGW, B_MASK, B_BETA, B_SGB = 0, 1024, 1152, 2176
NCB = 3200

NGROUPS = 30
NCH_G = [4, 4, 4, 4, 4, 2]


def _chunk(W, col0):
    K = W.shape[0] // 128
    return W[:, col0:col0 + 128].reshape(K, 128, 128).transpose(1, 0, 2).reshape(128, K * 128)


def _col(v):
    n = v.shape[0] // 128
    return v.reshape(n, 128).T


def host_layout(inp):
    w_in = inp["w_in"][0]
    w_up = inp["w_up"][0]
    groups = []

    def grp(chs):
        groups.append(np.concatenate(chs, axis=1))

    pidx = np.arange(128)
    r_, i_ = pidx // 32, pidx % 32
    zperm = np.concatenate([(4 * (sl // 4) + r_) * 128 + 32 * (sl % 4) + i_ for sl in range(8)])
    w_ga = w_in[:, 2048:3072][:, zperm]
    w_gb = w_in[:, 3072:4096][:, zperm]
    for g in range(4):
        grp([_chunk(w_ga, 128 * (2 * g)), _chunk(w_gb, 128 * (2 * g)),
             _chunk(w_ga, 128 * (2 * g + 1)), _chunk(w_gb, 128 * (2 * g + 1))])
    for g in range(2):
        grp([_chunk(w_in, 128 * (4 * g + i)) for i in range(4)])
    for g in range(2):
        grp([_chunk(w_in, 1024 + 128 * (4 * g + i)) for i in range(4)])
    wa, wb, wo = inp["w_a_out"][0], inp["w_b_out"][0], inp["w_o"][0]
    for dc in range(8):
        grp([_chunk(w_in, 4096 + 128 * dc), _chunk(w_in, 5120 + 128 * dc),
             _chunk(wa, 128 * dc), _chunk(wb, 128 * dc)])
    for g in range(2):
        grp([_chunk(wo, 128 * (4 * g + i)) for i in range(4)])
    def ffperm(G, c):
        ch = (4 * G + r_) * 128 + 32 * c + i_
        valid = r_ < NCH_G[G]
        return ch, valid

    def up_slot(base, G, c):
        ch, valid = ffperm(G, c)
        W = np.zeros((D, 128), np.float32)
        W[:, valid] = w_up[:, base + ch[valid]]
        return _chunk(W, 0)

    for G in range(6):
        grp([up_slot(0, G, c) for c in range(4)])
        grp([up_slot(DFF, G, c) for c in range(4)])
    wst = np.ascontiguousarray(np.stack(groups, 0), dtype=np.float32)
    wd = inp["w_down"][0]
    wdn = np.ascontiguousarray(np.stack([_chunk(wd, 128 * dc) for dc in range(8)], 0),
                               dtype=np.float32)

    cA = np.zeros((128, NCA), np.float32)
    cA[:, C_G1:C_G1 + 8] = _col(inp["norm1_g"][0])
    cA[:, C_G2:C_G2 + 8] = _col(inp["norm2_g"][0])
    cA[:, C_GF:C_GF + 8] = _col(inp["normf_g"])
    cA[:, C_GAMA:C_GAMA + 8] = _col(inp["sgu_ln_g"][0])
    cA[:, C_CMB:C_CMB + 8] = _col(inp["cm_dw_b"][0])
    cA[:, C_CMG:C_CMG + 8] = _col(inp["cm_ln_g"][0])
    cA[:, C_CMBT:C_CMBT + 8] = _col(inp["cm_ln_b"][0])
    cmw = inp["cm_dw_w"][0]
    cA[:, C_CMW:C_CMW + 248] = cmw[:, zperm].reshape(31, 8, 128).transpose(2, 1, 0).reshape(128, 248)
    cA[:, C_FDB:C_FDB + 44] = _col(inp["ffn_dw_b"][0])
    fdw = inp["ffn_dw_w"][0]
    for half in range(2):
        for G in range(6):
            for c in range(4):
                ch, valid = ffperm(G, c)
                col = C_FDW + (half * 6 + G) * 12 + c * 3
                cA[valid, col:col + 3] = fdw[:, half * DFF + ch[valid]].T
    cA[pidx, C_I32 + pidx % 32] = 1.0
    cA[:, C_ID:C_ID + 128] = np.eye(128, dtype=np.float32)

    cB = np.zeros((128, NCB), np.float32)
    cB[:, B_SGW:B_SGW + 1024] = inp["sgu_w"][0].transpose(2, 0, 1).reshape(128, 1024)
    pos = np.arange(128) // 64
    cB[:, B_MASK:B_MASK + 128] = (pos[:, None] <= pos[None, :]).astype(np.float32)
    cB[:, B_BETA:B_BETA + 1024] = np.broadcast_to(inp["sgu_ln_b"][0][None, :], (128, 1024))
    cB[:, B_SGB:B_SGB + 1024] = np.broadcast_to(inp["sgu_b"][0].reshape(1, 1024), (128, 1024))
    return wst, wdn, cA, cB


def build_program():
    nc = bass.Bass("TRN2", target_bir_lowering=False)
    xT_d = nc.dram_tensor("xT", [D, NTOK], F32, kind="ExternalInput").ap()
    wst_d = nc.dram_tensor("wst", [NGROUPS, 128, 4096], F32, kind="ExternalInput").ap()
    wdn_d = nc.dram_tensor("wdn", [8, 128, DFF], F32, kind="ExternalInput").ap()
    cA_d = nc.dram_tensor("cA", [128, NCA], F32, kind="ExternalInput").ap()
    cB_d = nc.dram_tensor("cB", [128, NCB], F32, kind="ExternalInput").ap()
    out_d = nc.dram_tensor("outT", [D, SEQ_CORE], F32, kind="ExternalOutput").ap()
    xT_v = xT_d.rearrange("(k p) c -> p k c", p=128)
    out_v = out_d.rearrange("(k p) c -> p k c", p=128)

    S = Sched()
    with ExitStack() as es:
        ARENA_BYTES = 206 * 1024
        arena = es.enter_context(nc.sbuf_tensor("arena", [128, ARENA_BYTES // 2], BF16))
        ps = [es.enter_context(nc.psum_tensor(f"ps{i}", [128, 512], F32)) for i in range(8)]
        eng_sems = {e: es.enter_context(nc.semaphore(f"s_{e}")) for e in ENGS}
        dma_keys = ["x", "x0", "x1", "x2", "xs", "out", "cA", "cB"] + [("ring", s) for s in range(RING)]
        dma_sems = {k: es.enter_context(nc.semaphore("d_" + (k if isinstance(k, str) else f"ring{k[1]}")))
                    for k in dma_keys}
        block = es.enter_context(nc.Block())

        def view(off, dt, K, C, name):
            esz = 4 if dt == F32 else 2
            nb = K * C * esz
            assert off % 4 == 0 and off + nb <= ARENA_BYTES, (name, off, nb)
            v = arena[:, off // 2:(off + nb) // 2]
            if dt == F32:
                v = v.bitcast(F32)
            v = v.rearrange("p (k c) -> p k c", k=K)
            return Buf(name, v, off, esz, K, C)

        o = 0
        R_A = o; o += 8 * NB * 4
        R_H = o; o += 8 * NB * 2
        R_C = o; o += 24 * NB * 2
        R_DG = o; o += 16 * 128 * 4
        R_SQ = o; o += 8 * 512 * 2
        R_RING = o; o += RING * 8192
        R_TMP = o; o += 16384
        R_DF = o; o += 24 * 32 * 2
        R_UPB = o; o += 8 * 1032 * 2
        R_K = o
        A = view(R_A, F32, 8, NB, "A")
        NBLK = view(R_A, BF16, 8, 1024, "nblk")
        GV = view(R_A + 16384, F32, 2, 1024, "gv")
        DIAGB = view(R_A, BF16, 248, 32, "diagB")
        TN2 = view(R_A + 24576, F32, 4, 512, "tn2")
        HT = view(R_H, BF16, 8, NB, "hT")
        U = view(R_C, BF16, 8, NB, "u")
        Z = view(R_C + 8 * NB * 2, BF16, 8, NB, "z")
        Y = view(R_C + 16 * NB * 2, BF16, 8, NB, "y")
        HH = view(R_C, BF16, 22, 1024, "hh")
        CB = view(R_C + 16 * NB * 2, F32, 1, NCB, "cB")
        DG = view(R_DG, F32, 16, 128, "dg")
        SQ = view(R_SQ, BF16, 8, 512, "sq")
        R12 = view(R_TMP, F32, 4, 512, "r12")
        RINGB = [view(R_RING + s * 8192, BF16, 1, 4096, f"ring{s}") for s in range(RING)]
        SGT = view(R_TMP, F32, 4, 512, "sgt")
        T12 = view(R_TMP + 8192, F32, 4, 512, "t12")
        DIAGF = view(R_DF, BF16, 24, 32, "diagF")
        UPB = view(R_UPB, BF16, 8, 1032, "upb")
        XS = view(R_UPB, F32, 8, 512, "xs")
        k = R_K
        CA = view(k, F32, 1, NCA, "cA"); k += NCA * 4
        IDB = view(k, BF16, 1, 128, "identb"); k += 256
        ONESB = view(k, BF16, 1, 128, "onesb"); k += 256
        WGT = view(k, BF16, 8, 128, "wgT"); k += 2048
        TT = view(k, F32, 8, 128, "T"); k += 4096
        MHALF = view(k, F32, 1, 8, "mhalf"); k += 32
        SM = view(k, F32, 8, 32, "sm"); k += 1024
        SMH = view(k, F32, 2, 2, "smh"); k += 64
        HPROD = view(k, F32, 8, 31, "hprod"); k += 1024
        HY = view(k, F32, 8, 2, "hy"); k += 64
        HYF = view(k, F32, 8, 2, "hyf"); k += 64
        HYH = view(k, BF16, 8, 2, "hyh"); k += 32
        HYL = view(k, BF16, 8, 2, "hyl"); k += 32
        I32B = view(k, BF16, 1, 32, "i32b"); k += 64
        ONESF = view(k, F32, 1, 128, "onesf"); k += 512
        ZC = view(k, BF16, 8, 32, "zc"); k += 512
        UPC = view(k, BF16, 48, 2, "upc"); k += 256
        STS = view(k, F32, 2, 12, "sts"); k += 128
        MV = view(k, F32, 2, 4, "mv"); k += 64
        assert k <= ARENA_BYTES, k

        def cacol(c):
            return CA.ap[:, 0, c:c + 1]

        bank_ctr = [0]
        reserved = set()

        def nb():
            while True:
                b = bank_ctr[0] % 8
                bank_ctr[0] += 1
                if b not in reserved:
                    return b

        def pe_mm(out, lhsT, rhs, start, stop, reads, bank, tp=None):
            if tp is None:
                S.add("pe", lambda h: h.matmul(out, lhsT=lhsT, rhs=rhs, start=start, stop=stop),
                      reads=reads, writes=[("ps", bank)])
            else:
                S.add("pe", lambda h: h.matmul(out, lhsT=lhsT, rhs=rhs, start=start, stop=stop, tile_position=tp),
                      reads=reads, writes=[("ps", bank)])

        def act(out, in_, func, reads, writes, scale=None, bias=None):
            kw = {}
            if scale is not None:
                kw["scale"] = scale
            if bias is not None:
                kw["bias"] = bias
            S.add("act", lambda h: h.activation(out=out, in_=in_, func=func, **kw), reads=reads, writes=writes)

        def tt(eng, out, in0, in1, op, reads, writes):
            S.add(eng, lambda h: h.tensor_tensor(out=out, in0=in0, in1=in1, op=op), reads=reads, writes=writes)

        def ts(eng, out, in0, s1, s2, op0, op1, reads, writes):
            if s2 is None:
                S.add(eng, lambda h: h.tensor_scalar(out=out, in0=in0, scalar1=s1, scalar2=None, op0=op0),
                      reads=reads, writes=writes)
            else:
                S.add(eng, lambda h: h.tensor_scalar(out=out, in0=in0, scalar1=s1, scalar2=s2, op0=op0, op1=op1),
                      reads=reads, writes=writes)

        def stt(eng, out, in0, scalar, in1, op0, op1, reads, writes):
            S.add(eng, lambda h: h.scalar_tensor_tensor(out=out, in0=in0, scalar=scalar, in1=in1, op0=op0, op1=op1),
                  reads=reads, writes=writes)

        def cp(eng, out, in_, reads, writes):
            S.add(eng, lambda h: h.tensor_copy(out=out, in_=in_), reads=reads, writes=writes)

        def memset(eng, buf, val):
            S.add(eng, lambda h: h.memset(buf.ap[:, :, :], val), writes=[buf.rng()])

        stream = []
        for p in range(2):
            for g in range(NGROUPS):
                stream.append((wst_d[g], 4096))
            for rep in range(1 if p == 0 else 2):
                for dc in range(8):
                    stream.append((wdn_d[dc], DFF))
        issued = [0]

        def ring_issue(upto):
            while issued[0] <= min(upto, len(stream) - 1):
                L = issued[0]
                s = L % RING
                src, w = stream[L]
                dst = RINGB[s].ap[:, 0, 0:w]
                S.add("pool", (lambda dst, src: lambda h: h.dma_start(out=dst, in_=src))(dst, src),
                      writes=[RINGB[s].rng()], dma_key=("ring", s))
                issued[0] += 1

        load_ctr = [0]

        def ring_next(prefetch=RING - 1):
            L = load_ctr[0]
            load_ctr[0] += 1
            ring_issue(L + prefetch)
            return RINGB[L % RING]

        ring_issue(1)
        S.add("sp", lambda h: h.dma_start(out=CA.ap[:, 0, :], in_=cA_d), writes=[CA.rng()], dma_key="cA")
        memset("pool", ONESB, 1.0 / 1024.0)
        memset("pool", MHALF, -0.5)
        memset("pool", ONESF, 1.0)
        cp("dve", IDB.ap[:, 0, :], CA.ap[:, 0, C_ID:C_ID + 128], [CA.rng()], [IDB.rng()])
        cp("dve", I32B.ap[:, 0, :], CA.ap[:, 0, C_I32:C_I32 + 32], [CA.rng()], [I32B.rng()])
        def setup_T():
            sgw = CB.ap[:, 0, B_SGW:B_SGW + 1024].rearrange("p (g i) -> p g i", g=8)
            maskb = CB.ap[:, 0, B_MASK:B_MASK + 128].unsqueeze(1).to_broadcast([128, 8, 128])
            tt("dve", sgw, sgw, maskb, ALU.mult, [CB.rng()], [CB.rng()])
            cp("dve", WGT.ap[:, :, :], sgw, [CB.rng()], [WGT.rng()])
            betaB = CB.ap[:, 0, B_BETA:B_BETA + 1024].rearrange("p (g i) -> p g i", g=8)
            sgb = CB.ap[:, 0, B_SGB:B_SGB + 1024].rearrange("p (g i) -> p g i", g=8)
            for g in range(8):
                b = nb()
                pe_mm(ps[b][:, 0:128], betaB[:, g, :], sgw[:, g, :], True, False, [CB.rng()], b)
                pe_mm(ps[b][:, 0:128], ONESF.ap[0:1, 0, :], sgb[0:1, g, :], False, True, [CB.rng(), ONESF.rng()], b)
                cp("dve", TT.ap[:, g, :], ps[b][:, 0:128], [("ps", b)], [TT.rng(g, g + 1)])

        def token_stats(srcfn, nstat, n, b):
            nq = (n + 127) // 128
            for q in range(nq):
                m = min(128, n - 128 * q)
                for s_ in range(nstat):
                    col = q * nstat + s_
                    for k in range(8):
                        ap_, rd = srcfn(s_, k, 128 * q, m)
                        pe_mm(ps[b][0:m, col:col + 1], ap_, ONESB.ap[:, 0, 0:1], k == 0, k == 7,
                              [ONESB.rng(), rd], b)

        dg_ctr = [0]

        def bcast_diag(colbuf, par, col0, n):
            nq = (n + 127) // 128
            mm = min(128, n)
            slot = dg_ctr[0] % 4
            dg_ctr[0] += 1
            dgv = DG.ap[0:mm, slot * 4:slot * 4 + nq, :]
            idb = CA.ap[0:mm, 0, C_ID:C_ID + 128].unsqueeze(1).to_broadcast([mm, nq, 128])
            cb = colbuf.ap[0:mm, par, col0:col0 + nq].unsqueeze(2).to_broadcast([mm, nq, 128])
            tt("dve", dgv, idb, cb, ALU.mult, [CA.rng(), colbuf.rng(par, par + 1, col0, col0 + nq)],
               [DG.rng(slot * 4, slot * 4 + nq)])
            return slot

        def bcast_mm(slot, n):
            nq = (n + 127) // 128
            b2 = nb()
            for q in range(nq):
                m = min(128, n - 128 * q)
                pe_mm(ps[b2][:, 128 * q:128 * q + m], ONESF.ap[0:m, 0, :], DG.ap[0:m, slot * 4 + q, 0:m], True, True,
                      [ONESF.rng(), DG.rng(slot * 4 + q, slot * 4 + q + 1)], b2)
            return b2

        sm_ctr = [0]

        def rms_a1(c0, n, src=None, sc0=None):
            src = A if src is None else src
            sc0 = c0 if sc0 is None else sc0
            act(SQ.ap[:, :, 0:n], src.ap[:, :, sc0:sc0 + n], AF.Square, [src.rng(c0=sc0, c1=sc0 + n)], [SQ.rng(c0=0, c1=n)])

        def rms_a2(c0, n):
            b = nb()
            token_stats(lambda s_, k, t0, m: (SQ.ap[:, k, t0:t0 + m], SQ.rng(k, k + 1, t0, t0 + m)), 1, n, b)
            nq = (n + 127) // 128
            mm = min(128, n)
            par = sm_ctr[0] % 8
            sm_ctr[0] += 1
            ts("dve", SM.ap[0:mm, par, 0:nq], ps[b][0:mm, 0:nq], EPS, None, ALU.add, None, [("ps", b)], [SM.rng(par, par + 1, 0, nq)])
            tt("pool", SM.ap[0:mm, par, 4:4 + nq], SM.ap[0:mm, par, 0:nq], MHALF.ap[0:mm, 0, 0:nq], ALU.pow,
               [SM.rng(par, par + 1, 0, nq), MHALF.rng()], [SM.rng(par, par + 1, 4, 4 + nq)])
            return bcast_diag(SM, par, 4, n)

        def rms_b(gc, dst, c0, n, slot, src=None, sc0=None):
            src = A if src is None else src
            sc0 = c0 if sc0 is None else sc0
            b2 = bcast_mm(slot, n)
            for k in range(8):
                stt("dve", dst.ap[:, k, c0:c0 + n], src.ap[:, k, sc0:sc0 + n], cacol(gc + k), ps[b2][:, 0:n],
                    ALU.mult, ALU.mult,
                    [src.rng(k, k + 1, sc0, sc0 + n), CA.rng(), ("ps", b2)],
                    [dst.rng(k, k + 1, c0, c0 + n)])

        def rmsnorm_tile(gc, dst, c0, n):
            rms_a1(c0, n)
            slot = rms_a2(c0, n)
            rms_b(gc, dst, c0, n, slot)

        deferred = []

        def flush_deferred(nmax=1):
            for _ in range(nmax):
                if deferred:
                    deferred.pop(0)()

        def ring4(slot):
            return slot.ap[:, 0, :].rearrange("p (c k m) -> p c k m", c=4, k=8)

        for p in range(2):
            MT = [(128, 512), (640, 512)]
            ET = ([(126, 2)] if p == 0 else []) + MT
            ZT = ([(96, 32)] if p == 0 else []) + MT
            XT = [MT[0], (0, 128), MT[1]] if p == 0 else MT
            lo = 0 if p == 0 else 128
            doff = 1024 * p

            def load_x(lo=lo, doff=doff):
                S.add("sp", lambda h, lo=lo, doff=doff: h.dma_start(out=A.ap[:, :, lo:NB], in_=xT_v[:, :, lo + doff:NB + doff]),
                      writes=[A.rng(c0=lo, c1=NB)], dma_key="x")

            if p == 0:
                for ti, (c0, n) in enumerate(XT):
                    S.add("sp", (lambda c0, n: lambda h: h.dma_start(out=A.ap[:, :, c0:c0 + n], in_=xT_v[:, :, c0:c0 + n]))(c0, n),
                          writes=[A.rng(c0=c0, c1=c0 + n)], dma_key=f"x{ti}")
                S.add("sp", lambda h: h.dma_start(out=CB.ap[:, 0, :], in_=cB_d), writes=[CB.rng()], dma_key="cB")
                for (c0, n) in XT:
                    rmsnorm_tile(C_G1, HT, c0, n)

            zpar = 0
            for g in range(4):
                slot = ring_next()
                w4 = ring4(slot)
                if g == 1 and p == 0:
                    setup_T()
                if g == 2:
                    while deferred:
                        flush_deferred()
                    for h2 in range(2):
                        dg = DIAGB.ap[:, h2 * 124:(h2 + 1) * 124, :]
                        i32b = CA.ap[:, 0, C_I32:C_I32 + 32].unsqueeze(1).to_broadcast([128, 124, 32])
                        wbc = CA.ap[:, 0, C_CMW + h2 * 124:C_CMW + (h2 + 1) * 124].unsqueeze(2).to_broadcast([128, 124, 32])
                        tt("pool", dg, i32b, wbc, ALU.mult, [CA.rng()], [DIAGB.rng(h2 * 124, (h2 + 1) * 124)])
                units = [(i, t) for i in range(2) for t in ZT]
                if g == 0:
                    zt0 = [ZT[1], ZT[0], ZT[2]] if p == 0 else ZT
                    units = [(i, t) for t in zt0 for i in range(2)]
                for (i, (c0, n)) in units:
                    c = 2 * g + i
                    if True:
                        bb = nb()
                        for k in range(8):
                            pe_mm(ps[bb][:, 0:n], w4[:, 2 * i + 1, k, :], HT.ap[:, k, c0:c0 + n], k == 0, k == 7,
                                  [slot.rng(), HT.rng(k, k + 1, c0, c0 + n)], bb)
                        sp_ = zpar % 4
                        zpar += 1
                        act(SGT.ap[:, sp_, 0:n], ps[bb][:, 0:n], AF.Sigmoid, [("ps", bb)], [SGT.rng(sp_, sp_ + 1, 0, n)])
                        ba = nb()
                        for k in range(8):
                            pe_mm(ps[ba][:, 0:n], w4[:, 2 * i, k, :], HT.ap[:, k, c0:c0 + n], k == 0, k == 7,
                                  [slot.rng(), HT.rng(k, k + 1, c0, c0 + n)], ba)
                        tt("dve", Z.ap[:, c, c0:c0 + n], ps[ba][:, 0:n], SGT.ap[:, sp_, 0:n], ALU.mult,
                           [("ps", ba), SGT.rng(sp_, sp_ + 1, 0, n)], [Z.rng(c, c + 1, c0, c0 + n)])
                        flush_deferred()
            if p == 0:
                cp("pool", ZC.ap[:, :, :], Z.ap[:, :, NB - 32:NB], [Z.rng(c0=NB - 32, c1=NB)], [ZC.rng()])
            else:
                cp("pool", Z.ap[:, :, 96:128], ZC.ap[:, :, :], [ZC.rng()], [Z.rng(c0=96, c1=128)])

            ln_par = {}

            def conv_unit(h2, c0, n):
                bks = [nb() for _ in range(4)]
                for k in range(31):
                    s0 = c0 - 30 + k
                    for r in range(4):
                        for c in range(4):
                            di = h2 * 124 + c * 31 + k
                            pe_mm(ps[bks[r]][32 * c:32 * c + 32, 0:n], DIAGB.ap[32 * r:32 * r + 32, di, :],
                                  Z.ap[32 * r:32 * r + 32, 4 * h2 + c, s0:s0 + n], k == 0, k == 30,
                                  [DIAGB.rng(di, di + 1), Z.rng(4 * h2 + c, 4 * h2 + c + 1, s0, s0 + n)], bks[r],
                                  tp=(32 * r, 32 * c))
                for r in range(4):
                    ch = 4 * h2 + r
                    act(Y.ap[:, ch, c0:c0 + n], ps[bks[r]][:, 0:n], AF.Identity, [("ps", bks[r]), CA.rng()],
                        [Y.rng(ch, ch + 1, c0, c0 + n)], bias=cacol(C_CMB + ch))

            def halo_conv():
                cmw3 = CA.ap[:, 0, C_CMW:C_CMW + 248].rearrange("p (s k) -> p s k", s=8)
                for t_idx, t in enumerate((126, 127)):
                    tt("dve", HPROD.ap[:, :, :], Z.ap[:, :, t - 30:t + 1], cmw3, ALU.mult,
                       [Z.rng(c0=t - 30, c1=t + 1), CA.rng()], [HPROD.rng()])
                    S.add("dve", (lambda o_, i_: lambda h: h.tensor_reduce(out=o_, in_=i_, axis=mybir.AxisListType.X, op=ALU.add))(
                        HY.ap[:, :, t_idx], HPROD.ap[:, :, :]), reads=[HPROD.rng()], writes=[HY.rng()])
                cp("dve", HYH.ap[:, :, :], HY.ap[:, :, :], [HY.rng()], [HYH.rng()])
                cp("dve", HYF.ap[:, :, :], HYH.ap[:, :, :], [HYH.rng()], [HYF.rng()])
                tt("dve", HYL.ap[:, :, :], HY.ap[:, :, :], HYF.ap[:, :, :], ALU.subtract, [HY.rng(), HYF.rng()], [HYL.rng()])
                for h2 in range(2):
                    bks = [nb() for _ in range(4)]
                    for r in range(4):
                        for c in range(4):
                            for part, (src, first) in enumerate(((HYH, True), (HYL, False))):
                                pe_mm(ps[bks[r]][32 * c:32 * c + 32, 0:2], I32B.ap[32 * r:32 * r + 32, 0, :],
                                      src.ap[32 * r:32 * r + 32, 4 * h2 + c, 0:2], first, not first,
                                      [I32B.rng(), src.rng()], bks[r], tp=(32 * r, 32 * c))
                    for r in range(4):
                        ch = 4 * h2 + r
                        act(Y.ap[:, ch, 126:128], ps[bks[r]][:, 0:2], AF.Identity, [("ps", bks[r]), CA.rng()],
                            [Y.rng(ch, ch + 1, 126, 128)], bias=cacol(C_CMB + ch))

            def ln_sq(c0, n):
                act(SQ.ap[:, :, 0:n], Y.ap[:, :, c0:c0 + n], AF.Square, [Y.rng(c0=c0, c1=c0 + n)], [SQ.rng(c0=0, c1=n)])

            def ln_stats(ti, c0, n):
                b = nb()

                def srcfn(s_, k, t0, m, c0=c0):
                    if s_ == 0:
                        return Y.ap[:, k, c0 + t0:c0 + t0 + m], Y.rng(k, k + 1, c0 + t0, c0 + t0 + m)
                    return SQ.ap[:, k, t0:t0 + m], SQ.rng(k, k + 1, t0, t0 + m)

                token_stats(srcfn, 2, n, b)
                nq = (n + 127) // 128
                mm = min(128, n)
                par = sm_ctr[0] % 8
                sm_ctr[0] += 1
                cp("dve", SM.ap[0:mm, par, 0:2 * nq], ps[b][0:mm, 0:2 * nq], [("ps", b)], [SM.rng(par, par + 1, 0, 2 * nq)])
                st2 = SM.ap[0:mm, par, 0:2 * nq].rearrange("p (q s) -> p q s", s=2)
                mean, e2 = st2[:, :, 0], st2[:, :, 1]
                tt("dve", SM.ap[0:mm, par, 8:8 + nq], mean, mean, ALU.mult, [SM.rng(par, par + 1, 0, 8)], [SM.rng(par, par + 1, 8, 8 + nq)])
                tt("dve", SM.ap[0:mm, par, 12:12 + nq], e2, SM.ap[0:mm, par, 8:8 + nq], ALU.subtract,
                   [SM.rng(par, par + 1, 0, 12)], [SM.rng(par, par + 1, 12, 12 + nq)])
                ts("dve", SM.ap[0:mm, par, 12:12 + nq], SM.ap[0:mm, par, 12:12 + nq], 0.0, EPS, ALU.max, ALU.add,
                   [SM.rng(par, par + 1, 12, 12 + nq)], [SM.rng(par, par + 1, 12, 12 + nq)])
                tt("pool", SM.ap[0:mm, par, 16:16 + nq], SM.ap[0:mm, par, 12:12 + nq], MHALF.ap[0:mm, 0, 0:nq], ALU.pow,
                   [SM.rng(par, par + 1, 12, 16), MHALF.rng()], [SM.rng(par, par + 1, 16, 16 + nq)])
                stt("dve", SM.ap[0:mm, par, 20:20 + nq], mean, -1.0, SM.ap[0:mm, par, 16:16 + nq], ALU.mult, ALU.mult,
                    [SM.rng(par, par + 1, 0, 20)], [SM.rng(par, par + 1, 20, 20 + nq)])
                ln_par[ti] = par

            lnap_ctr = [0]

            def ln_apply(tiles):
                for ti, (c0, n) in tiles:
                    s1 = bcast_diag(SM, ln_par[ti], 16, n)
                    s2 = bcast_diag(SM, ln_par[ti], 20, n)
                    b1 = bcast_mm(s1, n)
                    b2 = bcast_mm(s2, n)
                    if n == 2:
                        r1, r2 = SMH.ap[:, 0, 0:2], SMH.ap[:, 1, 0:2]
                        rr1, rr2 = SMH.rng(0, 1), SMH.rng(1, 2)
                    else:
                        i1, i2 = 2 * (ti % 2), 2 * (ti % 2) + 1
                        r1, r2 = R12.ap[:, i1, 0:n], R12.ap[:, i2, 0:n]
                        rr1, rr2 = R12.rng(i1, i1 + 1, 0, n), R12.rng(i2, i2 + 1, 0, n)
                    act(r1, ps[b1][:, 0:n], AF.Copy, [("ps", b1)], [rr1])
                    act(r2, ps[b2][:, 0:n], AF.Copy, [("ps", b2)], [rr2])
                    for c in range(8):
                        q = lnap_ctr[0] % 4
                        lnap_ctr[0] += 1
                        tt("dve", TN2.ap[:, q, 0:n], Y.ap[:, c, c0:c0 + n], r1, ALU.mult,
                           [Y.rng(c, c + 1, c0, c0 + n), rr1], [TN2.rng(q, q + 1, 0, n)])
                        tt("dve" if c % 4 == 3 else "pool", TN2.ap[:, q, 0:n], TN2.ap[:, q, 0:n], r2, ALU.add,
                           [TN2.rng(q, q + 1, 0, n), rr2], [TN2.rng(q, q + 1, 0, n)])
                        act(Z.ap[:, c, c0:c0 + n], TN2.ap[:, q, 0:n], AF.Silu, [TN2.rng(q, q + 1, 0, n), CA.rng()],
                            [Z.rng(c, c + 1, c0, c0 + n)], scale=cacol(C_CMG + c), bias=cacol(C_CMBT + c))

            ln_pending = None
            for ti, (c0, n) in enumerate(ET):
                if n == 2:
                    halo_conv()
                    ln_sq(c0, n)
                    ln_pending = (ti, c0, n)
                    continue
                conv_unit(0, c0, n)
                if ln_pending is not None:
                    ln_stats(*ln_pending)
                    ln_pending = None
                conv_unit(1, c0, n)
                ln_sq(c0, n)
                ln_pending = (ti, c0, n)
            ln_apply(list(enumerate(ET))[:-1])

            for g in range(2):
                slot = ring_next()
                w4 = ring4(slot)
                for i in range(4):
                    uc = 4 * g + i
                    for (c0, n) in ET:
                        b = nb()
                        for k in range(8):
                            pe_mm(ps[b][:, 0:n], w4[:, i, k, :], HT.ap[:, k, c0:c0 + n], k == 0, k == 7,
                                  [slot.rng(), HT.rng(k, k + 1, c0, c0 + n)], b)
                        act(U.ap[:, uc, c0:c0 + n], ps[b][:, 0:n], AF.Gelu, [("ps", b)], [U.rng(uc, uc + 1, c0, c0 + n)])
                    if g == 0 and i == 0:
                        ln_stats(*ln_pending)
                    if g == 0 and i == 1:
                        ln_apply(list(enumerate(ET))[-1:])

            slot0 = ring_next(prefetch=RING - 1)
            slot1 = ring_next(prefetch=RING - 2)
            wv = [ring4(slot0), ring4(slot1)]
            wslots = [slot0, slot1]
            sgu_q = []
            blocks = ([0] if p == 0 else []) + list(range(1, 9))
            nslot_of = {}
            for bi, blk in enumerate(blocks):
                cb = 128 * blk
                par = bi % 2
                ns = bi % 8
                nslot_of[blk] = ns
                for hh in range(2):
                    b = nb()
                    for k in range(8):
                        pe_mm(ps[b][:, :].rearrange("p (c m) -> p c m", c=4), HT.ap[:, k, cb:cb + 128], wv[hh][:, :, k, :],
                              k == 0, k == 7, [wslots[hh].rng(), HT.rng(k, k + 1, cb, cb + 128)], b)
                    act(GV.ap[:, par, hh * 512:(hh + 1) * 512], ps[b][:, :], AF.Gelu, [("ps", b)],
                        [GV.rng(par, par + 1, hh * 512, (hh + 1) * 512)])
                st3 = STS.ap[:, par, :].rearrange("p (a b) -> p a b", a=2)
                for hh in range(2):
                    S.add("dve", (lambda o_, i_: lambda h: h.bn_stats(out=o_, in_=i_))(st3[:, hh, :], GV.ap[:, par, hh * 512:(hh + 1) * 512]),
                          reads=[GV.rng(par, par + 1, hh * 512, (hh + 1) * 512)], writes=[STS.rng(par, par + 1, hh * 6, hh * 6 + 6)])
                S.add("dve", (lambda o_, i_: lambda h: h.bn_aggr(out=o_, in_=i_))(MV.ap[:, par, 0:2], st3),
                      reads=[STS.rng(par, par + 1)], writes=[MV.rng(par, par + 1, 0, 2)])
                ts("dve", MV.ap[:, par, 2:3], MV.ap[:, par, 1:2], EPS, None, ALU.add, None,
                   [MV.rng(par, par + 1, 0, 2)], [MV.rng(par, par + 1, 2, 3)])
                tt("pool", MV.ap[:, par, 3:4], MV.ap[:, par, 2:3], MHALF.ap[:, 0, 0:1], ALU.pow,
                   [MV.rng(par, par + 1, 2, 3), MHALF.rng()], [MV.rng(par, par + 1, 3, 4)])
                ts("dve", NBLK.ap[:, ns, :], GV.ap[:, par, :], MV.ap[:, par, 0:1], MV.ap[:, par, 3:4], ALU.subtract, ALU.mult,
                   [GV.rng(par, par + 1), MV.rng(par, par + 1)], [NBLK.rng(ns, ns + 1)])
                tile = None
                if p == 0 and blk == 0:
                    tile = (126, 2, [0])
                elif blk in (4, 8):
                    c0 = 128 if blk == 4 else 640
                    tile = (c0, 512, [blk - 3, blk - 2, blk - 1, blk])
                if tile is not None:
                    c0, n, tb = tile
                    for g in range(8):
                        def sgu_item(g=g, c0=c0, n=n, tb=tb, nsl=dict(nslot_of)):
                            b = nb()
                            tq = g % 4
                            if n == 2:
                                pe_mm(ps[b][:, 0:2], NBLK.ap[:, nsl[0], g * 128:(g + 1) * 128], WGT.ap[:, g, 126:128], True, True,
                                      [NBLK.rng(nsl[0], nsl[0] + 1), WGT.rng(g, g + 1)], b)
                                tin1 = TT.ap[:, g, 126:128]
                                pin, tout = ps[b][:, 0:2], T12.ap[:, tq, 0:2]
                                uin, uout = U.ap[:, g, 126:128], U.ap[:, g, 126:128]
                            else:
                                for q, bq in enumerate(tb):
                                    pe_mm(ps[b][:, q * 128:(q + 1) * 128], NBLK.ap[:, nsl[bq], g * 128:(g + 1) * 128], WGT.ap[:, g, :],
                                          True, True, [NBLK.rng(nsl[bq], nsl[bq] + 1), WGT.rng(g, g + 1)], b)
                                tin1 = TT.ap[:, g, :].unsqueeze(1).to_broadcast([128, 4, 128])
                                pin = ps[b][:, :].rearrange("p (a b) -> p a b", a=4)
                                tout = T12.ap[:, tq, :].rearrange("p (a b) -> p a b", a=4)
                                uin = U.ap[:, g, c0:c0 + n].rearrange("p (a b) -> p a b", a=4)
                                uout = uin
                            stt("dve", tout, pin, cacol(C_GAMA + g), tin1, ALU.mult, ALU.add,
                                [("ps", b), CA.rng(), TT.rng(g, g + 1)], [T12.rng(tq, tq + 1, 0, n)])
                            tt("pool", uout, tout, uin, ALU.mult,
                               [T12.rng(tq, tq + 1, 0, n), U.rng(g, g + 1, c0, c0 + n)], [U.rng(g, g + 1, c0, c0 + n)])
                        sgu_q.append(sgu_item)
                    if n == 2:
                        while sgu_q:
                            sgu_q.pop(0)()
                else:
                    for _ in range(2):
                        if sgu_q:
                            sgu_q.pop(0)()
            for _ in range(4):
                if sgu_q:
                    sgu_q.pop(0)()

            sp_ctr = 0
            for dc in range(8):
                slot = ring_next()
                w4 = ring4(slot)
                for (c0, n) in ET:
                    if dc == 0 and (c0, n) == ET[-1]:
                        while sgu_q:
                            sgu_q.pop(0)()
                        load_x()
                    srcs = [HT, HT, U, Z]
                    banks = []
                    for i in range(4):
                        b = nb()
                        banks.append(b)
                        for k in range(8):
                            pe_mm(ps[b][:, 0:n], w4[:, i, k, :], srcs[i].ap[:, k, c0:c0 + n], k == 0, k == 7,
                                  [slot.rng(), srcs[i].rng(k, k + 1, c0, c0 + n)], b)
                        if i < 2:
                            q = (sp_ctr * 2 + i) % 4
                            act(SGT.ap[:, q, 0:n], ps[b][:, 0:n], AF.Sigmoid, [("ps", b)], [SGT.rng(q, q + 1, 0, n)])
                        else:
                            q = (sp_ctr * 2 + i - 2) % 4
                            tq = (sp_ctr * 2 + i - 2) % 4
                            tt("dve", T12.ap[:, tq, 0:n], ps[b][:, 0:n], SGT.ap[:, q, 0:n], ALU.mult,
                               [("ps", b), SGT.rng(q, q + 1, 0, n)], [T12.rng(tq, tq + 1, 0, n)])
                    t1q = (sp_ctr * 2) % 4
                    t2q = (sp_ctr * 2 + 1) % 4
                    tt("pool", Y.ap[:, dc, c0:c0 + n], T12.ap[:, t1q, 0:n], T12.ap[:, t2q, 0:n], ALU.add,
                       [T12.rng(t1q, t1q + 1, 0, n), T12.rng(t2q, t2q + 1, 0, n)], [Y.rng(dc, dc + 1, c0, c0 + n)])
                    sp_ctr += 1

            slots4 = [ring_next(), ring_next(prefetch=RING - 2)]
            pend_a2 = None
            pend_b = None
            for (c0, n) in ET:
                for dc in range(8):
                    slot = slots4[dc // 4]
                    w4 = ring4(slot)
                    b = nb()
                    for k in range(8):
                        pe_mm(ps[b][:, 0:n], w4[:, dc % 4, k, :], Y.ap[:, k, c0:c0 + n], k == 0, k == 7,
                              [slot.rng(), Y.rng(k, k + 1, c0, c0 + n)], b)
                    tt("dve", A.ap[:, dc, c0:c0 + n], ps[b][:, 0:n], A.ap[:, dc, c0:c0 + n], ALU.add,
                       [("ps", b), A.rng(dc, dc + 1, c0, c0 + n)], [A.rng(dc, dc + 1, c0, c0 + n)])
                    if dc == 1 and pend_a2 is not None:
                        pc0, pn = pend_a2
                        pend_b = (pc0, pn, rms_a2(pc0, pn))
                        pend_a2 = None
                    if dc == 5 and pend_b is not None:
                        pc0, pn, sl = pend_b
                        rms_b(C_G2, HT, pc0, pn, sl)
                        pend_b = None
                rms_a1(c0, n)
                pend_a2 = (c0, n)

            for G in range(6):
                ncg = NCH_G[G]
                slots = [ring_next(), ring_next(prefetch=RING - 2)]
                for half in range(2):
                    dgs = DIAGF.ap[:, half * 12:(half + 1) * 12, :]
                    i32b = CA.ap[:, 0, C_I32:C_I32 + 32].unsqueeze(1).to_broadcast([128, 12, 32])
                    t0_ = C_FDW + (half * 6 + G) * 12
                    wbc = CA.ap[:, 0, t0_:t0_ + 12].unsqueeze(2).to_broadcast([128, 12, 32])
                    tt("pool", dgs, i32b, wbc, ALU.mult, [CA.rng()], [DIAGF.rng(half * 12, (half + 1) * 12)])
                for half in range(2):
                    slot = slots[half]
                    w4 = ring4(slot)

                    def up_unit(c, c0, n, half=half, slot=slot, w4=w4):
                        ui = half * 4 + c
                        b = nb()
                        for k in range(8):
                            pe_mm(ps[b][:, 0:n], w4[:, c, k, :], HT.ap[:, k, c0:c0 + n], k == 0, k == 7,
                                  [slot.rng(), HT.rng(k, k + 1, c0, c0 + n)], b)
                        u0 = c0 - 126
                        if n == 2:
                            act(UPB.ap[:, ui, u0:u0 + n], ps[b][:, 0:n], AF.Copy, [("ps", b), CA.rng()],
                                [UPB.rng(ui, ui + 1, u0, u0 + n)], scale=cacol(C_FLAG))
                        else:
                            act(UPB.ap[:, ui, u0:u0 + n], ps[b][:, 0:n], AF.Copy, [("ps", b)],
                                [UPB.rng(ui, ui + 1, u0, u0 + n)])

                    if G == 0 and half == 0:
                        lc0, ln_ = pend_a2
                        for (c0, n) in ET[:-1]:
                            for c in range(4):
                                up_unit(c, c0, n)
                                if (c0, n) == ET[-2] and c == 0:
                                    pend_b = (lc0, ln_, rms_a2(lc0, ln_))
                                if (c0, n) == ET[-2] and c == 2:
                                    rms_b(C_G2, HT, lc0, ln_, pend_b[2])
                        for c in range(4):
                            up_unit(c, lc0, ln_)
                    else:
                        for c in range(4):
                            for (c0, n) in ET:
                                up_unit(c, c0, n)
                    for c in range(4):
                        ui = half * 4 + c
                        cidx = half * 24 + G * 4 + c
                        if p == 0:
                            cp("pool", UPC.ap[:, cidx, :], UPB.ap[:, ui, 1024:1026], [UPB.rng(ui, ui + 1, 1024, 1026)],
                               [UPC.rng(cidx, cidx + 1)])
                        else:
                            cp("pool", UPB.ap[:, ui, 0:2], UPC.ap[:, cidx, :], [UPC.rng(cidx, cidx + 1)],
                               [UPB.rng(ui, ui + 1, 0, 2)])
                for (c0, n) in MT:
                    u0 = c0 - 126
                    for half in range(2):
                        bks = [nb() for _ in range(ncg)]
                        for k in range(3):
                            s0 = u0 - 2 + k
                            for r in range(ncg):
                                for c in range(4):
                                    di = half * 12 + c * 3 + k
                                    ui = half * 4 + c
                                    pe_mm(ps[bks[r]][32 * c:32 * c + 32, 0:n], DIAGF.ap[32 * r:32 * r + 32, di, :],
                                          UPB.ap[32 * r:32 * r + 32, ui, s0:s0 + n], k == 0, k == 2,
                                          [DIAGF.rng(di, di + 1), UPB.rng(ui, ui + 1, s0, s0 + n)], bks[r],
                                          tp=(32 * r, 32 * c))
                        for r in range(ncg):
                            j = 4 * G + r
                            if half == 0:
                                act(SGT.ap[:, r, 0:n], ps[bks[r]][:, 0:n], AF.Silu, [("ps", bks[r]), CA.rng()],
                                    [SGT.rng(r, r + 1, 0, n)], bias=cacol(C_FDB + j))
                            else:
                                stt("dve", HH.ap[:, j, c0 - 128:c0 - 128 + n], ps[bks[r]][:, 0:n], cacol(C_FDB + 22 + j),
                                    SGT.ap[:, r, 0:n], ALU.add, ALU.mult, [("ps", bks[r]), CA.rng(), SGT.rng(r, r + 1, 0, n)],
                                    [HH.rng(j, j + 1, c0 - 128, c0 - 128 + n)])

            nxt = []
            if p == 0:
                for (c0, n) in MT:
                    def st_load(c0=c0, n=n):
                        S.add("sp", lambda h: h.dma_start(out=XS.ap[:, :, 0:n], in_=xT_v[:, :, c0 + 1024:c0 + 1024 + n]),
                              writes=[XS.rng(c0=0, c1=n)], dma_key="xs")
                        rms_a1(c0, n, src=XS, sc0=0)
                    st = {}

                    def st_a2(c0=c0, n=n, st=st):
                        st["slot"] = rms_a2(c0, n)

                    def st_b(c0=c0, n=n, st=st):
                        rms_b(C_G1, HT, c0, n, st["slot"], src=XS, sc0=0)
                    nxt += [st_load, st_a2, st_b]
            def wdown_unit(slot, dc, c0, n):
                wd3 = slot.ap[:, 0, 0:DFF].rearrange("p (k m) -> p k m", k=22)
                b = nb()
                for k in range(22):
                    pe_mm(ps[b][:, 0:n], wd3[:, k, :], HH.ap[:, k, c0 - 128:c0 - 128 + n], k == 0, k == 21,
                          [slot.rng(), HH.rng(k, k + 1, c0 - 128, c0 - 128 + n)], b)
                tt("dve", A.ap[:, dc, c0:c0 + n], ps[b][:, 0:n], A.ap[:, dc, c0:c0 + n], ALU.add,
                   [("ps", b), A.rng(dc, dc + 1, c0, c0 + n)], [A.rng(dc, dc + 1, c0, c0 + n)])

            if p == 0:
                for dc in range(8):
                    slot = ring_next()
                    for (c0, n) in MT:
                        wdown_unit(slot, dc, c0, n)
                        if nxt and dc >= 1 and (c0, n) == MT[0]:
                            nxt.pop(0)()

            def fin_out(c0, n, doff=doff):
                d0 = c0 - 128 + doff
                S.add("sp", (lambda c0, n, d0: lambda h: h.dma_start(out=out_v[:, :, d0:d0 + n], in_=A.ap[:, :, c0:c0 + n]))(c0, n, d0),
                      reads=[A.rng(c0=c0, c1=c0 + n)], writes=[("dram", "out")], dma_key="out")

            if p == 0:
                for (c0, n) in MT:
                    st = {}

                    def f_a(c0=c0, n=n, st=st):
                        rms_a1(c0, n)
                        st["slot"] = rms_a2(c0, n)

                    def f_b(c0=c0, n=n, st=st):
                        rms_b(C_GF, A, c0, n, st["slot"])
                        fin_out(c0, n)
                    deferred += [f_a, f_b]
                flush_deferred()
            else:
                (ca, na), (cb_, nb_) = MT
                for dc in range(8):
                    wdown_unit(ring_next(), dc, ca, na)
                rms_a1(ca, na)
                st0 = {}
                for dc in range(8):
                    wdown_unit(ring_next(), dc, cb_, nb_)
                    if dc == 1:
                        st0["slot"] = rms_a2(ca, na)
                    if dc == 3:
                        rms_b(C_GF, A, ca, na, st0["slot"])
                        fin_out(ca, na)
                rmsnorm_tile(C_GF, A, cb_, nb_)
                fin_out(cb_, nb_)

        S.add("sp", None, reads=[("dram", "out")])
        S.finalize()
        S.emit(block, eng_sems, dma_sems)
    return nc


_CACHE = {}


def kernel(**inputs):
    inp = {k: np.asarray(v) for k, v in inputs.items()}
    x = inp["x"].astype(np.float32, copy=False)
    wst, wdn, cA, cB = host_layout(inp)
    in_maps = []
    for core in range(NCORES):
        b, s = core // 2, core % 2
        t0 = s * SEQ_CORE
        xT = np.zeros((D, NTOK), np.float32)
        xT[:, HALO:] = x[b, t0:t0 + SEQ_CORE, :].T
        cAc = cA.copy()
        if s == 1:
            xT[:, :HALO] = x[b, t0 - HALO:t0, :].T
            cAc[:, C_FLAG] = 1.0
        in_maps.append({"xT": xT, "wst": wst, "wdn": wdn, "cA": cAc, "cB": cB})
    if "nc" not in _CACHE:
        _CACHE["nc"] = build_program()
    nc = _CACHE["nc"]
    res = run_bass_kernel_spmd(nc, in_maps, core_ids=list(range(NCORES)))
    out = np.empty((4, 4096, D), np.float32)
    for core in range(NCORES):
        b, s = core // 2, core % 2
        t0 = s * SEQ_CORE
        out[b, t0:t0 + SEQ_CORE, :] = res.results[core]["outT"].T
    return out
```

```python
from contextlib import ExitStack
import numpy as np
import concourse.bass as bass
import concourse.mybir as mybir
from concourse.bass_utils import run_bass_kernel_spmd

F32 = mybir.dt.float32
BF16 = mybir.dt.bfloat16
AF = mybir.ActivationFunctionType
ALU = mybir.AluOpType

NCORES = 8
D = 1024
SEQ_CORE = 2048
HALO = 128
NTOK = HALO + SEQ_CORE
NB = 1152
DFF = 2816
EPS = 1e-6
RING = 4

GRAN = 256
ENGS = ("pe", "act", "dve", "pool", "sp")


class Buf:
    def __init__(self, name, ap, off_bytes, es, K, C):
        self.name, self.ap, self.off, self.es, self.K, self.C = name, ap, off_bytes, es, K, C

    def rng(self, k0=0, k1=None, c0=0, c1=None):
        k1 = self.K if k1 is None else k1
        c1 = self.C if c1 is None else c1
        return [(self.off + (k * self.C + c0) * self.es, self.off + (k * self.C + c1) * self.es)
                for k in range(k0, k1)]


def gran_cells(ranges):
    cells = set()
    for a, b in ranges:
        cells.update(range(a // GRAN, (b - 1) // GRAN + 1))
    return cells


class Op:
    __slots__ = ("idx", "eng", "fn", "deps", "is_dma", "sem_key", "sem_val", "seqpos",
                 "marked", "count", "waits")

    def __init__(self):
        self.deps = {}
        self.is_dma = False
        self.marked = False
        self.count = 0
        self.waits = []
        self.sem_key = None
        self.sem_val = 0


class Sched:
    def __init__(self):
        self.ops = []
        self.last_w = {}
        self.readers = {}
        self.dma_counts = {}
        self.eng_len = {e: 0 for e in ENGS}

    @staticmethod
    def _cells(spec):
        cells = set()
        for item in spec:
            if isinstance(item, tuple):
                cells.add(item)
            else:
                cells |= gran_cells(item)
        return cells

    def add(self, eng, fn, reads=(), writes=(), dma_key=None):
        op = Op()
        op.idx = len(self.ops)
        op.eng = eng
        op.fn = fn
        op.seqpos = self.eng_len[eng]
        self.eng_len[eng] += 1
        if dma_key is not None:
            op.is_dma = True
            op.sem_key = dma_key
            self.dma_counts[dma_key] = self.dma_counts.get(dma_key, 0) + 16
            op.sem_val = self.dma_counts[dma_key]
        for c in self._cells(reads):
            w = self.last_w.get(c)
            if w is not None:
                op.deps[w] = "RAW"
            self.readers.setdefault(c, []).append(op.idx)
        for c in self._cells(writes):
            w = self.last_w.get(c)
            if w is not None and w != op.idx:
                op.deps.setdefault(w, "WAW")
            for r in self.readers.get(c, ()):
                if r != op.idx:
                    op.deps.setdefault(r, "WAR")
            self.readers[c] = []
            self.last_w[c] = op.idx
        self.ops.append(op)
        return op

    def finalize(self):
        known = {e: {f: -1 for f in ENGS} for e in ENGS}
        known_dma = {e: {} for e in ENGS}
        for op in self.ops:
            E = op.eng
            need = []
            best = {}
            for d, kind in op.deps.items():
                Dp = self.ops[d]
                if Dp.is_dma:
                    if known_dma[E].get(Dp.sem_key, 0) < Dp.sem_val:
                        need.append(("dma", Dp.sem_key, Dp.sem_val))
                        known_dma[E][Dp.sem_key] = Dp.sem_val
                    continue
                Fe = Dp.eng
                if Fe == E and not op.is_dma:
                    if kind != "RAW" or E == "pe":
                        continue
                if Dp.seqpos <= known[E][Fe]:
                    continue
                if Fe not in best or Dp.seqpos > best[Fe].seqpos:
                    best[Fe] = Dp
            for Fe, Dp in best.items():
                known[E][Fe] = Dp.seqpos
                Dp.marked = True
                need.append(("eng", Fe, Dp))
            op.waits = need
        cnt = {e: 0 for e in ENGS}
        for op in self.ops:
            if op.is_dma:
                continue
            if op.marked:
                cnt[op.eng] += 1
            op.count = cnt[op.eng]

    def emit(self, block, eng_sems, dma_sems):
        per = {e: [] for e in ENGS}
        for op in self.ops:
            per[op.eng].append(op)

        def run(eng_name, handle):
            for op in per[eng_name]:
                for w in op.waits:
                    if w[0] == "dma":
                        handle.wait_ge(dma_sems[w[1]], w[2])
                    else:
                        handle.wait_ge(eng_sems[w[1]], w[2].count)
                if op.fn is None:
                    continue
                ins = op.fn(handle)
                if op.is_dma:
                    ins.then_inc(dma_sems[op.sem_key], 16)
                elif op.marked:
                    ins.then_inc(eng_sems[op.eng], 1)

        @block.tensor
        def _(h):
            run("pe", h)

        @block.scalar
        def _(h):
            run("act", h)

        @block.vector
        def _(h):
            run("dve", h)

        @block.gpsimd
        def _(h):
            run("pool", h)

        @block.sync
        def _(h):
            run("sp", h)


C_G1, C_G2, C_GF, C_GAMA, C_CMB, C_CMG, C_CMBT = 0, 8, 16, 24, 32, 40, 48
C_CMW = 56
C_FDB = C_CMW + 8 * 31
C_FDW = C_FDB + 44
C_FLAG = C_FDW + 144
C_I32 = 496
C_ID = C_I32 + 32
NCA = C_ID + 128
B_SGW, B_MASK, B_BETA, B_SGB = 0, 1024, 1152, 2176
NCB = 3200

NGROUPS = 30
NCH_G = [4, 4, 4, 4, 4, 2]


def _chunk(W, col0):
    K = W.shape[0] // 128
    return W[:, col0:col0 + 128].reshape(K, 128, 128).transpose(1, 0, 2).reshape(128, K * 128)


def _col(v):
    n = v.shape[0] // 128
    return v.reshape(n, 128).T


def host_layout(inp):
    w_in = inp["w_in"][0]
    w_up = inp["w_up"][0]
    groups = []

    def grp(chs):
        groups.append(np.concatenate(chs, axis=1))

    pidx = np.arange(128)
    r_, i_ = pidx // 32, pidx % 32
    zperm = np.concatenate([(4 * (sl // 4) + r_) * 128 + 32 * (sl % 4) + i_ for sl in range(8)])
    w_ga = w_in[:, 2048:3072][:, zperm]
    w_gb = w_in[:, 3072:4096][:, zperm]
    for g in range(4):
        grp([_chunk(w_ga, 128 * (2 * g)), _chunk(w_gb, 128 * (2 * g)),
             _chunk(w_ga, 128 * (2 * g + 1)), _chunk(w_gb, 128 * (2 * g + 1))])
    for g in range(2):
        grp([_chunk(w_in, 128 * (4 * g + i)) for i in range(4)])
    for g in range(2):
        grp([_chunk(w_in, 1024 + 128 * (4 * g + i)) for i in range(4)])
    wa, wb, wo = inp["w_a_out"][0], inp["w_b_out"][0], inp["w_o"][0]
    for dc in range(8):
        grp([_chunk(w_in, 4096 + 128 * dc), _chunk(w_in, 5120 + 128 * dc),
             _chunk(wa, 128 * dc), _chunk(wb, 128 * dc)])
    for g in range(2):
        grp([_chunk(wo, 128 * (4 * g + i)) for i in range(4)])
    def ffperm(G, c):
        ch = (4 * G + r_) * 128 + 32 * c + i_
        valid = r_ < NCH_G[G]
        return ch, valid

    def up_slot(base, G, c):
        ch, valid = ffperm(G, c)
        W = np.zeros((D, 128), np.float32)
        W[:, valid] = w_up[:, base + ch[valid]]
        return _chunk(W, 0)

    for G in range(6):
        grp([up_slot(0, G, c) for c in range(4)])
        grp([up_slot(DFF, G, c) for c in range(4)])
    wst = np.ascontiguousarray(np.stack(groups, 0), dtype=np.float32)
    wd = inp["w_down"][0]
    wdn = np.ascontiguousarray(np.stack([_chunk(wd, 128 * dc) for dc in range(8)], 0),
                               dtype=np.float32)

    cA = np.zeros((128, NCA), np.float32)
    cA[:, C_G1:C_G1 + 8] = _col(inp["norm1_g"][0])
    cA[:, C_G2:C_G2 + 8] = _col(inp["norm2_g"][0])
    cA[:, C_GF:C_GF + 8] = _col(inp["normf_g"])
    cA[:, C_GAMA:C_GAMA + 8] = _col(inp["sgu_ln_g"][0])
    cA[:, C_CMB:C_CMB + 8] = _col(inp["cm_dw_b"][0])
    cA[:, C_CMG:C_CMG + 8] = _col(inp["cm_ln_g"][0])
    cA[:, C_CMBT:C_CMBT + 8] = _col(inp["cm_ln_b"][0])
    cmw = inp["cm_dw_w"][0]
    cA[:, C_CMW:C_CMW + 248] = cmw[:, zperm].reshape(31, 8, 128).transpose(2, 1, 0).reshape(128, 248)
    cA[:, C_FDB:C_FDB + 44] = _col(inp["ffn_dw_b"][0])
    fdw = inp["ffn_dw_w"][0]
    for half in range(2):
        for G in range(6):
            for c in range(4):
                ch, valid = ffperm(G, c)
                col = C_FDW + (half * 6 + G) * 12 + c * 3
                cA[valid, col:col + 3] = fdw[:, half * DFF + ch[valid]].T
    cA[pidx, C_I32 + pidx % 32] = 1.0
    cA[:, C_ID:C_ID + 128] = np.eye(128, dtype=np.float32)

    cB = np.zeros((128, NCB), np.float32)
    cB[:, B_SGW:B_SGW + 1024] = inp["sgu_w"][0].transpose(2, 0, 1).reshape(128, 1024)
    pos = np.arange(128) // 64
    cB[:, B_MASK:B_MASK + 128] = (pos[:, None] <= pos[None, :]).astype(np.float32)
    cB[:, B_BETA:B_BETA + 1024] = np.broadcast_to(inp["sgu_ln_b"][0][None, :], (128, 1024))
    cB[:, B_SGB:B_SGB + 1024] = np.broadcast_to(inp["sgu_b"][0].reshape(1, 1024), (128, 1024))
    return wst, wdn, cA, cB


def build_program():
    nc = bass.Bass("TRN2", target_bir_lowering=False)
    xT_d = nc.dram_tensor("xT", [D, NTOK], F32, kind="ExternalInput").ap()
    wst_d = nc.dram_tensor("wst", [NGROUPS, 128, 4096], F32, kind="ExternalInput").ap()
    wdn_d = nc.dram_tensor("wdn", [8, 128, DFF], F32, kind="ExternalInput").ap()
    cA_d = nc.dram_tensor("cA", [128, NCA], F32, kind="ExternalInput").ap()
    cB_d = nc.dram_tensor("cB", [128, NCB], F32, kind="ExternalInput").ap()
    out_d = nc.dram_tensor("outT", [D, SEQ_CORE], F32, kind="ExternalOutput").ap()
    xT_v = xT_d.rearrange("(k p) c -> p k c", p=128)
    out_v = out_d.rearrange("(k p) c -> p k c", p=128)

    S = Sched()
    with ExitStack() as es:
        ARENA_BYTES = 206 * 1024
        arena = es.enter_context(nc.sbuf_tensor("arena", [128, ARENA_BYTES // 2], BF16))
        ps = [es.enter_context(nc.psum_tensor(f"ps{i}", [128, 512], F32)) for i in range(8)]
        eng_sems = {e: es.enter_context(nc.semaphore(f"s_{e}")) for e in ENGS}
        dma_keys = ["x", "x0", "x1", "x2", "xs", "out", "cA", "cB"] + [("ring", s) for s in range(RING)]
        dma_sems = {k: es.enter_context(nc.semaphore("d_" + (k if isinstance(k, str) else f"ring{k[1]}")))
                    for k in dma_keys}
        block = es.enter_context(nc.Block())

        def view(off, dt, K, C, name):
            esz = 4 if dt == F32 else 2
            nb = K * C * esz
            assert off % 4 == 0 and off + nb <= ARENA_BYTES, (name, off, nb)
            v = arena[:, off // 2:(off + nb) // 2]
            if dt == F32:
                v = v.bitcast(F32)
            v = v.rearrange("p (k c) -> p k c", k=K)
            return Buf(name, v, off, esz, K, C)

        o = 0
        R_A = o; o += 8 * NB * 4
        R_H = o; o += 8 * NB * 2
        R_C = o; o += 24 * NB * 2
        R_DG = o; o += 16 * 128 * 4
        R_SQ = o; o += 8 * 512 * 2
        R_RING = o; o += RING * 8192
        R_TMP = o; o += 16384
        R_DF = o; o += 24 * 32 * 2
        R_UPB = o; o += 8 * 1032 * 2
        R_K = o
        A = view(R_A, F32, 8, NB, "A")
        NBLK = view(R_A, BF16, 8, 1024, "nblk")
        GV = view(R_A + 16384, F32, 2, 1024, "gv")
        DIAGB = view(R_A, BF16, 248, 32, "diagB")
        TN2 = view(R_A + 24576, F32, 4, 512, "tn2")
        HT = view(R_H, BF16, 8, NB, "hT")
        U = view(R_C, BF16, 8, NB, "u")
        Z = view(R_C + 8 * NB * 2, BF16, 8, NB, "z")
        Y = view(R_C + 16 * NB * 2, BF16, 8, NB, "y")
        HH = view(R_C, BF16, 22, 1024, "hh")
        CB = view(R_C + 16 * NB * 2, F32, 1, NCB, "cB")
        DG = view(R_DG, F32, 16, 128, "dg")
        SQ = view(R_SQ, BF16, 8, 512, "sq")
        R12 = view(R_TMP, F32, 4, 512, "r12")
        RINGB = [view(R_RING + s * 8192, BF16, 1, 4096, f"ring{s}") for s in range(RING)]
        SGT = view(R_TMP, F32, 4, 512, "sgt")
        T12 = view(R_TMP + 8192, F32, 4, 512, "t12")
        DIAGF = view(R_DF, BF16, 24, 32, "diagF")
        UPB = view(R_UPB, BF16, 8, 1032, "upb")
        XS = view(R_UPB, F32, 8, 512, "xs")
        k = R_K
        CA = view(k, F32, 1, NCA, "cA"); k += NCA * 4
        IDB = view(k, BF16, 1, 128, "identb"); k += 256
        ONESB = view(k, BF16, 1, 128, "onesb"); k += 256
        WGT = view(k, BF16, 8, 128, "wgT"); k += 2048
        TT = view(k, F32, 8, 128, "T"); k += 4096
        MHALF = view(k, F32, 1, 8, "mhalf"); k += 32
        SM = view(k, F32, 8, 32, "sm"); k += 1024
        SMH = view(k, F32, 2, 2, "smh"); k += 64
        HPROD = view(k, F32, 8, 31, "hprod"); k += 1024
        HY = view(k, F32, 8, 2, "hy"); k += 64
        HYF = view(k, F32, 8, 2, "hyf"); k += 64
        HYH = view(k, BF16, 8, 2, "hyh"); k += 32
        HYL = view(k, BF16, 8, 2, "hyl"); k += 32
        I32B = view(k, BF16, 1, 32, "i32b"); k += 64
        ONESF = view(k, F32, 1, 128, "onesf"); k += 512
        ZC = view(k, BF16, 8, 32, "zc"); k += 512
        UPC = view(k, BF16, 48, 2, "upc"); k += 256
        STS = view(k, F32, 2, 12, "sts"); k += 128
        MV = view(k, F32, 2, 4, "mv"); k += 64
        assert k <= ARENA_BYTES, k

        def cacol(c):
            return CA.ap[:, 0, c:c + 1]

        bank_ctr = [0]
        reserved = set()

        def nb():
            while True:
                b = bank_ctr[0] % 8
                bank_ctr[0] += 1
                if b not in reserved:
                    return b

        def pe_mm(out, lhsT, rhs, start, stop, reads, bank, tp=None):
            if tp is None:
                S.add("pe", lambda h: h.matmul(out, lhsT=lhsT, rhs=rhs, start=start, stop=stop),
                      reads=reads, writes=[("ps", bank)])
            else:
                S.add("pe", lambda h: h.matmul(out, lhsT=lhsT, rhs=rhs, start=start, stop=stop, tile_position=tp),
                      reads=reads, writes=[("ps", bank)])

        def act(out, in_, func, reads, writes, scale=None, bias=None):
            kw = {}
            if scale is not None:
                kw["scale"] = scale
            if bias is not None:
                kw["bias"] = bias
            S.add("act", lambda h: h.activation(out=out, in_=in_, func=func, **kw), reads=reads, writes=writes)

        def tt(eng, out, in0, in1, op, reads, writes):
            S.add(eng, lambda h: h.tensor_tensor(out=out, in0=in0, in1=in1, op=op), reads=reads, writes=writes)

        def ts(eng, out, in0, s1, s2, op0, op1, reads, writes):
            if s2 is None:
                S.add(eng, lambda h: h.tensor_scalar(out=out, in0=in0, scalar1=s1, scalar2=None, op0=op0),
                      reads=reads, writes=writes)
            else:
                S.add(eng, lambda h: h.tensor_scalar(out=out, in0=in0, scalar1=s1, scalar2=s2, op0=op0, op1=op1),
                      reads=reads, writes=writes)

        def stt(eng, out, in0, scalar, in1, op0, op1, reads, writes):
            S.add(eng, lambda h: h.scalar_tensor_tensor(out=out, in0=in0, scalar=scalar, in1=in1, op0=op0, op1=op1),
                  reads=reads, writes=writes)

        def cp(eng, out, in_, reads, writes):
            S.add(eng, lambda h: h.tensor_copy(out=out, in_=in_), reads=reads, writes=writes)

        def memset(eng, buf, val):
            S.add(eng, lambda h: h.memset(buf.ap[:, :, :], val), writes=[buf.rng()])

        stream = []
        for p in range(2):
            for g in range(NGROUPS):
                stream.append((wst_d[g], 4096))
            for rep in range(1 if p == 0 else 2):
                for dc in range(8):
                    stream.append((wdn_d[dc], DFF))
        issued = [0]

        def ring_issue(upto, extra_reads=()):
            while issued[0] <= min(upto, len(stream) - 1):
                L = issued[0]
                s = L % RING
                src, w = stream[L]
                dst = RINGB[s].ap[:, 0, 0:w]
                S.add("pool", (lambda dst, src: lambda h: h.dma_start(out=dst, in_=src))(dst, src),
                      reads=list(extra_reads), writes=[RINGB[s].rng()], dma_key=("ring", s))
                issued[0] += 1

        load_ctr = [0]

        def ring_next(prefetch=RING - 1):
            L = load_ctr[0]
            load_ctr[0] += 1
            ring_issue(L + prefetch)
            return RINGB[L % RING]

        S.add("sp", lambda h: h.dma_start(out=CA.ap[:, 0, :], in_=cA_d), writes=[CA.rng()], dma_key="cA")
        memset("pool", ONESB, 1.0 / 1024.0)
        memset("pool", MHALF, -0.5)
        memset("pool", ONESF, 1.0)
        cp("dve", IDB.ap[:, 0, :], CA.ap[:, 0, C_ID:C_ID + 128], [CA.rng()], [IDB.rng()])
        cp("dve", I32B.ap[:, 0, :], CA.ap[:, 0, C_I32:C_I32 + 32], [CA.rng()], [I32B.rng()])
        def setup_T():
            sgw = CB.ap[:, 0, B_SGW:B_SGW + 1024].rearrange("p (g i) -> p g i", g=8)
            maskb = CB.ap[:, 0, B_MASK:B_MASK + 128].unsqueeze(1).to_broadcast([128, 8, 128])
            tt("dve", sgw, sgw, maskb, ALU.mult, [CB.rng()], [CB.rng()])
            cp("dve", WGT.ap[:, :, :], sgw, [CB.rng()], [WGT.rng()])
            betaB = CB.ap[:, 0, B_BETA:B_BETA + 1024].rearrange("p (g i) -> p g i", g=8)
            sgb = CB.ap[:, 0, B_SGB:B_SGB + 1024].rearrange("p (g i) -> p g i", g=8)
            for g in range(8):
                b = nb()
                pe_mm(ps[b][:, 0:128], betaB[:, g, :], sgw[:, g, :], True, False, [CB.rng()], b)
                pe_mm(ps[b][:, 0:128], ONESF.ap[0:1, 0, :], sgb[0:1, g, :], False, True, [CB.rng(), ONESF.rng()], b)
                cp("dve", TT.ap[:, g, :], ps[b][:, 0:128], [("ps", b)], [TT.rng(g, g + 1)])

        def token_stats(srcfn, nstat, n, b):
            nq = (n + 127) // 128
            for q in range(nq):
                m = min(128, n - 128 * q)
                for s_ in range(nstat):
                    col = q * nstat + s_
                    for k in range(8):
                        ap_, rd = srcfn(s_, k, 128 * q, m)
                        pe_mm(ps[b][0:m, col:col + 1], ap_, ONESB.ap[:, 0, 0:1], k == 0, k == 7,
                              [ONESB.rng(), rd], b)

        dg_ctr = [0]

        def bcast_diag(colbuf, par, col0, n):
            nq = (n + 127) // 128
            mm = min(128, n)
            slot = dg_ctr[0] % 4
            dg_ctr[0] += 1
            dgv = DG.ap[0:mm, slot * 4:slot * 4 + nq, :]
            idb = CA.ap[0:mm, 0, C_ID:C_ID + 128].unsqueeze(1).to_broadcast([mm, nq, 128])
            cb = colbuf.ap[0:mm, par, col0:col0 + nq].unsqueeze(2).to_broadcast([mm, nq, 128])
            tt("dve", dgv, idb, cb, ALU.mult, [CA.rng(), colbuf.rng(par, par + 1, col0, col0 + nq)],
               [DG.rng(slot * 4, slot * 4 + nq)])
            return slot

        def bcast_mm(slot, n):
            nq = (n + 127) // 128
            b2 = nb()
            for q in range(nq):
                m = min(128, n - 128 * q)
                pe_mm(ps[b2][:, 128 * q:128 * q + m], ONESF.ap[0:m, 0, :], DG.ap[0:m, slot * 4 + q, 0:m], True, True,
                      [ONESF.rng(), DG.rng(slot * 4 + q, slot * 4 + q + 1)], b2)
            return b2

        sm_ctr = [0]

        def rms_a1(c0, n, src=None, sc0=None):
            src = A if src is None else src
            sc0 = c0 if sc0 is None else sc0
            act(SQ.ap[:, :, 0:n], src.ap[:, :, sc0:sc0 + n], AF.Square, [src.rng(c0=sc0, c1=sc0 + n)], [SQ.rng(c0=0, c1=n)])

        def rms_a2(c0, n):
            b = nb()
            token_stats(lambda s_, k, t0, m: (SQ.ap[:, k, t0:t0 + m], SQ.rng(k, k + 1, t0, t0 + m)), 1, n, b)
            nq = (n + 127) // 128
            mm = min(128, n)
            par = sm_ctr[0] % 8
            sm_ctr[0] += 1
            ts("dve", SM.ap[0:mm, par, 0:nq], ps[b][0:mm, 0:nq], EPS, None, ALU.add, None, [("ps", b)], [SM.rng(par, par + 1, 0, nq)])
            tt("pool", SM.ap[0:mm, par, 4:4 + nq], SM.ap[0:mm, par, 0:nq], MHALF.ap[0:mm, 0, 0:nq], ALU.pow,
               [SM.rng(par, par + 1, 0, nq), MHALF.rng()], [SM.rng(par, par + 1, 4, 4 + nq)])
            return bcast_diag(SM, par, 4, n)

        def rms_b(gc, dst, c0, n, slot, src=None, sc0=None):
            src = A if src is None else src
            sc0 = c0 if sc0 is None else sc0
            b2 = bcast_mm(slot, n)
            for k in range(8):
                stt("dve", dst.ap[:, k, c0:c0 + n], src.ap[:, k, sc0:sc0 + n], cacol(gc + k), ps[b2][:, 0:n],
                    ALU.mult, ALU.mult,
                    [src.rng(k, k + 1, sc0, sc0 + n), CA.rng(), ("ps", b2)],
                    [dst.rng(k, k + 1, c0, c0 + n)])

        def rmsnorm_tile(gc, dst, c0, n):
            rms_a1(c0, n)
            slot = rms_a2(c0, n)
            rms_b(gc, dst, c0, n, slot)

        deferred = []

        def flush_deferred(nmax=1):
            for _ in range(nmax):
                if deferred:
                    deferred.pop(0)()

        def ring4(slot):
            return slot.ap[:, 0, :].rearrange("p (c k m) -> p c k m", c=4, k=8)

        for p in range(2):
            MT = [(128, 512), (640, 512)]
            ET = ([(126, 2)] if p == 0 else []) + MT
            ZT = ([(96, 32)] if p == 0 else []) + MT
            XT = [MT[0], (0, 128), MT[1]] if p == 0 else MT
            lo = 0 if p == 0 else 128
            doff = 1024 * p

            def load_x(lo=lo, doff=doff):
                S.add("sp", lambda h, lo=lo, doff=doff: h.dma_start(out=A.ap[:, :, lo:NB], in_=xT_v[:, :, lo + doff:NB + doff]),
                      writes=[A.rng(c0=lo, c1=NB)], dma_key="x")

            if p == 0:
                GATE = ("gate", "x_first")
                for ti, (c0, n) in enumerate(XT):
                    S.add("sp", (lambda c0, n: lambda h: h.dma_start(out=A.ap[:, :, c0:c0 + n], in_=xT_v[:, :, c0:c0 + n]))(c0, n),
                          reads=([GATE] if ti > 0 else []),
                          writes=[A.rng(c0=c0, c1=c0 + n)] + ([GATE] if ti == 0 else []), dma_key=f"x{ti}")
                    if ti == 0:
                        ring_issue(1, extra_reads=[GATE])
                S.add("sp", lambda h: h.dma_start(out=CB.ap[:, 0, :], in_=cB_d), reads=[GATE], writes=[CB.rng()], dma_key="cB")
                for (c0, n) in XT:
                    rmsnorm_tile(C_G1, HT, c0, n)

            zpar = 0
            for g in range(4):
                slot = ring_next()
                w4 = ring4(slot)
                if g == 1 and p == 0:
                    setup_T()
                if g == 2:
                    while deferred:
                        flush_deferred()
                    for h2 in range(2):
                        dg = DIAGB.ap[:, h2 * 124:(h2 + 1) * 124, :]
                        i32b = CA.ap[:, 0, C_I32:C_I32 + 32].unsqueeze(1).to_broadcast([128, 124, 32])
                        wbc = CA.ap[:, 0, C_CMW + h2 * 124:C_CMW + (h2 + 1) * 124].unsqueeze(2).to_broadcast([128, 124, 32])
                        tt("pool", dg, i32b, wbc, ALU.mult, [CA.rng()], [DIAGB.rng(h2 * 124, (h2 + 1) * 124)])
                units = [(i, t) for i in range(2) for t in ZT]
                if g == 0:
                    zt0 = [ZT[1], ZT[0], ZT[2]] if p == 0 else ZT
                    units = [(i, t) for t in zt0 for i in range(2)]
                for (i, (c0, n)) in units:
                    c = 2 * g + i
                    if True:
                        bb = nb()
                        for k in range(8):
                            pe_mm(ps[bb][:, 0:n], w4[:, 2 * i + 1, k, :], HT.ap[:, k, c0:c0 + n], k == 0, k == 7,
                                  [slot.rng(), HT.rng(k, k + 1, c0, c0 + n)], bb)
                        sp_ = zpar % 4
                        zpar += 1
                        act(SGT.ap[:, sp_, 0:n], ps[bb][:, 0:n], AF.Sigmoid, [("ps", bb)], [SGT.rng(sp_, sp_ + 1, 0, n)])
                        ba = nb()
                        for k in range(8):
                            pe_mm(ps[ba][:, 0:n], w4[:, 2 * i, k, :], HT.ap[:, k, c0:c0 + n], k == 0, k == 7,
                                  [slot.rng(), HT.rng(k, k + 1, c0, c0 + n)], ba)
                        tt("dve", Z.ap[:, c, c0:c0 + n], ps[ba][:, 0:n], SGT.ap[:, sp_, 0:n], ALU.mult,
                           [("ps", ba), SGT.rng(sp_, sp_ + 1, 0, n)], [Z.rng(c, c + 1, c0, c0 + n)])
                        flush_deferred()
            if p == 0:
                cp("pool", ZC.ap[:, :, :], Z.ap[:, :, NB - 32:NB], [Z.rng(c0=NB - 32, c1=NB)], [ZC.rng()])
            else:
                cp("pool", Z.ap[:, :, 96:128], ZC.ap[:, :, :], [ZC.rng()], [Z.rng(c0=96, c1=128)])

            ln_par = {}

            def conv_unit(h2, c0, n):
                bks = [nb() for _ in range(4)]
                for k in range(31):
                    s0 = c0 - 30 + k
                    for r in range(4):
                        for c in range(4):
                            di = h2 * 124 + c * 31 + k
                            pe_mm(ps[bks[r]][32 * c:32 * c + 32, 0:n], DIAGB.ap[32 * r:32 * r + 32, di, :],
                                  Z.ap[32 * r:32 * r + 32, 4 * h2 + c, s0:s0 + n], k == 0, k == 30,
                                  [DIAGB.rng(di, di + 1), Z.rng(4 * h2 + c, 4 * h2 + c + 1, s0, s0 + n)], bks[r],
                                  tp=(32 * r, 32 * c))
                for r in range(4):
                    ch = 4 * h2 + r
                    act(Y.ap[:, ch, c0:c0 + n], ps[bks[r]][:, 0:n], AF.Identity, [("ps", bks[r]), CA.rng()],
                        [Y.rng(ch, ch + 1, c0, c0 + n)], bias=cacol(C_CMB + ch))

            def halo_conv():
                cmw3 = CA.ap[:, 0, C_CMW:C_CMW + 248].rearrange("p (s k) -> p s k", s=8)
                for t_idx, t in enumerate((126, 127)):
                    tt("dve", HPROD.ap[:, :, :], Z.ap[:, :, t - 30:t + 1], cmw3, ALU.mult,
                       [Z.rng(c0=t - 30, c1=t + 1), CA.rng()], [HPROD.rng()])
                    S.add("dve", (lambda o_, i_: lambda h: h.tensor_reduce(out=o_, in_=i_, axis=mybir.AxisListType.X, op=ALU.add))(
                        HY.ap[:, :, t_idx], HPROD.ap[:, :, :]), reads=[HPROD.rng()], writes=[HY.rng()])
                cp("dve", HYH.ap[:, :, :], HY.ap[:, :, :], [HY.rng()], [HYH.rng()])
                cp("dve", HYF.ap[:, :, :], HYH.ap[:, :, :], [HYH.rng()], [HYF.rng()])
                tt("dve", HYL.ap[:, :, :], HY.ap[:, :, :], HYF.ap[:, :, :], ALU.subtract, [HY.rng(), HYF.rng()], [HYL.rng()])
                for h2 in range(2):
                    bks = [nb() for _ in range(4)]
                    for r in range(4):
                        for c in range(4):
                            for part, (src, first) in enumerate(((HYH, True), (HYL, False))):
                                pe_mm(ps[bks[r]][32 * c:32 * c + 32, 0:2], I32B.ap[32 * r:32 * r + 32, 0, :],
                                      src.ap[32 * r:32 * r + 32, 4 * h2 + c, 0:2], first, not first,
                                      [I32B.rng(), src.rng()], bks[r], tp=(32 * r, 32 * c))
                    for r in range(4):
                        ch = 4 * h2 + r
                        act(Y.ap[:, ch, 126:128], ps[bks[r]][:, 0:2], AF.Identity, [("ps", bks[r]), CA.rng()],
                            [Y.rng(ch, ch + 1, 126, 128)], bias=cacol(C_CMB + ch))

            def ln_sq(c0, n):
                act(SQ.ap[:, :, 0:n], Y.ap[:, :, c0:c0 + n], AF.Square, [Y.rng(c0=c0, c1=c0 + n)], [SQ.rng(c0=0, c1=n)])

            def ln_stats(ti, c0, n):
                b = nb()

                def srcfn(s_, k, t0, m, c0=c0):
                    if s_ == 0:
                        return Y.ap[:, k, c0 + t0:c0 + t0 + m], Y.rng(k, k + 1, c0 + t0, c0 + t0 + m)
                    return SQ.ap[:, k, t0:t0 + m], SQ.rng(k, k + 1, t0, t0 + m)

                token_stats(srcfn, 2, n, b)
                nq = (n + 127) // 128
                mm = min(128, n)
                par = sm_ctr[0] % 8
                sm_ctr[0] += 1
                cp("dve", SM.ap[0:mm, par, 0:2 * nq], ps[b][0:mm, 0:2 * nq], [("ps", b)], [SM.rng(par, par + 1, 0, 2 * nq)])
                st2 = SM.ap[0:mm, par, 0:2 * nq].rearrange("p (q s) -> p q s", s=2)
                mean, e2 = st2[:, :, 0], st2[:, :, 1]
                tt("dve", SM.ap[0:mm, par, 8:8 + nq], mean, mean, ALU.mult, [SM.rng(par, par + 1, 0, 8)], [SM.rng(par, par + 1, 8, 8 + nq)])
                tt("dve", SM.ap[0:mm, par, 12:12 + nq], e2, SM.ap[0:mm, par, 8:8 + nq], ALU.subtract,
                   [SM.rng(par, par + 1, 0, 12)], [SM.rng(par, par + 1, 12, 12 + nq)])
                ts("dve", SM.ap[0:mm, par, 12:12 + nq], SM.ap[0:mm, par, 12:12 + nq], 0.0, EPS, ALU.max, ALU.add,
                   [SM.rng(par, par + 1, 12, 12 + nq)], [SM.rng(par, par + 1, 12, 12 + nq)])
                tt("pool", SM.ap[0:mm, par, 16:16 + nq], SM.ap[0:mm, par, 12:12 + nq], MHALF.ap[0:mm, 0, 0:nq], ALU.pow,
                   [SM.rng(par, par + 1, 12, 16), MHALF.rng()], [SM.rng(par, par + 1, 16, 16 + nq)])
                stt("dve", SM.ap[0:mm, par, 20:20 + nq], mean, -1.0, SM.ap[0:mm, par, 16:16 + nq], ALU.mult, ALU.mult,
                    [SM.rng(par, par + 1, 0, 20)], [SM.rng(par, par + 1, 20, 20 + nq)])
                ln_par[ti] = par

            lnap_ctr = [0]

            def ln_apply(tiles):
                for ti, (c0, n) in tiles:
                    s1 = bcast_diag(SM, ln_par[ti], 16, n)
                    s2 = bcast_diag(SM, ln_par[ti], 20, n)
                    b1 = bcast_mm(s1, n)
                    b2 = bcast_mm(s2, n)
                    if n == 2:
                        r1, r2 = SMH.ap[:, 0, 0:2], SMH.ap[:, 1, 0:2]
                        rr1, rr2 = SMH.rng(0, 1), SMH.rng(1, 2)
                    else:
                        i1, i2 = 2 * (ti % 2), 2 * (ti % 2) + 1
                        r1, r2 = R12.ap[:, i1, 0:n], R12.ap[:, i2, 0:n]
                        rr1, rr2 = R12.rng(i1, i1 + 1, 0, n), R12.rng(i2, i2 + 1, 0, n)
                    act(r1, ps[b1][:, 0:n], AF.Copy, [("ps", b1)], [rr1])
                    act(r2, ps[b2][:, 0:n], AF.Copy, [("ps", b2)], [rr2])
                    for c in range(8):
                        q = lnap_ctr[0] % 4
                        lnap_ctr[0] += 1
                        tt("dve", TN2.ap[:, q, 0:n], Y.ap[:, c, c0:c0 + n], r1, ALU.mult,
                           [Y.rng(c, c + 1, c0, c0 + n), rr1], [TN2.rng(q, q + 1, 0, n)])
                        tt("dve" if c % 4 == 3 else "pool", TN2.ap[:, q, 0:n], TN2.ap[:, q, 0:n], r2, ALU.add,
                           [TN2.rng(q, q + 1, 0, n), rr2], [TN2.rng(q, q + 1, 0, n)])
                        act(Z.ap[:, c, c0:c0 + n], TN2.ap[:, q, 0:n], AF.Silu, [TN2.rng(q, q + 1, 0, n), CA.rng()],
                            [Z.rng(c, c + 1, c0, c0 + n)], scale=cacol(C_CMG + c), bias=cacol(C_CMBT + c))

            ln_pending = None
            for ti, (c0, n) in enumerate(ET):
                if n == 2:
                    halo_conv()
                    ln_sq(c0, n)
                    ln_pending = (ti, c0, n)
                    continue
                conv_unit(0, c0, n)
                if ln_pending is not None:
                    ln_stats(*ln_pending)
                    ln_pending = None
                conv_unit(1, c0, n)
                ln_sq(c0, n)
                ln_pending = (ti, c0, n)
            ln_apply(list(enumerate(ET))[:-1])

            for g in range(2):
                slot = ring_next()
                w4 = ring4(slot)
                for i in range(4):
                    uc = 4 * g + i
                    for (c0, n) in ET:
                        b = nb()
                        for k in range(8):
                            pe_mm(ps[b][:, 0:n], w4[:, i, k, :], HT.ap[:, k, c0:c0 + n], k == 0, k == 7,
                                  [slot.rng(), HT.rng(k, k + 1, c0, c0 + n)], b)
                        act(U.ap[:, uc, c0:c0 + n], ps[b][:, 0:n], AF.Gelu, [("ps", b)], [U.rng(uc, uc + 1, c0, c0 + n)])
                    if g == 0 and i == 0:
                        ln_stats(*ln_pending)
                    if g == 0 and i == 1:
                        ln_apply(list(enumerate(ET))[-1:])

            slot0 = ring_next(prefetch=RING - 1)
            slot1 = ring_next(prefetch=RING - 2)
            wv = [ring4(slot0), ring4(slot1)]
            wslots = [slot0, slot1]
            sgu_q = []
            blocks = ([0] if p == 0 else []) + list(range(1, 9))
            nslot_of = {}
            for bi, blk in enumerate(blocks):
                cb = 128 * blk
                par = bi % 2
                ns = bi % 8
                nslot_of[blk] = ns
                for hh in range(2):
                    b = nb()
                    for k in range(8):
                        pe_mm(ps[b][:, :].rearrange("p (c m) -> p c m", c=4), HT.ap[:, k, cb:cb + 128], wv[hh][:, :, k, :],
                              k == 0, k == 7, [wslots[hh].rng(), HT.rng(k, k + 1, cb, cb + 128)], b)
                    act(GV.ap[:, par, hh * 512:(hh + 1) * 512], ps[b][:, :], AF.Gelu, [("ps", b)],
                        [GV.rng(par, par + 1, hh * 512, (hh + 1) * 512)])
                st3 = STS.ap[:, par, :].rearrange("p (a b) -> p a b", a=2)
                for hh in range(2):
                    S.add("dve", (lambda o_, i_: lambda h: h.bn_stats(out=o_, in_=i_))(st3[:, hh, :], GV.ap[:, par, hh * 512:(hh + 1) * 512]),
                          reads=[GV.rng(par, par + 1, hh * 512, (hh + 1) * 512)], writes=[STS.rng(par, par + 1, hh * 6, hh * 6 + 6)])
                S.add("dve", (lambda o_, i_: lambda h: h.bn_aggr(out=o_, in_=i_))(MV.ap[:, par, 0:2], st3),
                      reads=[STS.rng(par, par + 1)], writes=[MV.rng(par, par + 1, 0, 2)])
                ts("dve", MV.ap[:, par, 2:3], MV.ap[:, par, 1:2], EPS, None, ALU.add, None,
                   [MV.rng(par, par + 1, 0, 2)], [MV.rng(par, par + 1, 2, 3)])
                tt("pool", MV.ap[:, par, 3:4], MV.ap[:, par, 2:3], MHALF.ap[:, 0, 0:1], ALU.pow,
                   [MV.rng(par, par + 1, 2, 3), MHALF.rng()], [MV.rng(par, par + 1, 3, 4)])
                ts("dve", NBLK.ap[:, ns, :], GV.ap[:, par, :], MV.ap[:, par, 0:1], MV.ap[:, par, 3:4], ALU.subtract, ALU.mult,
                   [GV.rng(par, par + 1), MV.rng(par, par + 1)], [NBLK.rng(ns, ns + 1)])
                tile = None
                if p == 0 and blk == 0:
                    tile = (126, 2, [0])
                elif blk in (4, 8):
                    c0 = 128 if blk == 4 else 640
                    tile = (c0, 512, [blk - 3, blk - 2, blk - 1, blk])
                if tile is not None:
                    c0, n, tb = tile
                    for g in range(8):
                        def sgu_item(g=g, c0=c0, n=n, tb=tb, nsl=dict(nslot_of)):
                            b = nb()
                            tq = g % 4
                            if n == 2:
                                pe_mm(ps[b][:, 0:2], NBLK.ap[:, nsl[0], g * 128:(g + 1) * 128], WGT.ap[:, g, 126:128], True, True,
                                      [NBLK.rng(nsl[0], nsl[0] + 1), WGT.rng(g, g + 1)], b)
                                tin1 = TT.ap[:, g, 126:128]
                                pin, tout = ps[b][:, 0:2], T12.ap[:, tq, 0:2]
                                uin, uout = U.ap[:, g, 126:128], U.ap[:, g, 126:128]
                            else:
                                for q, bq in enumerate(tb):
                                    pe_mm(ps[b][:, q * 128:(q + 1) * 128], NBLK.ap[:, nsl[bq], g * 128:(g + 1) * 128], WGT.ap[:, g, :],
                                          True, True, [NBLK.rng(nsl[bq], nsl[bq] + 1), WGT.rng(g, g + 1)], b)
                                tin1 = TT.ap[:, g, :].unsqueeze(1).to_broadcast([128, 4, 128])
                                pin = ps[b][:, :].rearrange("p (a b) -> p a b", a=4)
                                tout = T12.ap[:, tq, :].rearrange("p (a b) -> p a b", a=4)
                                uin = U.ap[:, g, c0:c0 + n].rearrange("p (a b) -> p a b", a=4)
                                uout = uin
                            stt("dve", tout, pin, cacol(C_GAMA + g), tin1, ALU.mult, ALU.add,
                                [("ps", b), CA.rng(), TT.rng(g, g + 1)], [T12.rng(tq, tq + 1, 0, n)])
                            tt("pool", uout, tout, uin, ALU.mult,
                               [T12.rng(tq, tq + 1, 0, n), U.rng(g, g + 1, c0, c0 + n)], [U.rng(g, g + 1, c0, c0 + n)])
                        sgu_q.append(sgu_item)
                    if n == 2:
                        while sgu_q:
                            sgu_q.pop(0)()
                else:
                    for _ in range(2):
                        if sgu_q:
                            sgu_q.pop(0)()
            for _ in range(4):
                if sgu_q:
                    sgu_q.pop(0)()

            sp_ctr = 0
            for dc in range(8):
                slot = ring_next()
                w4 = ring4(slot)
                for (c0, n) in ET:
                    if dc == 0 and (c0, n) == ET[-1]:
                        while sgu_q:
                            sgu_q.pop(0)()
                        load_x()
                    srcs = [HT, HT, U, Z]
                    banks = []
                    for i in range(4):
                        b = nb()
                        banks.append(b)
                        for k in range(8):
                            pe_mm(ps[b][:, 0:n], w4[:, i, k, :], srcs[i].ap[:, k, c0:c0 + n], k == 0, k == 7,
                                  [slot.rng(), srcs[i].rng(k, k + 1, c0, c0 + n)], b)
                        if i < 2:
                            q = (sp_ctr * 2 + i) % 4
                            act(SGT.ap[:, q, 0:n], ps[b][:, 0:n], AF.Sigmoid, [("ps", b)], [SGT.rng(q, q + 1, 0, n)])
                        else:
                            q = (sp_ctr * 2 + i - 2) % 4
                            tq = (sp_ctr * 2 + i - 2) % 4
                            tt("dve", T12.ap[:, tq, 0:n], ps[b][:, 0:n], SGT.ap[:, q, 0:n], ALU.mult,
                               [("ps", b), SGT.rng(q, q + 1, 0, n)], [T12.rng(tq, tq + 1, 0, n)])
                    t1q = (sp_ctr * 2) % 4
                    t2q = (sp_ctr * 2 + 1) % 4
                    tt("pool", Y.ap[:, dc, c0:c0 + n], T12.ap[:, t1q, 0:n], T12.ap[:, t2q, 0:n], ALU.add,
                       [T12.rng(t1q, t1q + 1, 0, n), T12.rng(t2q, t2q + 1, 0, n)], [Y.rng(dc, dc + 1, c0, c0 + n)])
                    sp_ctr += 1

            slots4 = [ring_next(), ring_next(prefetch=RING - 2)]
            pend_a2 = None
            pend_b = None
            for (c0, n) in ET:
                for dc in range(8):
                    slot = slots4[dc // 4]
                    w4 = ring4(slot)
                    b = nb()
                    for k in range(8):
                        pe_mm(ps[b][:, 0:n], w4[:, dc % 4, k, :], Y.ap[:, k, c0:c0 + n], k == 0, k == 7,
                              [slot.rng(), Y.rng(k, k + 1, c0, c0 + n)], b)
                    tt("dve", A.ap[:, dc, c0:c0 + n], ps[b][:, 0:n], A.ap[:, dc, c0:c0 + n], ALU.add,
                       [("ps", b), A.rng(dc, dc + 1, c0, c0 + n)], [A.rng(dc, dc + 1, c0, c0 + n)])
                    if dc == 1 and pend_a2 is not None:
                        pc0, pn = pend_a2
                        pend_b = (pc0, pn, rms_a2(pc0, pn))
                        pend_a2 = None
                    if dc == 5 and pend_b is not None:
                        pc0, pn, sl = pend_b
                        rms_b(C_G2, HT, pc0, pn, sl)
                        pend_b = None
                rms_a1(c0, n)
                pend_a2 = (c0, n)

            for G in range(6):
                ncg = NCH_G[G]
                slots = [ring_next(), ring_next(prefetch=RING - 2)]
                for half in range(2):
                    dgs = DIAGF.ap[:, half * 12:(half + 1) * 12, :]
                    i32b = CA.ap[:, 0, C_I32:C_I32 + 32].unsqueeze(1).to_broadcast([128, 12, 32])
                    t0_ = C_FDW + (half * 6 + G) * 12
                    wbc = CA.ap[:, 0, t0_:t0_ + 12].unsqueeze(2).to_broadcast([128, 12, 32])
                    tt("pool", dgs, i32b, wbc, ALU.mult, [CA.rng()], [DIAGF.rng(half * 12, (half + 1) * 12)])
                for half in range(2):
                    slot = slots[half]
                    w4 = ring4(slot)

                    def up_unit(c, c0, n, half=half, slot=slot, w4=w4):
                        ui = half * 4 + c
                        b = nb()
                        for k in range(8):
                            pe_mm(ps[b][:, 0:n], w4[:, c, k, :], HT.ap[:, k, c0:c0 + n], k == 0, k == 7,
                                  [slot.rng(), HT.rng(k, k + 1, c0, c0 + n)], b)
                        u0 = c0 - 126
                        if n == 2:
                            act(UPB.ap[:, ui, u0:u0 + n], ps[b][:, 0:n], AF.Copy, [("ps", b), CA.rng()],
                                [UPB.rng(ui, ui + 1, u0, u0 + n)], scale=cacol(C_FLAG))
                        else:
                            act(UPB.ap[:, ui, u0:u0 + n], ps[b][:, 0:n], AF.Copy, [("ps", b)],
                                [UPB.rng(ui, ui + 1, u0, u0 + n)])

                    if G == 0 and half == 0:
                        lc0, ln_ = pend_a2
                        for (c0, n) in ET[:-1]:
                            for c in range(4):
                                up_unit(c, c0, n)
                                if (c0, n) == ET[-2] and c == 0:
                                    pend_b = (lc0, ln_, rms_a2(lc0, ln_))
                                if (c0, n) == ET[-2] and c == 2:
                                    rms_b(C_G2, HT, lc0, ln_, pend_b[2])
                        for c in range(4):
                            up_unit(c, lc0, ln_)
                    else:
                        for c in range(4):
                            for (c0, n) in ET:
                                up_unit(c, c0, n)
                    for c in range(4):
                        ui = half * 4 + c
                        cidx = half * 24 + G * 4 + c
                        if p == 0:
                            cp("pool", UPC.ap[:, cidx, :], UPB.ap[:, ui, 1024:1026], [UPB.rng(ui, ui + 1, 1024, 1026)],
                               [UPC.rng(cidx, cidx + 1)])
                        else:
                            cp("pool", UPB.ap[:, ui, 0:2], UPC.ap[:, cidx, :], [UPC.rng(cidx, cidx + 1)],
                               [UPB.rng(ui, ui + 1, 0, 2)])
                for (c0, n) in MT:
                    u0 = c0 - 126
                    for half in range(2):
                        bks = [nb() for _ in range(ncg)]
                        for k in range(3):
                            s0 = u0 - 2 + k
                            for r in range(ncg):
                                for c in range(4):
                                    di = half * 12 + c * 3 + k
                                    ui = half * 4 + c
                                    pe_mm(ps[bks[r]][32 * c:32 * c + 32, 0:n], DIAGF.ap[32 * r:32 * r + 32, di, :],
                                          UPB.ap[32 * r:32 * r + 32, ui, s0:s0 + n], k == 0, k == 2,
                                          [DIAGF.rng(di, di + 1), UPB.rng(ui, ui + 1, s0, s0 + n)], bks[r],
                                          tp=(32 * r, 32 * c))
                        for r in range(ncg):
                            j = 4 * G + r
                            if half == 0:
                                act(SGT.ap[:, r, 0:n], ps[bks[r]][:, 0:n], AF.Silu, [("ps", bks[r]), CA.rng()],
                                    [SGT.rng(r, r + 1, 0, n)], bias=cacol(C_FDB + j))
                            else:
                                stt("dve", HH.ap[:, j, c0 - 128:c0 - 128 + n], ps[bks[r]][:, 0:n], cacol(C_FDB + 22 + j),
                                    SGT.ap[:, r, 0:n], ALU.add, ALU.mult, [("ps", bks[r]), CA.rng(), SGT.rng(r, r + 1, 0, n)],
                                    [HH.rng(j, j + 1, c0 - 128, c0 - 128 + n)])

            nxt = []
            if p == 0:
                for (c0, n) in MT:
                    def st_load(c0=c0, n=n):
                        S.add("sp", lambda h: h.dma_start(out=XS.ap[:, :, 0:n], in_=xT_v[:, :, c0 + 1024:c0 + 1024 + n]),
                              writes=[XS.rng(c0=0, c1=n)], dma_key="xs")
                        rms_a1(c0, n, src=XS, sc0=0)
                    st = {}

                    def st_a2(c0=c0, n=n, st=st):
                        st["slot"] = rms_a2(c0, n)

                    def st_b(c0=c0, n=n, st=st):
                        rms_b(C_G1, HT, c0, n, st["slot"], src=XS, sc0=0)
                    nxt += [st_load, st_a2, st_b]
            def wdown_unit(slot, dc, c0, n):
                wd3 = slot.ap[:, 0, 0:DFF].rearrange("p (k m) -> p k m", k=22)
                b = nb()
                for k in range(22):
                    pe_mm(ps[b][:, 0:n], wd3[:, k, :], HH.ap[:, k, c0 - 128:c0 - 128 + n], k == 0, k == 21,
                          [slot.rng(), HH.rng(k, k + 1, c0 - 128, c0 - 128 + n)], b)
                tt("dve", A.ap[:, dc, c0:c0 + n], ps[b][:, 0:n], A.ap[:, dc, c0:c0 + n], ALU.add,
                   [("ps", b), A.rng(dc, dc + 1, c0, c0 + n)], [A.rng(dc, dc + 1, c0, c0 + n)])

            if p == 0:
                for dc in range(8):
                    slot = ring_next()
                    for (c0, n) in MT:
                        wdown_unit(slot, dc, c0, n)
                        if nxt and dc >= 1 and (c0, n) == MT[0]:
                            nxt.pop(0)()

            def fin_out(c0, n, doff=doff):
                d0 = c0 - 128 + doff
                S.add("sp", (lambda c0, n, d0: lambda h: h.dma_start(out=out_v[:, :, d0:d0 + n], in_=A.ap[:, :, c0:c0 + n]))(c0, n, d0),
                      reads=[A.rng(c0=c0, c1=c0 + n)], writes=[("dram", "out")], dma_key="out")

            if p == 0:
                for (c0, n) in MT:
                    st = {}

                    def f_a(c0=c0, n=n, st=st):
                        rms_a1(c0, n)
                        st["slot"] = rms_a2(c0, n)

                    def f_b(c0=c0, n=n, st=st):
                        rms_b(C_GF, A, c0, n, st["slot"])
                        fin_out(c0, n)
                    deferred += [f_a, f_b]
                flush_deferred()
            else:
                (ca, na), (cb_, nb_) = MT
                for dc in range(8):
                    wdown_unit(ring_next(), dc, ca, na)
                rms_a1(ca, na)
                st0 = {}
                for dc in range(8):
                    wdown_unit(ring_next(), dc, cb_, nb_)
                    if dc == 1:
                        st0["slot"] = rms_a2(ca, na)
                    if dc == 3:
                        rms_b(C_GF, A, ca, na, st0["slot"])
                        fin_out(ca, na)
                rmsnorm_tile(C_GF, A, cb_, nb_)
                fin_out(cb_, nb_)

        S.add("sp", None, reads=[("dram", "out")])
        S.finalize()
        S.emit(block, eng_sems, dma_sems)
    return nc


_CACHE = {}


def kernel(**inputs):
    inp = {k: np.asarray(v) for k, v in inputs.items()}
    x = inp["x"].astype(np.float32, copy=False)
    wst, wdn, cA, cB = host_layout(inp)
    in_maps = []
    for core in range(NCORES):
        b, s = core // 2, core % 2
        t0 = s * SEQ_CORE
        xT = np.zeros((D, NTOK), np.float32)
        xT[:, HALO:] = x[b, t0:t0 + SEQ_CORE, :].T
        cAc = cA.copy()
        if s == 1:
            xT[:, :HALO] = x[b, t0 - HALO:t0, :].T
            cAc[:, C_FLAG] = 1.0
        in_maps.append({"xT": xT, "wst": wst, "wdn": wdn, "cA": cAc, "cB": cB})
    if "nc" not in _CACHE:
        _CACHE["nc"] = build_program()
    nc = _CACHE["nc"]
    res = run_bass_kernel_spmd(nc, in_maps, core_ids=list(range(NCORES)))
    out = np.empty((4, 4096, D), np.float32)
    for core in range(NCORES):
        b, s = core // 2, core % 2
        t0 = s * SEQ_CORE
        out[b, t0:t0 + SEQ_CORE, :] = res.results[core]["outT"].T
    return out
```

```python
from contextlib import ExitStack
import numpy as np
import concourse.bass as bass
import concourse.mybir as mybir
from concourse.bass_utils import run_bass_kernel_spmd

F32 = mybir.dt.float32
BF16 = mybir.dt.bfloat16
AF = mybir.ActivationFunctionType
ALU = mybir.AluOpType

NCORES = 8
D = 1024
SEQ_CORE = 2048
HALO = 128
NTOK = HALO + SEQ_CORE
NB = 1152
DFF = 2816
EPS = 1e-6
RING = 4

GRAN = 256
ENGS = ("pe", "act", "dve", "pool", "sp")


class Buf:
    def __init__(self, name, ap, off_bytes, es, K, C):
        self.name, self.ap, self.off, self.es, self.K, self.C = name, ap, off_bytes, es, K, C

    def rng(self, k0=0, k1=None, c0=0, c1=None):
        k1 = self.K if k1 is None else k1
        c1 = self.C if c1 is None else c1
        return [(self.off + (k * self.C + c0) * self.es, self.off + (k * self.C + c1) * self.es)
                for k in range(k0, k1)]


def gran_cells(ranges):
    cells = set()
    for a, b in ranges:
        cells.update(range(a // GRAN, (b - 1) // GRAN + 1))
    return cells


class Op:
    __slots__ = ("idx", "eng", "fn", "deps", "is_dma", "sem_key", "sem_val", "seqpos",
                 "marked", "count", "waits")

    def __init__(self):
        self.deps = {}
        self.is_dma = False
        self.marked = False
        self.count = 0
        self.waits = []
        self.sem_key = None
        self.sem_val = 0


class Sched:
    def __init__(self):
        self.ops = []
        self.last_w = {}
        self.readers = {}
        self.dma_counts = {}
        self.eng_len = {e: 0 for e in ENGS}

    @staticmethod
    def _cells(spec):
        cells = set()
        for item in spec:
            if isinstance(item, tuple):
                cells.add(item)
            else:
                cells |= gran_cells(item)
        return cells

    def add(self, eng, fn, reads=(), writes=(), dma_key=None):
        op = Op()
        op.idx = len(self.ops)
        op.eng = eng
        op.fn = fn
        op.seqpos = self.eng_len[eng]
        self.eng_len[eng] += 1
        if dma_key is not None:
            op.is_dma = True
            op.sem_key = dma_key
            self.dma_counts[dma_key] = self.dma_counts.get(dma_key, 0) + 16
            op.sem_val = self.dma_counts[dma_key]
        for c in self._cells(reads):
            w = self.last_w.get(c)
            if w is not None:
                op.deps[w] = "RAW"
            self.readers.setdefault(c, []).append(op.idx)
        for c in self._cells(writes):
            w = self.last_w.get(c)
            if w is not None and w != op.idx:
                op.deps.setdefault(w, "WAW")
            for r in self.readers.get(c, ()):
                if r != op.idx:
                    op.deps.setdefault(r, "WAR")
            self.readers[c] = []
            self.last_w[c] = op.idx
        self.ops.append(op)
        return op

    def finalize(self):
        known = {e: {f: -1 for f in ENGS} for e in ENGS}
        known_dma = {e: {} for e in ENGS}
        for op in self.ops:
            E = op.eng
            need = []
            best = {}
            for d, kind in op.deps.items():
                Dp = self.ops[d]
                if Dp.is_dma:
                    if known_dma[E].get(Dp.sem_key, 0) < Dp.sem_val:
                        need.append(("dma", Dp.sem_key, Dp.sem_val))
                        known_dma[E][Dp.sem_key] = Dp.sem_val
                    continue
                Fe = Dp.eng
                if Fe == E and not op.is_dma:
                    if kind != "RAW" or E == "pe":
                        continue
                if Dp.seqpos <= known[E][Fe]:
                    continue
                if Fe not in best or Dp.seqpos > best[Fe].seqpos:
                    best[Fe] = Dp
            for Fe, Dp in best.items():
                known[E][Fe] = Dp.seqpos
                Dp.marked = True
                need.append(("eng", Fe, Dp))
            op.waits = need
        cnt = {e: 0 for e in ENGS}
        for op in self.ops:
            if op.is_dma:
                continue
            if op.marked:
                cnt[op.eng] += 1
            op.count = cnt[op.eng]

    def emit(self, block, eng_sems, dma_sems):
        per = {e: [] for e in ENGS}
        for op in self.ops:
            per[op.eng].append(op)

        def run(eng_name, handle):
            for op in per[eng_name]:
                for w in op.waits:
                    if w[0] == "dma":
                        handle.wait_ge(dma_sems[w[1]], w[2])
                    else:
                        handle.wait_ge(eng_sems[w[1]], w[2].count)
                if op.fn is None:
                    continue
                ins = op.fn(handle)
                if op.is_dma:
                    ins.then_inc(dma_sems[op.sem_key], 16)
                elif op.marked:
                    ins.then_inc(eng_sems[op.eng], 1)

        @block.tensor
        def _(h):
            run("pe", h)

        @block.scalar
        def _(h):
            run("act", h)

        @block.vector
        def _(h):
            run("dve", h)

        @block.gpsimd
        def _(h):
            run("pool", h)

        @block.sync
        def _(h):
            run("sp", h)


C_G1, C_G2, C_GF, C_GAMA, C_CMB, C_CMG, C_CMBT = 0, 8, 16, 24, 32, 40, 48
C_CMW = 56
C_FDB = C_CMW + 8 * 31
C_FDW = C_FDB + 44
C_FLAG = C_FDW + 144
C_I32 = 496
C_ID = C_I32 + 32
NCA = C_ID + 128
B_SGW, B_MASK, B_BETA, B_SGB = 0, 1024, 1152, 2176
NCB = 3200

NGROUPS = 30
NCH_G = [4, 4, 4, 4, 4, 2]


def _chunk(W, col0):
    K = W.shape[0] // 128
    return W[:, col0:col0 + 128].reshape(K, 128, 128).transpose(1, 0, 2).reshape(128, K * 128)


def _col(v):
    n = v.shape[0] // 128
    return v.reshape(n, 128).T


def host_layout(inp):
    w_in = inp["w_in"][0]
    w_up = inp["w_up"][0]
    groups = []

    def grp(chs):
        groups.append(np.concatenate(chs, axis=1))

    pidx = np.arange(128)
    r_, i_ = pidx // 32, pidx % 32
    zperm = np.concatenate([(4 * (sl // 4) + r_) * 128 + 32 * (sl % 4) + i_ for sl in range(8)])
    w_ga = w_in[:, 2048:3072][:, zperm]
    w_gb = w_in[:, 3072:4096][:, zperm]
    for g in range(4):
        grp([_chunk(w_ga, 128 * (2 * g)), _chunk(w_gb, 128 * (2 * g)),
             _chunk(w_ga, 128 * (2 * g + 1)), _chunk(w_gb, 128 * (2 * g + 1))])
    for g in range(2):
        grp([_chunk(w_in, 128 * (4 * g + i)) for i in range(4)])
    for g in range(2):
        grp([_chunk(w_in, 1024 + 128 * (4 * g + i)) for i in range(4)])
    wa, wb, wo = inp["w_a_out"][0], inp["w_b_out"][0], inp["w_o"][0]
    for dc in range(8):
        grp([_chunk(w_in, 4096 + 128 * dc), _chunk(w_in, 5120 + 128 * dc),
             _chunk(wa, 128 * dc), _chunk(wb, 128 * dc)])
    for g in range(2):
        grp([_chunk(wo, 128 * (4 * g + i)) for i in range(4)])
    def ffperm(G, c):
        ch = (4 * G + r_) * 128 + 32 * c + i_
        valid = r_ < NCH_G[G]
        return ch, valid

    def up_slot(base, G, c):
        ch, valid = ffperm(G, c)
        W = np.zeros((D, 128), np.float32)
        W[:, valid] = w_up[:, base + ch[valid]]
        return _chunk(W, 0)

    for G in range(6):
        grp([up_slot(0, G, c) for c in range(4)])
        grp([up_slot(DFF, G, c) for c in range(4)])
    wst = np.ascontiguousarray(np.stack(groups, 0), dtype=np.float32)
    wd = inp["w_down"][0]
    wdn = np.ascontiguousarray(np.stack([_chunk(wd, 128 * dc) for dc in range(8)], 0),
                               dtype=np.float32)

    cA = np.zeros((128, NCA), np.float32)
    cA[:, C_G1:C_G1 + 8] = _col(inp["norm1_g"][0])
    cA[:, C_G2:C_G2 + 8] = _col(inp["norm2_g"][0])
    cA[:, C_GF:C_GF + 8] = _col(inp["normf_g"])
    cA[:, C_GAMA:C_GAMA + 8] = _col(inp["sgu_ln_g"][0])
    cA[:, C_CMB:C_CMB + 8] = _col(inp["cm_dw_b"][0])
    cA[:, C_CMG:C_CMG + 8] = _col(inp["cm_ln_g"][0])
    cA[:, C_CMBT:C_CMBT + 8] = _col(inp["cm_ln_b"][0])
    cmw = inp["cm_dw_w"][0]
    cA[:, C_CMW:C_CMW + 248] = cmw[:, zperm].reshape(31, 8, 128).transpose(2, 1, 0).reshape(128, 248)
    cA[:, C_FDB:C_FDB + 44] = _col(inp["ffn_dw_b"][0])
    fdw = inp["ffn_dw_w"][0]
    for half in range(2):
        for G in range(6):
            for c in range(4):
                ch, valid = ffperm(G, c)
                col = C_FDW + (half * 6 + G) * 12 + c * 3
                cA[valid, col:col + 3] = fdw[:, half * DFF + ch[valid]].T
    cA[pidx, C_I32 + pidx % 32] = 1.0
    cA[:, C_ID:C_ID + 128] = np.eye(128, dtype=np.float32)

    cB = np.zeros((128, NCB), np.float32)
    cB[:, B_SGW:B_SGW + 1024] = inp["sgu_w"][0].transpose(2, 0, 1).reshape(128, 1024)
    pos = np.arange(128) // 64
    cB[:, B_MASK:B_MASK + 128] = (pos[:, None] <= pos[None, :]).astype(np.float32)
    cB[:, B_BETA:B_BETA + 1024] = np.broadcast_to(inp["sgu_ln_b"][0][None, :], (128, 1024))
    cB[:, B_SGB:B_SGB + 1024] = np.broadcast_to(inp["sgu_b"][0].reshape(1, 1024), (128, 1024))
    return wst, wdn, cA, cB


def build_program():
    nc = bass.Bass("TRN2", target_bir_lowering=False)
    xT_d = nc.dram_tensor("xT", [D, NTOK], F32, kind="ExternalInput").ap()
    wst_d = nc.dram_tensor("wst", [NGROUPS, 128, 4096], F32, kind="ExternalInput").ap()
    wdn_d = nc.dram_tensor("wdn", [8, 128, DFF], F32, kind="ExternalInput").ap()
    cA_d = nc.dram_tensor("cA", [128, NCA], F32, kind="ExternalInput").ap()
    cB_d = nc.dram_tensor("cB", [128, NCB], F32, kind="ExternalInput").ap()
    out_d = nc.dram_tensor("outT", [D, SEQ_CORE], F32, kind="ExternalOutput").ap()
    xT_v = xT_d.rearrange("(k p) c -> p k c", p=128)
    out_v = out_d.rearrange("(k p) c -> p k c", p=128)

    S = Sched()
    with ExitStack() as es:
        ARENA_BYTES = 206 * 1024
        arena = es.enter_context(nc.sbuf_tensor("arena", [128, ARENA_BYTES // 2], BF16))
        ps = [es.enter_context(nc.psum_tensor(f"ps{i}", [128, 512], F32)) for i in range(8)]
        eng_sems = {e: es.enter_context(nc.semaphore(f"s_{e}")) for e in ENGS}
        dma_keys = ["x", "x0", "x1", "x2", "xs", "out", "cA", "cB"] + [("ring", s) for s in range(RING)]
        dma_sems = {k: es.enter_context(nc.semaphore("d_" + (k if isinstance(k, str) else f"ring{k[1]}")))
                    for k in dma_keys}
        block = es.enter_context(nc.Block())

        def view(off, dt, K, C, name):
            esz = 4 if dt == F32 else 2
            nb = K * C * esz
            assert off % 4 == 0 and off + nb <= ARENA_BYTES, (name, off, nb)
            v = arena[:, off // 2:(off + nb) // 2]
            if dt == F32:
                v = v.bitcast(F32)
            v = v.rearrange("p (k c) -> p k c", k=K)
            return Buf(name, v, off, esz, K, C)

        o = 0
        R_A = o; o += 8 * NB * 4
        R_H = o; o += 8 * NB * 2
        R_C = o; o += 24 * NB * 2
        R_DG = o; o += 16 * 128 * 4
        R_SQ = o; o += 8 * 512 * 2
        R_RING = o; o += RING * 8192
        R_TMP = o; o += 16384
        R_DF = o; o += 24 * 32 * 2
        R_UPB = o; o += 8 * 1032 * 2
        R_K = o
        A = view(R_A, F32, 8, NB, "A")
        NBLK = view(R_A, BF16, 8, 1024, "nblk")
        GV = view(R_A + 16384, F32, 2, 1024, "gv")
        DIAGB = view(R_A, BF16, 248, 32, "diagB")
        TN2 = view(R_A + 24576, F32, 4, 512, "tn2")
        HT = view(R_H, BF16, 8, NB, "hT")
        U = view(R_C, BF16, 8, NB, "u")
        Z = view(R_C + 8 * NB * 2, BF16, 8, NB, "z")
        Y = view(R_C + 16 * NB * 2, BF16, 8, NB, "y")
        HH = view(R_C, BF16, 22, 1024, "hh")
        CB = view(R_C + 16 * NB * 2, F32, 1, NCB, "cB")
        DG = view(R_DG, F32, 16, 128, "dg")
        SQ = view(R_SQ, BF16, 8, 512, "sq")
        R12 = view(R_TMP, F32, 4, 512, "r12")
        RINGB = [view(R_RING + s * 8192, BF16, 1, 4096, f"ring{s}") for s in range(RING)]
        SGT = view(R_TMP, F32, 4, 512, "sgt")
        T12 = view(R_TMP + 8192, F32, 4, 512, "t12")
        DIAGF = view(R_DF, BF16, 24, 32, "diagF")
        UPB = view(R_UPB, BF16, 8, 1032, "upb")
        XS = view(R_UPB, F32, 8, 512, "xs")
        k = R_K
        CA = view(k, F32, 1, NCA, "cA"); k += NCA * 4
        IDB = view(k, BF16, 1, 128, "identb"); k += 256
        ONESB = view(k, BF16, 1, 128, "onesb"); k += 256
        WGT = view(k, BF16, 8, 128, "wgT"); k += 2048
        TT = view(k, F32, 8, 128, "T"); k += 4096
        MHALF = view(k, F32, 1, 8, "mhalf"); k += 32
        SM = view(k, F32, 8, 32, "sm"); k += 1024
        SMH = view(k, F32, 2, 2, "smh"); k += 64
        HPROD = view(k, F32, 8, 31, "hprod"); k += 1024
        HY = view(k, F32, 8, 2, "hy"); k += 64
        HYF = view(k, F32, 8, 2, "hyf"); k += 64
        HYH = view(k, BF16, 8, 2, "hyh"); k += 32
        HYL = view(k, BF16, 8, 2, "hyl"); k += 32
        I32B = view(k, BF16, 1, 32, "i32b"); k += 64
        ONESF = view(k, F32, 1, 128, "onesf"); k += 512
        ZC = view(k, BF16, 8, 32, "zc"); k += 512
        UPC = view(k, BF16, 48, 2, "upc"); k += 256
        STS = view(k, F32, 2, 12, "sts"); k += 128
        MV = view(k, F32, 2, 4, "mv"); k += 64
        assert k <= ARENA_BYTES, k

        def cacol(c):
            return CA.ap[:, 0, c:c + 1]

        bank_ctr = [0]
        reserved = set()

        def nb():
            while True:
                b = bank_ctr[0] % 8
                bank_ctr[0] += 1
                if b not in reserved:
                    return b

        def pe_mm(out, lhsT, rhs, start, stop, reads, bank, tp=None):
            if tp is None:
                S.add("pe", lambda h: h.matmul(out, lhsT=lhsT, rhs=rhs, start=start, stop=stop),
                      reads=reads, writes=[("ps", bank)])
            else:
                S.add("pe", lambda h: h.matmul(out, lhsT=lhsT, rhs=rhs, start=start, stop=stop, tile_position=tp),
                      reads=reads, writes=[("ps", bank)])

        def act(out, in_, func, reads, writes, scale=None, bias=None):
            kw = {}
            if scale is not None:
                kw["scale"] = scale
            if bias is not None:
                kw["bias"] = bias
            S.add("act", lambda h: h.activation(out=out, in_=in_, func=func, **kw), reads=reads, writes=writes)

        def tt(eng, out, in0, in1, op, reads, writes):
            S.add(eng, lambda h: h.tensor_tensor(out=out, in0=in0, in1=in1, op=op), reads=reads, writes=writes)

        def ts(eng, out, in0, s1, s2, op0, op1, reads, writes):
            if s2 is None:
                S.add(eng, lambda h: h.tensor_scalar(out=out, in0=in0, scalar1=s1, scalar2=None, op0=op0),
                      reads=reads, writes=writes)
            else:
                S.add(eng, lambda h: h.tensor_scalar(out=out, in0=in0, scalar1=s1, scalar2=s2, op0=op0, op1=op1),
                      reads=reads, writes=writes)

        def stt(eng, out, in0, scalar, in1, op0, op1, reads, writes):
            S.add(eng, lambda h: h.scalar_tensor_tensor(out=out, in0=in0, scalar=scalar, in1=in1, op0=op0, op1=op1),
                  reads=reads, writes=writes)

        def cp(eng, out, in_, reads, writes):
            S.add(eng, lambda h: h.tensor_copy(out=out, in_=in_), reads=reads, writes=writes)

        def memset(eng, buf, val):
            S.add(eng, lambda h: h.memset(buf.ap[:, :, :], val), writes=[buf.rng()])

        stream = []
        for p in range(2):
            for g in range(NGROUPS):
                stream.append((wst_d[g], 4096))
            for rep in range(1 if p == 0 else 2):
                for dc in range(8):
                    stream.append((wdn_d[dc], DFF))
        issued = [0]

        def ring_issue(upto):
            while issued[0] <= min(upto, len(stream) - 1):
                L = issued[0]
                s = L % RING
                src, w = stream[L]
                dst = RINGB[s].ap[:, 0, 0:w]
                S.add("pool", (lambda dst, src: lambda h: h.dma_start(out=dst, in_=src))(dst, src),
                      writes=[RINGB[s].rng()], dma_key=("ring", s))
                issued[0] += 1

        load_ctr = [0]

        def ring_next(prefetch=RING - 1):
            L = load_ctr[0]
            load_ctr[0] += 1
            ring_issue(L + prefetch)
            return RINGB[L % RING]

        ring_issue(RING - 1)
        S.add("sp", lambda h: h.dma_start(out=CA.ap[:, 0, :], in_=cA_d), writes=[CA.rng()], dma_key="cA")
        S.add("sp", lambda h: h.dma_start(out=CB.ap[:, 0, :], in_=cB_d), writes=[CB.rng()], dma_key="cB")
        memset("pool", ONESB, 1.0 / 1024.0)
        memset("pool", MHALF, -0.5)
        memset("pool", ONESF, 1.0)
        cp("dve", IDB.ap[:, 0, :], CA.ap[:, 0, C_ID:C_ID + 128], [CA.rng()], [IDB.rng()])
        cp("dve", I32B.ap[:, 0, :], CA.ap[:, 0, C_I32:C_I32 + 32], [CA.rng()], [I32B.rng()])
        sgw = CB.ap[:, 0, B_SGW:B_SGW + 1024].rearrange("p (g i) -> p g i", g=8)
        maskb = CB.ap[:, 0, B_MASK:B_MASK + 128].unsqueeze(1).to_broadcast([128, 8, 128])
        tt("dve", sgw, sgw, maskb, ALU.mult, [CB.rng()], [CB.rng()])
        cp("dve", WGT.ap[:, :, :], sgw, [CB.rng()], [WGT.rng()])
        betaB = CB.ap[:, 0, B_BETA:B_BETA + 1024].rearrange("p (g i) -> p g i", g=8)
        sgb = CB.ap[:, 0, B_SGB:B_SGB + 1024].rearrange("p (g i) -> p g i", g=8)
        for g in range(8):
            b = nb()
            pe_mm(ps[b][:, 0:128], betaB[:, g, :], sgw[:, g, :], True, False, [CB.rng()], b)
            pe_mm(ps[b][:, 0:128], ONESF.ap[0:1, 0, :], sgb[0:1, g, :], False, True, [CB.rng(), ONESF.rng()], b)
            cp("dve", TT.ap[:, g, :], ps[b][:, 0:128], [("ps", b)], [TT.rng(g, g + 1)])

        def token_stats(srcfn, nstat, n, b):
            nq = (n + 127) // 128
            for q in range(nq):
                m = min(128, n - 128 * q)
                for s_ in range(nstat):
                    col = q * nstat + s_
                    for k in range(8):
                        ap_, rd = srcfn(s_, k, 128 * q, m)
                        pe_mm(ps[b][0:m, col:col + 1], ap_, ONESB.ap[:, 0, 0:1], k == 0, k == 7,
                              [ONESB.rng(), rd], b)

        dg_ctr = [0]

        def bcast_diag(colbuf, par, col0, n):
            nq = (n + 127) // 128
            mm = min(128, n)
            slot = dg_ctr[0] % 4
            dg_ctr[0] += 1
            dgv = DG.ap[0:mm, slot * 4:slot * 4 + nq, :]
            idb = CA.ap[0:mm, 0, C_ID:C_ID + 128].unsqueeze(1).to_broadcast([mm, nq, 128])
            cb = colbuf.ap[0:mm, par, col0:col0 + nq].unsqueeze(2).to_broadcast([mm, nq, 128])
            tt("dve", dgv, idb, cb, ALU.mult, [CA.rng(), colbuf.rng(par, par + 1, col0, col0 + nq)],
               [DG.rng(slot * 4, slot * 4 + nq)])
            return slot

        def bcast_mm(slot, n):
            nq = (n + 127) // 128
            b2 = nb()
            for q in range(nq):
                m = min(128, n - 128 * q)
                pe_mm(ps[b2][:, 128 * q:128 * q + m], ONESF.ap[0:m, 0, :], DG.ap[0:m, slot * 4 + q, 0:m], True, True,
                      [ONESF.rng(), DG.rng(slot * 4 + q, slot * 4 + q + 1)], b2)
            return b2

        sm_ctr = [0]

        def rms_a1(c0, n, src=None, sc0=None):
            src = A if src is None else src
            sc0 = c0 if sc0 is None else sc0
            act(SQ.ap[:, :, 0:n], src.ap[:, :, sc0:sc0 + n], AF.Square, [src.rng(c0=sc0, c1=sc0 + n)], [SQ.rng(c0=0, c1=n)])

        def rms_a2(c0, n):
            b = nb()
            token_stats(lambda s_, k, t0, m: (SQ.ap[:, k, t0:t0 + m], SQ.rng(k, k + 1, t0, t0 + m)), 1, n, b)
            nq = (n + 127) // 128
            mm = min(128, n)
            par = sm_ctr[0] % 8
            sm_ctr[0] += 1
            ts("dve", SM.ap[0:mm, par, 0:nq], ps[b][0:mm, 0:nq], EPS, None, ALU.add, None, [("ps", b)], [SM.rng(par, par + 1, 0, nq)])
            tt("pool", SM.ap[0:mm, par, 4:4 + nq], SM.ap[0:mm, par, 0:nq], MHALF.ap[0:mm, 0, 0:nq], ALU.pow,
               [SM.rng(par, par + 1, 0, nq), MHALF.rng()], [SM.rng(par, par + 1, 4, 4 + nq)])
            return bcast_diag(SM, par, 4, n)

        def rms_b(gc, dst, c0, n, slot, src=None, sc0=None):
            src = A if src is None else src
            sc0 = c0 if sc0 is None else sc0
            b2 = bcast_mm(slot, n)
            for k in range(8):
                stt("dve", dst.ap[:, k, c0:c0 + n], src.ap[:, k, sc0:sc0 + n], cacol(gc + k), ps[b2][:, 0:n],
                    ALU.mult, ALU.mult,
                    [src.rng(k, k + 1, sc0, sc0 + n), CA.rng(), ("ps", b2)],
                    [dst.rng(k, k + 1, c0, c0 + n)])

        def rmsnorm_tile(gc, dst, c0, n):
            rms_a1(c0, n)
            slot = rms_a2(c0, n)
            rms_b(gc, dst, c0, n, slot)

        deferred = []

        def flush_deferred(nmax=1):
            for _ in range(nmax):
                if deferred:
                    deferred.pop(0)()

        def ring4(slot):
            return slot.ap[:, 0, :].rearrange("p (c k m) -> p c k m", c=4, k=8)

        for p in range(2):
            MT = [(128, 512), (640, 512)]
            ET = ([(126, 2)] if p == 0 else []) + MT
            ZT = ([(96, 32)] if p == 0 else []) + MT
            XT = ([(0, 128)] if p == 0 else []) + MT
            lo = 0 if p == 0 else 128
            doff = 1024 * p

            def load_x(lo=lo, doff=doff):
                S.add("sp", lambda h, lo=lo, doff=doff: h.dma_start(out=A.ap[:, :, lo:NB], in_=xT_v[:, :, lo + doff:NB + doff]),
                      writes=[A.rng(c0=lo, c1=NB)], dma_key="x")

            if p == 0:
                for ti, (c0, n) in enumerate(XT):
                    S.add("sp", (lambda c0, n: lambda h: h.dma_start(out=A.ap[:, :, c0:c0 + n], in_=xT_v[:, :, c0:c0 + n]))(c0, n),
                          writes=[A.rng(c0=c0, c1=c0 + n)], dma_key=f"x{ti}")
                for (c0, n) in XT:
                    rmsnorm_tile(C_G1, HT, c0, n)

            zpar = 0
            for g in range(4):
                slot = ring_next()
                w4 = ring4(slot)
                if g == 2:
                    while deferred:
                        flush_deferred()
                    for h2 in range(2):
                        dg = DIAGB.ap[:, h2 * 124:(h2 + 1) * 124, :]
                        i32b = CA.ap[:, 0, C_I32:C_I32 + 32].unsqueeze(1).to_broadcast([128, 124, 32])
                        wbc = CA.ap[:, 0, C_CMW + h2 * 124:C_CMW + (h2 + 1) * 124].unsqueeze(2).to_broadcast([128, 124, 32])
                        tt("pool", dg, i32b, wbc, ALU.mult, [CA.rng()], [DIAGB.rng(h2 * 124, (h2 + 1) * 124)])
                units = [(i, t) for i in range(2) for t in ZT]
                if g == 0:
                    units = [(i, t) for t in ZT for i in range(2)]
                for (i, (c0, n)) in units:
                    c = 2 * g + i
                    if True:
                        bb = nb()
                        for k in range(8):
                            pe_mm(ps[bb][:, 0:n], w4[:, 2 * i + 1, k, :], HT.ap[:, k, c0:c0 + n], k == 0, k == 7,
                                  [slot.rng(), HT.rng(k, k + 1, c0, c0 + n)], bb)
                        sp_ = zpar % 4
                        zpar += 1
                        act(SGT.ap[:, sp_, 0:n], ps[bb][:, 0:n], AF.Sigmoid, [("ps", bb)], [SGT.rng(sp_, sp_ + 1, 0, n)])
                        ba = nb()
                        for k in range(8):
                            pe_mm(ps[ba][:, 0:n], w4[:, 2 * i, k, :], HT.ap[:, k, c0:c0 + n], k == 0, k == 7,
                                  [slot.rng(), HT.rng(k, k + 1, c0, c0 + n)], ba)
                        tt("dve", Z.ap[:, c, c0:c0 + n], ps[ba][:, 0:n], SGT.ap[:, sp_, 0:n], ALU.mult,
                           [("ps", ba), SGT.rng(sp_, sp_ + 1, 0, n)], [Z.rng(c, c + 1, c0, c0 + n)])
                        flush_deferred()
            if p == 0:
                cp("pool", ZC.ap[:, :, :], Z.ap[:, :, NB - 32:NB], [Z.rng(c0=NB - 32, c1=NB)], [ZC.rng()])
            else:
                cp("pool", Z.ap[:, :, 96:128], ZC.ap[:, :, :], [ZC.rng()], [Z.rng(c0=96, c1=128)])

            ln_par = {}

            def conv_unit(h2, c0, n):
                bks = [nb() for _ in range(4)]
                for k in range(31):
                    s0 = c0 - 30 + k
                    for r in range(4):
                        for c in range(4):
                            di = h2 * 124 + c * 31 + k
                            pe_mm(ps[bks[r]][32 * c:32 * c + 32, 0:n], DIAGB.ap[32 * r:32 * r + 32, di, :],
                                  Z.ap[32 * r:32 * r + 32, 4 * h2 + c, s0:s0 + n], k == 0, k == 30,
                                  [DIAGB.rng(di, di + 1), Z.rng(4 * h2 + c, 4 * h2 + c + 1, s0, s0 + n)], bks[r],
                                  tp=(32 * r, 32 * c))
                for r in range(4):
                    ch = 4 * h2 + r
                    act(Y.ap[:, ch, c0:c0 + n], ps[bks[r]][:, 0:n], AF.Identity, [("ps", bks[r]), CA.rng()],
                        [Y.rng(ch, ch + 1, c0, c0 + n)], bias=cacol(C_CMB + ch))

            def halo_conv():
                cmw3 = CA.ap[:, 0, C_CMW:C_CMW + 248].rearrange("p (s k) -> p s k", s=8)
                for t_idx, t in enumerate((126, 127)):
                    tt("dve", HPROD.ap[:, :, :], Z.ap[:, :, t - 30:t + 1], cmw3, ALU.mult,
                       [Z.rng(c0=t - 30, c1=t + 1), CA.rng()], [HPROD.rng()])
                    S.add("dve", (lambda o_, i_: lambda h: h.tensor_reduce(out=o_, in_=i_, axis=mybir.AxisListType.X, op=ALU.add))(
                        HY.ap[:, :, t_idx], HPROD.ap[:, :, :]), reads=[HPROD.rng()], writes=[HY.rng()])
                cp("dve", HYH.ap[:, :, :], HY.ap[:, :, :], [HY.rng()], [HYH.rng()])
                cp("dve", HYF.ap[:, :, :], HYH.ap[:, :, :], [HYH.rng()], [HYF.rng()])
                tt("dve", HYL.ap[:, :, :], HY.ap[:, :, :], HYF.ap[:, :, :], ALU.subtract, [HY.rng(), HYF.rng()], [HYL.rng()])
                for h2 in range(2):
                    bks = [nb() for _ in range(4)]
                    for r in range(4):
                        for c in range(4):
                            for part, (src, first) in enumerate(((HYH, True), (HYL, False))):
                                pe_mm(ps[bks[r]][32 * c:32 * c + 32, 0:2], I32B.ap[32 * r:32 * r + 32, 0, :],
                                      src.ap[32 * r:32 * r + 32, 4 * h2 + c, 0:2], first, not first,
                                      [I32B.rng(), src.rng()], bks[r], tp=(32 * r, 32 * c))
                    for r in range(4):
                        ch = 4 * h2 + r
                        act(Y.ap[:, ch, 126:128], ps[bks[r]][:, 0:2], AF.Identity, [("ps", bks[r]), CA.rng()],
                            [Y.rng(ch, ch + 1, 126, 128)], bias=cacol(C_CMB + ch))

            def ln_sq(c0, n):
                act(SQ.ap[:, :, 0:n], Y.ap[:, :, c0:c0 + n], AF.Square, [Y.rng(c0=c0, c1=c0 + n)], [SQ.rng(c0=0, c1=n)])

            def ln_stats(ti, c0, n):
                b = nb()

                def srcfn(s_, k, t0, m, c0=c0):
                    if s_ == 0:
                        return Y.ap[:, k, c0 + t0:c0 + t0 + m], Y.rng(k, k + 1, c0 + t0, c0 + t0 + m)
                    return SQ.ap[:, k, t0:t0 + m], SQ.rng(k, k + 1, t0, t0 + m)

                token_stats(srcfn, 2, n, b)
                nq = (n + 127) // 128
                mm = min(128, n)
                par = sm_ctr[0] % 8
                sm_ctr[0] += 1
                cp("dve", SM.ap[0:mm, par, 0:2 * nq], ps[b][0:mm, 0:2 * nq], [("ps", b)], [SM.rng(par, par + 1, 0, 2 * nq)])
                st2 = SM.ap[0:mm, par, 0:2 * nq].rearrange("p (q s) -> p q s", s=2)
                mean, e2 = st2[:, :, 0], st2[:, :, 1]
                tt("dve", SM.ap[0:mm, par, 8:8 + nq], mean, mean, ALU.mult, [SM.rng(par, par + 1, 0, 8)], [SM.rng(par, par + 1, 8, 8 + nq)])
                tt("dve", SM.ap[0:mm, par, 12:12 + nq], e2, SM.ap[0:mm, par, 8:8 + nq], ALU.subtract,
                   [SM.rng(par, par + 1, 0, 12)], [SM.rng(par, par + 1, 12, 12 + nq)])
                ts("dve", SM.ap[0:mm, par, 12:12 + nq], SM.ap[0:mm, par, 12:12 + nq], 0.0, EPS, ALU.max, ALU.add,
                   [SM.rng(par, par + 1, 12, 12 + nq)], [SM.rng(par, par + 1, 12, 12 + nq)])
                tt("pool", SM.ap[0:mm, par, 16:16 + nq], SM.ap[0:mm, par, 12:12 + nq], MHALF.ap[0:mm, 0, 0:nq], ALU.pow,
                   [SM.rng(par, par + 1, 12, 16), MHALF.rng()], [SM.rng(par, par + 1, 16, 16 + nq)])
                stt("dve", SM.ap[0:mm, par, 20:20 + nq], mean, -1.0, SM.ap[0:mm, par, 16:16 + nq], ALU.mult, ALU.mult,
                    [SM.rng(par, par + 1, 0, 20)], [SM.rng(par, par + 1, 20, 20 + nq)])
                ln_par[ti] = par

            lnap_ctr = [0]

            def ln_apply(tiles):
                for ti, (c0, n) in tiles:
                    s1 = bcast_diag(SM, ln_par[ti], 16, n)
                    s2 = bcast_diag(SM, ln_par[ti], 20, n)
                    b1 = bcast_mm(s1, n)
                    b2 = bcast_mm(s2, n)
                    if n == 2:
                        r1, r2 = SMH.ap[:, 0, 0:2], SMH.ap[:, 1, 0:2]
                        rr1, rr2 = SMH.rng(0, 1), SMH.rng(1, 2)
                    else:
                        i1, i2 = 2 * (ti % 2), 2 * (ti % 2) + 1
                        r1, r2 = R12.ap[:, i1, 0:n], R12.ap[:, i2, 0:n]
                        rr1, rr2 = R12.rng(i1, i1 + 1, 0, n), R12.rng(i2, i2 + 1, 0, n)
                    act(r1, ps[b1][:, 0:n], AF.Copy, [("ps", b1)], [rr1])
                    act(r2, ps[b2][:, 0:n], AF.Copy, [("ps", b2)], [rr2])
                    for c in range(8):
                        q = lnap_ctr[0] % 4
                        lnap_ctr[0] += 1
                        tt("dve", TN2.ap[:, q, 0:n], Y.ap[:, c, c0:c0 + n], r1, ALU.mult,
                           [Y.rng(c, c + 1, c0, c0 + n), rr1], [TN2.rng(q, q + 1, 0, n)])
                        tt("dve" if c % 4 == 3 else "pool", TN2.ap[:, q, 0:n], TN2.ap[:, q, 0:n], r2, ALU.add,
                           [TN2.rng(q, q + 1, 0, n), rr2], [TN2.rng(q, q + 1, 0, n)])
                        act(Z.ap[:, c, c0:c0 + n], TN2.ap[:, q, 0:n], AF.Silu, [TN2.rng(q, q + 1, 0, n), CA.rng()],
                            [Z.rng(c, c + 1, c0, c0 + n)], scale=cacol(C_CMG + c), bias=cacol(C_CMBT + c))

            ln_pending = None
            for ti, (c0, n) in enumerate(ET):
                if n == 2:
                    halo_conv()
                    ln_sq(c0, n)
                    ln_pending = (ti, c0, n)
                    continue
                conv_unit(0, c0, n)
                if ln_pending is not None:
                    ln_stats(*ln_pending)
                    ln_pending = None
                conv_unit(1, c0, n)
                ln_sq(c0, n)
                ln_pending = (ti, c0, n)
            ln_apply(list(enumerate(ET))[:-1])

            for g in range(2):
                slot = ring_next()
                w4 = ring4(slot)
                for i in range(4):
                    uc = 4 * g + i
                    for (c0, n) in ET:
                        b = nb()
                        for k in range(8):
                            pe_mm(ps[b][:, 0:n], w4[:, i, k, :], HT.ap[:, k, c0:c0 + n], k == 0, k == 7,
                                  [slot.rng(), HT.rng(k, k + 1, c0, c0 + n)], b)
                        act(U.ap[:, uc, c0:c0 + n], ps[b][:, 0:n], AF.Gelu, [("ps", b)], [U.rng(uc, uc + 1, c0, c0 + n)])
                    if g == 0 and i == 0:
                        ln_stats(*ln_pending)
                    if g == 0 and i == 1:
                        ln_apply(list(enumerate(ET))[-1:])

            slot0 = ring_next(prefetch=RING - 1)
            slot1 = ring_next(prefetch=RING - 2)
            wv = [ring4(slot0), ring4(slot1)]
            wslots = [slot0, slot1]
            sgu_q = []
            blocks = ([0] if p == 0 else []) + list(range(1, 9))
            nslot_of = {}
            for bi, blk in enumerate(blocks):
                cb = 128 * blk
                par = bi % 2
                ns = bi % 8
                nslot_of[blk] = ns
                for hh in range(2):
                    b = nb()
                    for k in range(8):
                        pe_mm(ps[b][:, :].rearrange("p (c m) -> p c m", c=4), HT.ap[:, k, cb:cb + 128], wv[hh][:, :, k, :],
                              k == 0, k == 7, [wslots[hh].rng(), HT.rng(k, k + 1, cb, cb + 128)], b)
                    act(GV.ap[:, par, hh * 512:(hh + 1) * 512], ps[b][:, :], AF.Gelu, [("ps", b)],
                        [GV.rng(par, par + 1, hh * 512, (hh + 1) * 512)])
                st3 = STS.ap[:, par, :].rearrange("p (a b) -> p a b", a=2)
                for hh in range(2):
                    S.add("dve", (lambda o_, i_: lambda h: h.bn_stats(out=o_, in_=i_))(st3[:, hh, :], GV.ap[:, par, hh * 512:(hh + 1) * 512]),
                          reads=[GV.rng(par, par + 1, hh * 512, (hh + 1) * 512)], writes=[STS.rng(par, par + 1, hh * 6, hh * 6 + 6)])
                S.add("dve", (lambda o_, i_: lambda h: h.bn_aggr(out=o_, in_=i_))(MV.ap[:, par, 0:2], st3),
                      reads=[STS.rng(par, par + 1)], writes=[MV.rng(par, par + 1, 0, 2)])
                ts("dve", MV.ap[:, par, 2:3], MV.ap[:, par, 1:2], EPS, None, ALU.add, None,
                   [MV.rng(par, par + 1, 0, 2)], [MV.rng(par, par + 1, 2, 3)])
                tt("pool", MV.ap[:, par, 3:4], MV.ap[:, par, 2:3], MHALF.ap[:, 0, 0:1], ALU.pow,
                   [MV.rng(par, par + 1, 2, 3), MHALF.rng()], [MV.rng(par, par + 1, 3, 4)])
                ts("dve", NBLK.ap[:, ns, :], GV.ap[:, par, :], MV.ap[:, par, 0:1], MV.ap[:, par, 3:4], ALU.subtract, ALU.mult,
                   [GV.rng(par, par + 1), MV.rng(par, par + 1)], [NBLK.rng(ns, ns + 1)])
                tile = None
                if p == 0 and blk == 0:
                    tile = (126, 2, [0])
                elif blk in (4, 8):
                    c0 = 128 if blk == 4 else 640
                    tile = (c0, 512, [blk - 3, blk - 2, blk - 1, blk])
                if tile is not None:
                    c0, n, tb = tile
                    for g in range(8):
                        def sgu_item(g=g, c0=c0, n=n, tb=tb, nsl=dict(nslot_of)):
                            b = nb()
                            tq = g % 4
                            if n == 2:
                                pe_mm(ps[b][:, 0:2], NBLK.ap[:, nsl[0], g * 128:(g + 1) * 128], WGT.ap[:, g, 126:128], True, True,
                                      [NBLK.rng(nsl[0], nsl[0] + 1), WGT.rng(g, g + 1)], b)
                                tin1 = TT.ap[:, g, 126:128]
                                pin, tout = ps[b][:, 0:2], T12.ap[:, tq, 0:2]
                                uin, uout = U.ap[:, g, 126:128], U.ap[:, g, 126:128]
                            else:
                                for q, bq in enumerate(tb):
                                    pe_mm(ps[b][:, q * 128:(q + 1) * 128], NBLK.ap[:, nsl[bq], g * 128:(g + 1) * 128], WGT.ap[:, g, :],
                                          True, True, [NBLK.rng(nsl[bq], nsl[bq] + 1), WGT.rng(g, g + 1)], b)
                                tin1 = TT.ap[:, g, :].unsqueeze(1).to_broadcast([128, 4, 128])
                                pin = ps[b][:, :].rearrange("p (a b) -> p a b", a=4)
                                tout = T12.ap[:, tq, :].rearrange("p (a b) -> p a b", a=4)
                                uin = U.ap[:, g, c0:c0 + n].rearrange("p (a b) -> p a b", a=4)
                                uout = uin
                            stt("dve", tout, pin, cacol(C_GAMA + g), tin1, ALU.mult, ALU.add,
                                [("ps", b), CA.rng(), TT.rng(g, g + 1)], [T12.rng(tq, tq + 1, 0, n)])
                            tt("pool", uout, tout, uin, ALU.mult,
                               [T12.rng(tq, tq + 1, 0, n), U.rng(g, g + 1, c0, c0 + n)], [U.rng(g, g + 1, c0, c0 + n)])
                        sgu_q.append(sgu_item)
                    if n == 2:
                        while sgu_q:
                            sgu_q.pop(0)()
                else:
                    for _ in range(2):
                        if sgu_q:
                            sgu_q.pop(0)()
            for _ in range(4):
                if sgu_q:
                    sgu_q.pop(0)()

            sp_ctr = 0
            for dc in range(8):
                slot = ring_next()
                w4 = ring4(slot)
                for (c0, n) in ET:
                    if dc == 0 and (c0, n) == ET[-1]:
                        while sgu_q:
                            sgu_q.pop(0)()
                        load_x()
                    srcs = [HT, HT, U, Z]
                    banks = []
                    for i in range(4):
                        b = nb()
                        banks.append(b)
                        for k in range(8):
                            pe_mm(ps[b][:, 0:n], w4[:, i, k, :], srcs[i].ap[:, k, c0:c0 + n], k == 0, k == 7,
                                  [slot.rng(), srcs[i].rng(k, k + 1, c0, c0 + n)], b)
                        if i < 2:
                            q = (sp_ctr * 2 + i) % 4
                            act(SGT.ap[:, q, 0:n], ps[b][:, 0:n], AF.Sigmoid, [("ps", b)], [SGT.rng(q, q + 1, 0, n)])
                        else:
                            q = (sp_ctr * 2 + i - 2) % 4
                            tq = (sp_ctr * 2 + i - 2) % 4
                            tt("dve", T12.ap[:, tq, 0:n], ps[b][:, 0:n], SGT.ap[:, q, 0:n], ALU.mult,
                               [("ps", b), SGT.rng(q, q + 1, 0, n)], [T12.rng(tq, tq + 1, 0, n)])
                    t1q = (sp_ctr * 2) % 4
                    t2q = (sp_ctr * 2 + 1) % 4
                    tt("pool", Y.ap[:, dc, c0:c0 + n], T12.ap[:, t1q, 0:n], T12.ap[:, t2q, 0:n], ALU.add,
                       [T12.rng(t1q, t1q + 1, 0, n), T12.rng(t2q, t2q + 1, 0, n)], [Y.rng(dc, dc + 1, c0, c0 + n)])
                    sp_ctr += 1

            slots4 = [ring_next(), ring_next(prefetch=RING - 2)]
            pend_a2 = None
            pend_b = None
            for (c0, n) in ET:
                for dc in range(8):
                    slot = slots4[dc // 4]
                    w4 = ring4(slot)
                    b = nb()
                    for k in range(8):
                        pe_mm(ps[b][:, 0:n], w4[:, dc % 4, k, :], Y.ap[:, k, c0:c0 + n], k == 0, k == 7,
                              [slot.rng(), Y.rng(k, k + 1, c0, c0 + n)], b)
                    tt("dve", A.ap[:, dc, c0:c0 + n], ps[b][:, 0:n], A.ap[:, dc, c0:c0 + n], ALU.add,
                       [("ps", b), A.rng(dc, dc + 1, c0, c0 + n)], [A.rng(dc, dc + 1, c0, c0 + n)])
                    if dc == 2 and pend_a2 is not None:
                        pc0, pn = pend_a2
                        pend_b = (pc0, pn, rms_a2(pc0, pn))
                        pend_a2 = None
                    if dc == 6 and pend_b is not None:
                        pc0, pn, sl = pend_b
                        rms_b(C_G2, HT, pc0, pn, sl)
                        pend_b = None
                rms_a1(c0, n)
                pend_a2 = (c0, n)

            for G in range(6):
                ncg = NCH_G[G]
                slots = [ring_next(), ring_next(prefetch=RING - 2)]
                for half in range(2):
                    dgs = DIAGF.ap[:, half * 12:(half + 1) * 12, :]
                    i32b = CA.ap[:, 0, C_I32:C_I32 + 32].unsqueeze(1).to_broadcast([128, 12, 32])
                    t0_ = C_FDW + (half * 6 + G) * 12
                    wbc = CA.ap[:, 0, t0_:t0_ + 12].unsqueeze(2).to_broadcast([128, 12, 32])
                    tt("pool", dgs, i32b, wbc, ALU.mult, [CA.rng()], [DIAGF.rng(half * 12, (half + 1) * 12)])
                for half in range(2):
                    slot = slots[half]
                    w4 = ring4(slot)

                    def up_unit(c, c0, n, half=half, slot=slot, w4=w4):
                        ui = half * 4 + c
                        b = nb()
                        for k in range(8):
                            pe_mm(ps[b][:, 0:n], w4[:, c, k, :], HT.ap[:, k, c0:c0 + n], k == 0, k == 7,
                                  [slot.rng(), HT.rng(k, k + 1, c0, c0 + n)], b)
                        u0 = c0 - 126
                        if n == 2:
                            act(UPB.ap[:, ui, u0:u0 + n], ps[b][:, 0:n], AF.Copy, [("ps", b), CA.rng()],
                                [UPB.rng(ui, ui + 1, u0, u0 + n)], scale=cacol(C_FLAG))
                        else:
                            act(UPB.ap[:, ui, u0:u0 + n], ps[b][:, 0:n], AF.Copy, [("ps", b)],
                                [UPB.rng(ui, ui + 1, u0, u0 + n)])

                    if G == 0 and half == 0:
                        lc0, ln_ = pend_a2
                        for (c0, n) in ET[:-1]:
                            for c in range(4):
                                up_unit(c, c0, n)
                                if (c0, n) == ET[-2] and c == 0:
                                    pend_b = (lc0, ln_, rms_a2(lc0, ln_))
                                if (c0, n) == ET[-2] and c == 2:
                                    rms_b(C_G2, HT, lc0, ln_, pend_b[2])
                        for c in range(4):
                            up_unit(c, lc0, ln_)
                    else:
                        for c in range(4):
                            for (c0, n) in ET:
                                up_unit(c, c0, n)
                    for c in range(4):
                        ui = half * 4 + c
                        cidx = half * 24 + G * 4 + c
                        if p == 0:
                            cp("pool", UPC.ap[:, cidx, :], UPB.ap[:, ui, 1024:1026], [UPB.rng(ui, ui + 1, 1024, 1026)],
                               [UPC.rng(cidx, cidx + 1)])
                        else:
                            cp("pool", UPB.ap[:, ui, 0:2], UPC.ap[:, cidx, :], [UPC.rng(cidx, cidx + 1)],
                               [UPB.rng(ui, ui + 1, 0, 2)])
                for (c0, n) in MT:
                    u0 = c0 - 126
                    for half in range(2):
                        bks = [nb() for _ in range(ncg)]
                        for k in range(3):
                            s0 = u0 - 2 + k
                            for r in range(ncg):
                                for c in range(4):
                                    di = half * 12 + c * 3 + k
                                    ui = half * 4 + c
                                    pe_mm(ps[bks[r]][32 * c:32 * c + 32, 0:n], DIAGF.ap[32 * r:32 * r + 32, di, :],
                                          UPB.ap[32 * r:32 * r + 32, ui, s0:s0 + n], k == 0, k == 2,
                                          [DIAGF.rng(di, di + 1), UPB.rng(ui, ui + 1, s0, s0 + n)], bks[r],
                                          tp=(32 * r, 32 * c))
                        for r in range(ncg):
                            j = 4 * G + r
                            if half == 0:
                                act(SGT.ap[:, r, 0:n], ps[bks[r]][:, 0:n], AF.Silu, [("ps", bks[r]), CA.rng()],
                                    [SGT.rng(r, r + 1, 0, n)], bias=cacol(C_FDB + j))
                            else:
                                stt("dve", HH.ap[:, j, c0 - 128:c0 - 128 + n], ps[bks[r]][:, 0:n], cacol(C_FDB + 22 + j),
                                    SGT.ap[:, r, 0:n], ALU.add, ALU.mult, [("ps", bks[r]), CA.rng(), SGT.rng(r, r + 1, 0, n)],
                                    [HH.rng(j, j + 1, c0 - 128, c0 - 128 + n)])

            nxt = []
            if p == 0:
                for (c0, n) in MT:
                    def st_load(c0=c0, n=n):
                        S.add("sp", lambda h: h.dma_start(out=XS.ap[:, :, 0:n], in_=xT_v[:, :, c0 + 1024:c0 + 1024 + n]),
                              writes=[XS.rng(c0=0, c1=n)], dma_key="xs")
                        rms_a1(c0, n, src=XS, sc0=0)
                    st = {}

                    def st_a2(c0=c0, n=n, st=st):
                        st["slot"] = rms_a2(c0, n)

                    def st_b(c0=c0, n=n, st=st):
                        rms_b(C_G1, HT, c0, n, st["slot"], src=XS, sc0=0)
                    nxt += [st_load, st_a2, st_b]
            def wdown_unit(slot, dc, c0, n):
                wd3 = slot.ap[:, 0, 0:DFF].rearrange("p (k m) -> p k m", k=22)
                b = nb()
                for k in range(22):
                    pe_mm(ps[b][:, 0:n], wd3[:, k, :], HH.ap[:, k, c0 - 128:c0 - 128 + n], k == 0, k == 21,
                          [slot.rng(), HH.rng(k, k + 1, c0 - 128, c0 - 128 + n)], b)
                tt("dve", A.ap[:, dc, c0:c0 + n], ps[b][:, 0:n], A.ap[:, dc, c0:c0 + n], ALU.add,
                   [("ps", b), A.rng(dc, dc + 1, c0, c0 + n)], [A.rng(dc, dc + 1, c0, c0 + n)])

            if p == 0:
                for dc in range(8):
                    slot = ring_next()
                    for (c0, n) in MT:
                        wdown_unit(slot, dc, c0, n)
                        if nxt and dc >= 1 and (c0, n) == MT[0]:
                            nxt.pop(0)()

            def fin_out(c0, n, doff=doff):
                d0 = c0 - 128 + doff
                S.add("sp", (lambda c0, n, d0: lambda h: h.dma_start(out=out_v[:, :, d0:d0 + n], in_=A.ap[:, :, c0:c0 + n]))(c0, n, d0),
                      reads=[A.rng(c0=c0, c1=c0 + n)], writes=[("dram", "out")], dma_key="out")

            if p == 0:
                for (c0, n) in MT:
                    st = {}

                    def f_a(c0=c0, n=n, st=st):
                        rms_a1(c0, n)
                        st["slot"] = rms_a2(c0, n)

                    def f_b(c0=c0, n=n, st=st):
                        rms_b(C_GF, A, c0, n, st["slot"])
                        fin_out(c0, n)
                    deferred += [f_a, f_b]
                flush_deferred()
            else:
                (ca, na), (cb_, nb_) = MT
                for dc in range(8):
                    wdown_unit(ring_next(), dc, ca, na)
                rms_a1(ca, na)
                st0 = {}
                for dc in range(8):
                    wdown_unit(ring_next(), dc, cb_, nb_)
                    if dc == 1:
                        st0["slot"] = rms_a2(ca, na)
                    if dc == 3:
                        rms_b(C_GF, A, ca, na, st0["slot"])
                        fin_out(ca, na)
                rmsnorm_tile(C_GF, A, cb_, nb_)
                fin_out(cb_, nb_)

        S.add("sp", None, reads=[("dram", "out")])
        S.finalize()
        S.emit(block, eng_sems, dma_sems)
    return nc


_CACHE = {}


def kernel(**inputs):
    inp = {k: np.asarray(v) for k, v in inputs.items()}
    x = inp["x"].astype(np.float32, copy=False)
    wst, wdn, cA, cB = host_layout(inp)
    in_maps = []
    for core in range(NCORES):
        b, s = core // 2, core % 2
        t0 = s * SEQ_CORE
        xT = np.zeros((D, NTOK), np.float32)
        xT[:, HALO:] = x[b, t0:t0 + SEQ_CORE, :].T
        cAc = cA.copy()
        if s == 1:
            xT[:, :HALO] = x[b, t0 - HALO:t0, :].T
            cAc[:, C_FLAG] = 1.0
        in_maps.append({"xT": xT, "wst": wst, "wdn": wdn, "cA": cAc, "cB": cB})
    if "nc" not in _CACHE:
        _CACHE["nc"] = build_program()
    nc = _CACHE["nc"]
    res = run_bass_kernel_spmd(nc, in_maps, core_ids=list(range(NCORES)))
    out = np.empty((4, 4096, D), np.float32)
    for core in range(NCORES):
        b, s = core // 2, core % 2
        t0 = s * SEQ_CORE
        out[b, t0:t0 + SEQ_CORE, :] = res.results[core]["outT"].T
    return out
```

```python
from contextlib import ExitStack
import numpy as np
import concourse.bass as bass
import concourse.mybir as mybir
from concourse.bass_utils import run_bass_kernel_spmd

F32 = mybir.dt.float32
BF16 = mybir.dt.bfloat16
AF = mybir.ActivationFunctionType
ALU = mybir.AluOpType

NCORES = 8
D = 1024
SEQ_CORE = 2048
HALO = 128
NTOK = HALO + SEQ_CORE
NB = 1152
DFF = 2816
EPS = 1e-6
RING = 4

GRAN = 256
ENGS = ("pe", "act", "dve", "pool", "sp")


class Buf:
    def __init__(self, name, ap, off_bytes, es, K, C):
        self.name, self.ap, self.off, self.es, self.K, self.C = name, ap, off_bytes, es, K, C

    def rng(self, k0=0, k1=None, c0=0, c1=None):
        k1 = self.K if k1 is None else k1
        c1 = self.C if c1 is None else c1
        return [(self.off + (k * self.C + c0) * self.es, self.off + (k * self.C + c1) * self.es)
                for k in range(k0, k1)]


def gran_cells(ranges):
    cells = set()
    for a, b in ranges:
        cells.update(range(a // GRAN, (b - 1) // GRAN + 1))
    return cells


class Op:
    __slots__ = ("idx", "eng", "fn", "deps", "is_dma", "sem_key", "sem_val", "seqpos",
                 "marked", "count", "waits")

    def __init__(self):
        self.deps = {}
        self.is_dma = False
        self.marked = False
        self.count = 0
        self.waits = []
        self.sem_key = None
        self.sem_val = 0


class Sched:
    def __init__(self):
        self.ops = []
        self.last_w = {}
        self.readers = {}
        self.dma_counts = {}
        self.eng_len = {e: 0 for e in ENGS}

    @staticmethod
    def _cells(spec):
        cells = set()
        for item in spec:
            if isinstance(item, tuple):
                cells.add(item)
            else:
                cells |= gran_cells(item)
        return cells

    def add(self, eng, fn, reads=(), writes=(), dma_key=None):
        op = Op()
        op.idx = len(self.ops)
        op.eng = eng
        op.fn = fn
        op.seqpos = self.eng_len[eng]
        self.eng_len[eng] += 1
        if dma_key is not None:
            op.is_dma = True
            op.sem_key = dma_key
            self.dma_counts[dma_key] = self.dma_counts.get(dma_key, 0) + 16
            op.sem_val = self.dma_counts[dma_key]
        for c in self._cells(reads):
            w = self.last_w.get(c)
            if w is not None:
                op.deps[w] = "RAW"
            self.readers.setdefault(c, []).append(op.idx)
        for c in self._cells(writes):
            w = self.last_w.get(c)
            if w is not None and w != op.idx:
                op.deps.setdefault(w, "WAW")
            for r in self.readers.get(c, ()):
                if r != op.idx:
                    op.deps.setdefault(r, "WAR")
            self.readers[c] = []
            self.last_w[c] = op.idx
        self.ops.append(op)
        return op

    def finalize(self):
        known = {e: {f: -1 for f in ENGS} for e in ENGS}
        known_dma = {e: {} for e in ENGS}
        for op in self.ops:
            E = op.eng
            need = []
            best = {}
            for d, kind in op.deps.items():
                Dp = self.ops[d]
                if Dp.is_dma:
                    if known_dma[E].get(Dp.sem_key, 0) < Dp.sem_val:
                        need.append(("dma", Dp.sem_key, Dp.sem_val))
                        known_dma[E][Dp.sem_key] = Dp.sem_val
                    continue
                Fe = Dp.eng
                if Fe == E and not op.is_dma:
                    if kind != "RAW" or E == "pe":
                        continue
                if Dp.seqpos <= known[E][Fe]:
                    continue
                if Fe not in best or Dp.seqpos > best[Fe].seqpos:
                    best[Fe] = Dp
            for Fe, Dp in best.items():
                known[E][Fe] = Dp.seqpos
                Dp.marked = True
                need.append(("eng", Fe, Dp))
            op.waits = need
        cnt = {e: 0 for e in ENGS}
        for op in self.ops:
            if op.is_dma:
                continue
            if op.marked:
                cnt[op.eng] += 1
            op.count = cnt[op.eng]

    def emit(self, block, eng_sems, dma_sems):
        per = {e: [] for e in ENGS}
        for op in self.ops:
            per[op.eng].append(op)

        def run(eng_name, handle):
            for op in per[eng_name]:
                for w in op.waits:
                    if w[0] == "dma":
                        handle.wait_ge(dma_sems[w[1]], w[2])
                    else:
                        handle.wait_ge(eng_sems[w[1]], w[2].count)
                if op.fn is None:
                    continue
                ins = op.fn(handle)
                if op.is_dma:
                    ins.then_inc(dma_sems[op.sem_key], 16)
                elif op.marked:
                    ins.then_inc(eng_sems[op.eng], 1)

        @block.tensor
        def _(h):
            run("pe", h)

        @block.scalar
        def _(h):
            run("act", h)

        @block.vector
        def _(h):
            run("dve", h)

        @block.gpsimd
        def _(h):
            run("pool", h)

        @block.sync
        def _(h):
            run("sp", h)


C_G1, C_G2, C_GF, C_GAMA, C_CMB, C_CMG, C_CMBT = 0, 8, 16, 24, 32, 40, 48
C_CMW = 56
C_FDB = C_CMW + 8 * 31
C_FDW = C_FDB + 44
C_FLAG = C_FDW + 144
C_I32 = 496
C_ID = C_I32 + 32
NCA = C_ID + 128
B_SGW, B_MASK, B_BETA, B_SGB = 0, 1024, 1152, 2176
NCB = 3200

NGROUPS = 30
NCH_G = [4, 4, 4, 4, 4, 2]


def _chunk(W, col0):
    K = W.shape[0] // 128
    return W[:, col0:col0 + 128].reshape(K, 128, 128).transpose(1, 0, 2).reshape(128, K * 128)


def _col(v):
    n = v.shape[0] // 128
    return v.reshape(n, 128).T


def host_layout(inp):
    w_in = inp["w_in"][0]
    w_up = inp["w_up"][0]
    groups = []

    def grp(chs):
        groups.append(np.concatenate(chs, axis=1))

    pidx = np.arange(128)
    r_, i_ = pidx // 32, pidx % 32
    zperm = np.concatenate([(4 * (sl // 4) + r_) * 128 + 32 * (sl % 4) + i_ for sl in range(8)])
    w_ga = w_in[:, 2048:3072][:, zperm]
    w_gb = w_in[:, 3072:4096][:, zperm]
    for g in range(4):
        grp([_chunk(w_ga, 128 * (2 * g)), _chunk(w_gb, 128 * (2 * g)),
             _chunk(w_ga, 128 * (2 * g + 1)), _chunk(w_gb, 128 * (2 * g + 1))])
    for g in range(2):
        grp([_chunk(w_in, 128 * (4 * g + i)) for i in range(4)])
    for g in range(2):
        grp([_chunk(w_in, 1024 + 128 * (4 * g + i)) for i in range(4)])
    wa, wb, wo = inp["w_a_out"][0], inp["w_b_out"][0], inp["w_o"][0]
    for dc in range(8):
        grp([_chunk(w_in, 4096 + 128 * dc), _chunk(w_in, 5120 + 128 * dc),
             _chunk(wa, 128 * dc), _chunk(wb, 128 * dc)])
    for g in range(2):
        grp([_chunk(wo, 128 * (4 * g + i)) for i in range(4)])
    def ffperm(G, c):
        ch = (4 * G + r_) * 128 + 32 * c + i_
        valid = r_ < NCH_G[G]
        return ch, valid

    def up_slot(base, G, c):
        ch, valid = ffperm(G, c)
        W = np.zeros((D, 128), np.float32)
        W[:, valid] = w_up[:, base + ch[valid]]
        return _chunk(W, 0)

    for G in range(6):
        grp([up_slot(0, G, c) for c in range(4)])
        grp([up_slot(DFF, G, c) for c in range(4)])
    wst = np.ascontiguousarray(np.stack(groups, 0), dtype=np.float32)
    wd = inp["w_down"][0]
    wdn = np.ascontiguousarray(np.stack([_chunk(wd, 128 * dc) for dc in range(8)], 0),
                               dtype=np.float32)

    cA = np.zeros((128, NCA), np.float32)
    cA[:, C_G1:C_G1 + 8] = _col(inp["norm1_g"][0])
    cA[:, C_G2:C_G2 + 8] = _col(inp["norm2_g"][0])
    cA[:, C_GF:C_GF + 8] = _col(inp["normf_g"])
    cA[:, C_GAMA:C_GAMA + 8] = _col(inp["sgu_ln_g"][0])
    cA[:, C_CMB:C_CMB + 8] = _col(inp["cm_dw_b"][0])
    cA[:, C_CMG:C_CMG + 8] = _col(inp["cm_ln_g"][0])
    cA[:, C_CMBT:C_CMBT + 8] = _col(inp["cm_ln_b"][0])
    cmw = inp["cm_dw_w"][0]
    cA[:, C_CMW:C_CMW + 248] = cmw[:, zperm].reshape(31, 8, 128).transpose(2, 1, 0).reshape(128, 248)
    cA[:, C_FDB:C_FDB + 44] = _col(inp["ffn_dw_b"][0])
    fdw = inp["ffn_dw_w"][0]
    for half in range(2):
        for G in range(6):
            for c in range(4):
                ch, valid = ffperm(G, c)
                col = C_FDW + (half * 6 + G) * 12 + c * 3
                cA[valid, col:col + 3] = fdw[:, half * DFF + ch[valid]].T
    cA[pidx, C_I32 + pidx % 32] = 1.0
    cA[:, C_ID:C_ID + 128] = np.eye(128, dtype=np.float32)

    cB = np.zeros((128, NCB), np.float32)
    cB[:, B_SGW:B_SGW + 1024] = inp["sgu_w"][0].transpose(2, 0, 1).reshape(128, 1024)
    pos = np.arange(128) // 64
    cB[:, B_MASK:B_MASK + 128] = (pos[:, None] <= pos[None, :]).astype(np.float32)
    cB[:, B_BETA:B_BETA + 1024] = np.broadcast_to(inp["sgu_ln_b"][0][None, :], (128, 1024))
    cB[:, B_SGB:B_SGB + 1024] = np.broadcast_to(inp["sgu_b"][0].reshape(1, 1024), (128, 1024))
    return wst, wdn, cA, cB


def build_program():
    nc = bass.Bass("TRN2", target_bir_lowering=False)
    xT_d = nc.dram_tensor("xT", [D, NTOK], F32, kind="ExternalInput").ap()
    wst_d = nc.dram_tensor("wst", [NGROUPS, 128, 4096], F32, kind="ExternalInput").ap()
    wdn_d = nc.dram_tensor("wdn", [8, 128, DFF], F32, kind="ExternalInput").ap()
    cA_d = nc.dram_tensor("cA", [128, NCA], F32, kind="ExternalInput").ap()
    cB_d = nc.dram_tensor("cB", [128, NCB], F32, kind="ExternalInput").ap()
    out_d = nc.dram_tensor("outT", [D, SEQ_CORE], F32, kind="ExternalOutput").ap()
    xT_v = xT_d.rearrange("(k p) c -> p k c", p=128)
    out_v = out_d.rearrange("(k p) c -> p k c", p=128)

    S = Sched()
    with ExitStack() as es:
        ARENA_BYTES = 206 * 1024
        arena = es.enter_context(nc.sbuf_tensor("arena", [128, ARENA_BYTES // 2], BF16))
        ps = [es.enter_context(nc.psum_tensor(f"ps{i}", [128, 512], F32)) for i in range(8)]
        eng_sems = {e: es.enter_context(nc.semaphore(f"s_{e}")) for e in ENGS}
        dma_keys = ["x", "x0", "x1", "x2", "xs", "out", "cA", "cB"] + [("ring", s) for s in range(RING)]
        dma_sems = {k: es.enter_context(nc.semaphore("d_" + (k if isinstance(k, str) else f"ring{k[1]}")))
                    for k in dma_keys}
        block = es.enter_context(nc.Block())

        def view(off, dt, K, C, name):
            esz = 4 if dt == F32 else 2
            nb = K * C * esz
            assert off % 4 == 0 and off + nb <= ARENA_BYTES, (name, off, nb)
            v = arena[:, off // 2:(off + nb) // 2]
            if dt == F32:
                v = v.bitcast(F32)
            v = v.rearrange("p (k c) -> p k c", k=K)
            return Buf(name, v, off, esz, K, C)

        o = 0
        R_A = o; o += 8 * NB * 4
        R_H = o; o += 8 * NB * 2
        R_C = o; o += 24 * NB * 2
        R_DG = o; o += 16 * 128 * 4
        R_SQ = o; o += 8 * 512 * 2
        R_RING = o; o += RING * 8192
        R_TMP = o; o += 16384
        R_DF = o; o += 24 * 32 * 2
        R_UPB = o; o += 8 * 1032 * 2
        R_K = o
        A = view(R_A, F32, 8, NB, "A")
        NBLK = view(R_A, BF16, 8, 1024, "nblk")
        GV = view(R_A + 16384, F32, 2, 1024, "gv")
        DIAGB = view(R_A, BF16, 248, 32, "diagB")
        TN2 = view(R_A + 24576, F32, 4, 512, "tn2")
        HT = view(R_H, BF16, 8, NB, "hT")
        U = view(R_C, BF16, 8, NB, "u")
        Z = view(R_C + 8 * NB * 2, BF16, 8, NB, "z")
        Y = view(R_C + 16 * NB * 2, BF16, 8, NB, "y")
        HH = view(R_C, BF16, 22, 1024, "hh")
        CB = view(R_C + 16 * NB * 2, F32, 1, NCB, "cB")
        DG = view(R_DG, F32, 16, 128, "dg")
        SQ = view(R_SQ, BF16, 8, 512, "sq")
        R12 = view(R_TMP, F32, 4, 512, "r12")
        RINGB = [view(R_RING + s * 8192, BF16, 1, 4096, f"ring{s}") for s in range(RING)]
        SGT = view(R_TMP, F32, 4, 512, "sgt")
        T12 = view(R_TMP + 8192, F32, 4, 512, "t12")
        DIAGF = view(R_DF, BF16, 24, 32, "diagF")
        UPB = view(R_UPB, BF16, 8, 1032, "upb")
        XS = view(R_UPB, F32, 8, 512, "xs")
        k = R_K
        CA = view(k, F32, 1, NCA, "cA"); k += NCA * 4
        IDB = view(k, BF16, 1, 128, "identb"); k += 256
        ONESB = view(k, BF16, 1, 128, "onesb"); k += 256
        WGT = view(k, BF16, 8, 128, "wgT"); k += 2048
        TT = view(k, F32, 8, 128, "T"); k += 4096
        MHALF = view(k, F32, 1, 8, "mhalf"); k += 32
        SM = view(k, F32, 8, 32, "sm"); k += 1024
        SMH = view(k, F32, 2, 2, "smh"); k += 64
        HPROD = view(k, F32, 8, 31, "hprod"); k += 1024
        HY = view(k, F32, 8, 2, "hy"); k += 64
        HYF = view(k, F32, 8, 2, "hyf"); k += 64
        HYH = view(k, BF16, 8, 2, "hyh"); k += 32
        HYL = view(k, BF16, 8, 2, "hyl"); k += 32
        I32B = view(k, BF16, 1, 32, "i32b"); k += 64
        ONESF = view(k, F32, 1, 128, "onesf"); k += 512
        ZC = view(k, BF16, 8, 32, "zc"); k += 512
        UPC = view(k, BF16, 48, 2, "upc"); k += 256
        STS = view(k, F32, 2, 12, "sts"); k += 128
        MV = view(k, F32, 2, 4, "mv"); k += 64
        assert k <= ARENA_BYTES, k

        def cacol(c):
            return CA.ap[:, 0, c:c + 1]

        bank_ctr = [0]
        reserved = set()

        def nb():
            while True:
                b = bank_ctr[0] % 8
                bank_ctr[0] += 1
                if b not in reserved:
                    return b

        def pe_mm(out, lhsT, rhs, start, stop, reads, bank, tp=None):
            if tp is None:
                S.add("pe", lambda h: h.matmul(out, lhsT=lhsT, rhs=rhs, start=start, stop=stop),
                      reads=reads, writes=[("ps", bank)])
            else:
                S.add("pe", lambda h: h.matmul(out, lhsT=lhsT, rhs=rhs, start=start, stop=stop, tile_position=tp),
                      reads=reads, writes=[("ps", bank)])

        def act(out, in_, func, reads, writes, scale=None, bias=None):
            kw = {}
            if scale is not None:
                kw["scale"] = scale
            if bias is not None:
                kw["bias"] = bias
            S.add("act", lambda h: h.activation(out=out, in_=in_, func=func, **kw), reads=reads, writes=writes)

        def tt(eng, out, in0, in1, op, reads, writes):
            S.add(eng, lambda h: h.tensor_tensor(out=out, in0=in0, in1=in1, op=op), reads=reads, writes=writes)

        def ts(eng, out, in0, s1, s2, op0, op1, reads, writes):
            if s2 is None:
                S.add(eng, lambda h: h.tensor_scalar(out=out, in0=in0, scalar1=s1, scalar2=None, op0=op0),
                      reads=reads, writes=writes)
            else:
                S.add(eng, lambda h: h.tensor_scalar(out=out, in0=in0, scalar1=s1, scalar2=s2, op0=op0, op1=op1),
                      reads=reads, writes=writes)

        def stt(eng, out, in0, scalar, in1, op0, op1, reads, writes):
            S.add(eng, lambda h: h.scalar_tensor_tensor(out=out, in0=in0, scalar=scalar, in1=in1, op0=op0, op1=op1),
                  reads=reads, writes=writes)

        def cp(eng, out, in_, reads, writes):
            S.add(eng, lambda h: h.tensor_copy(out=out, in_=in_), reads=reads, writes=writes)

        def memset(eng, buf, val):
            S.add(eng, lambda h: h.memset(buf.ap[:, :, :], val), writes=[buf.rng()])

        stream = []
        for p in range(2):
            for g in range(NGROUPS):
                stream.append((wst_d[g], 4096))
            for rep in range(1 if p == 0 else 2):
                for dc in range(8):
                    stream.append((wdn_d[dc], DFF))
        issued = [0]

        def ring_issue(upto):
            while issued[0] <= min(upto, len(stream) - 1):
                L = issued[0]
                s = L % RING
                src, w = stream[L]
                dst = RINGB[s].ap[:, 0, 0:w]
                S.add("pool", (lambda dst, src: lambda h: h.dma_start(out=dst, in_=src))(dst, src),
                      writes=[RINGB[s].rng()], dma_key=("ring", s))
                issued[0] += 1

        load_ctr = [0]

        def ring_next(prefetch=RING - 1):
            L = load_ctr[0]
            load_ctr[0] += 1
            ring_issue(L + prefetch)
            return RINGB[L % RING]

        ring_issue(RING - 1)
        S.add("sp", lambda h: h.dma_start(out=CA.ap[:, 0, :], in_=cA_d), writes=[CA.rng()], dma_key="cA")
        S.add("sp", lambda h: h.dma_start(out=CB.ap[:, 0, :], in_=cB_d), writes=[CB.rng()], dma_key="cB")
        memset("pool", ONESB, 1.0 / 1024.0)
        memset("pool", MHALF, -0.5)
        memset("pool", ONESF, 1.0)
        cp("dve", IDB.ap[:, 0, :], CA.ap[:, 0, C_ID:C_ID + 128], [CA.rng()], [IDB.rng()])
        cp("dve", I32B.ap[:, 0, :], CA.ap[:, 0, C_I32:C_I32 + 32], [CA.rng()], [I32B.rng()])
        sgw = CB.ap[:, 0, B_SGW:B_SGW + 1024].rearrange("p (g i) -> p g i", g=8)
        maskb = CB.ap[:, 0, B_MASK:B_MASK + 128].unsqueeze(1).to_broadcast([128, 8, 128])
        tt("dve", sgw, sgw, maskb, ALU.mult, [CB.rng()], [CB.rng()])
        cp("dve", WGT.ap[:, :, :], sgw, [CB.rng()], [WGT.rng()])
        betaB = CB.ap[:, 0, B_BETA:B_BETA + 1024].rearrange("p (g i) -> p g i", g=8)
        sgb = CB.ap[:, 0, B_SGB:B_SGB + 1024].rearrange("p (g i) -> p g i", g=8)
        for g in range(8):
            b = nb()
            pe_mm(ps[b][:, 0:128], betaB[:, g, :], sgw[:, g, :], True, False, [CB.rng()], b)
            pe_mm(ps[b][:, 0:128], ONESF.ap[0:1, 0, :], sgb[0:1, g, :], False, True, [CB.rng(), ONESF.rng()], b)
            cp("dve", TT.ap[:, g, :], ps[b][:, 0:128], [("ps", b)], [TT.rng(g, g + 1)])

        def token_stats(srcfn, nstat, n, b):
            nq = (n + 127) // 128
            for q in range(nq):
                m = min(128, n - 128 * q)
                for s_ in range(nstat):
                    col = q * nstat + s_
                    for k in range(8):
                        ap_, rd = srcfn(s_, k, 128 * q, m)
                        pe_mm(ps[b][0:m, col:col + 1], ap_, ONESB.ap[:, 0, 0:1], k == 0, k == 7,
                              [ONESB.rng(), rd], b)

        dg_ctr = [0]

        def bcast_diag(colbuf, par, col0, n):
            nq = (n + 127) // 128
            mm = min(128, n)
            slot = dg_ctr[0] % 4
            dg_ctr[0] += 1
            dgv = DG.ap[0:mm, slot * 4:slot * 4 + nq, :]
            idb = CA.ap[0:mm, 0, C_ID:C_ID + 128].unsqueeze(1).to_broadcast([mm, nq, 128])
            cb = colbuf.ap[0:mm, par, col0:col0 + nq].unsqueeze(2).to_broadcast([mm, nq, 128])
            tt("dve", dgv, idb, cb, ALU.mult, [CA.rng(), colbuf.rng(par, par + 1, col0, col0 + nq)],
               [DG.rng(slot * 4, slot * 4 + nq)])
            return slot

        def bcast_mm(slot, n):
            nq = (n + 127) // 128
            b2 = nb()
            for q in range(nq):
                m = min(128, n - 128 * q)
                pe_mm(ps[b2][:, 128 * q:128 * q + m], ONESF.ap[0:m, 0, :], DG.ap[0:m, slot * 4 + q, 0:m], True, True,
                      [ONESF.rng(), DG.rng(slot * 4 + q, slot * 4 + q + 1)], b2)
            return b2

        sm_ctr = [0]

        def rms_a1(c0, n, src=None, sc0=None):
            src = A if src is None else src
            sc0 = c0 if sc0 is None else sc0
            act(SQ.ap[:, :, 0:n], src.ap[:, :, sc0:sc0 + n], AF.Square, [src.rng(c0=sc0, c1=sc0 + n)], [SQ.rng(c0=0, c1=n)])

        def rms_a2(c0, n):
            b = nb()
            token_stats(lambda s_, k, t0, m: (SQ.ap[:, k, t0:t0 + m], SQ.rng(k, k + 1, t0, t0 + m)), 1, n, b)
            nq = (n + 127) // 128
            mm = min(128, n)
            par = sm_ctr[0] % 8
            sm_ctr[0] += 1
            ts("dve", SM.ap[0:mm, par, 0:nq], ps[b][0:mm, 0:nq], EPS, None, ALU.add, None, [("ps", b)], [SM.rng(par, par + 1, 0, nq)])
            tt("pool", SM.ap[0:mm, par, 4:4 + nq], SM.ap[0:mm, par, 0:nq], MHALF.ap[0:mm, 0, 0:nq], ALU.pow,
               [SM.rng(par, par + 1, 0, nq), MHALF.rng()], [SM.rng(par, par + 1, 4, 4 + nq)])
            return bcast_diag(SM, par, 4, n)

        def rms_b(gc, dst, c0, n, slot, src=None, sc0=None):
            src = A if src is None else src
            sc0 = c0 if sc0 is None else sc0
            b2 = bcast_mm(slot, n)
            for k in range(8):
                stt("dve", dst.ap[:, k, c0:c0 + n], src.ap[:, k, sc0:sc0 + n], cacol(gc + k), ps[b2][:, 0:n],
                    ALU.mult, ALU.mult,
                    [src.rng(k, k + 1, sc0, sc0 + n), CA.rng(), ("ps", b2)],
                    [dst.rng(k, k + 1, c0, c0 + n)])

        def rmsnorm_tile(gc, dst, c0, n):
            rms_a1(c0, n)
            slot = rms_a2(c0, n)
            rms_b(gc, dst, c0, n, slot)

        deferred = []

        def flush_deferred(nmax=1):
            for _ in range(nmax):
                if deferred:
                    deferred.pop(0)()

        def ring4(slot):
            return slot.ap[:, 0, :].rearrange("p (c k m) -> p c k m", c=4, k=8)

        for p in range(2):
            MT = [(128, 512), (640, 512)]
            ET = ([(126, 2)] if p == 0 else []) + MT
            ZT = ([(96, 32)] if p == 0 else []) + MT
            XT = ([(0, 128)] if p == 0 else []) + MT
            lo = 0 if p == 0 else 128
            doff = 1024 * p

            def load_x(lo=lo, doff=doff):
                S.add("sp", lambda h, lo=lo, doff=doff: h.dma_start(out=A.ap[:, :, lo:NB], in_=xT_v[:, :, lo + doff:NB + doff]),
                      writes=[A.rng(c0=lo, c1=NB)], dma_key="x")

            if p == 0:
                for ti, (c0, n) in enumerate(XT):
                    S.add("sp", (lambda c0, n: lambda h: h.dma_start(out=A.ap[:, :, c0:c0 + n], in_=xT_v[:, :, c0:c0 + n]))(c0, n),
                          writes=[A.rng(c0=c0, c1=c0 + n)], dma_key=f"x{ti}")
                for (c0, n) in XT:
                    rmsnorm_tile(C_G1, HT, c0, n)

            zpar = 0
            for g in range(4):
                slot = ring_next()
                w4 = ring4(slot)
                if g == 2:
                    while deferred:
                        flush_deferred()
                    for h2 in range(2):
                        dg = DIAGB.ap[:, h2 * 124:(h2 + 1) * 124, :]
                        i32b = CA.ap[:, 0, C_I32:C_I32 + 32].unsqueeze(1).to_broadcast([128, 124, 32])
                        wbc = CA.ap[:, 0, C_CMW + h2 * 124:C_CMW + (h2 + 1) * 124].unsqueeze(2).to_broadcast([128, 124, 32])
                        tt("pool", dg, i32b, wbc, ALU.mult, [CA.rng()], [DIAGB.rng(h2 * 124, (h2 + 1) * 124)])
                units = [(i, t) for i in range(2) for t in ZT]
                if g == 0:
                    units = [(i, t) for t in ZT for i in range(2)]
                for (i, (c0, n)) in units:
                    c = 2 * g + i
                    if True:
                        bb = nb()
                        for k in range(8):
                            pe_mm(ps[bb][:, 0:n], w4[:, 2 * i + 1, k, :], HT.ap[:, k, c0:c0 + n], k == 0, k == 7,
                                  [slot.rng(), HT.rng(k, k + 1, c0, c0 + n)], bb)
                        sp_ = zpar % 4
                        zpar += 1
                        act(SGT.ap[:, sp_, 0:n], ps[bb][:, 0:n], AF.Sigmoid, [("ps", bb)], [SGT.rng(sp_, sp_ + 1, 0, n)])
                        ba = nb()
                        for k in range(8):
                            pe_mm(ps[ba][:, 0:n], w4[:, 2 * i, k, :], HT.ap[:, k, c0:c0 + n], k == 0, k == 7,
                                  [slot.rng(), HT.rng(k, k + 1, c0, c0 + n)], ba)
                        tt("dve", Z.ap[:, c, c0:c0 + n], ps[ba][:, 0:n], SGT.ap[:, sp_, 0:n], ALU.mult,
                           [("ps", ba), SGT.rng(sp_, sp_ + 1, 0, n)], [Z.rng(c, c + 1, c0, c0 + n)])
                        flush_deferred()
            if p == 0:
                cp("pool", ZC.ap[:, :, :], Z.ap[:, :, NB - 32:NB], [Z.rng(c0=NB - 32, c1=NB)], [ZC.rng()])
            else:
                cp("pool", Z.ap[:, :, 96:128], ZC.ap[:, :, :], [ZC.rng()], [Z.rng(c0=96, c1=128)])

            ln_par = {}

            def conv_unit(h2, c0, n):
                bks = [nb() for _ in range(4)]
                for k in range(31):
                    s0 = c0 - 30 + k
                    for r in range(4):
                        for c in range(4):
                            di = h2 * 124 + c * 31 + k
                            pe_mm(ps[bks[r]][32 * c:32 * c + 32, 0:n], DIAGB.ap[32 * r:32 * r + 32, di, :],
                                  Z.ap[32 * r:32 * r + 32, 4 * h2 + c, s0:s0 + n], k == 0, k == 30,
                                  [DIAGB.rng(di, di + 1), Z.rng(4 * h2 + c, 4 * h2 + c + 1, s0, s0 + n)], bks[r],
                                  tp=(32 * r, 32 * c))
                for r in range(4):
                    ch = 4 * h2 + r
                    act(Y.ap[:, ch, c0:c0 + n], ps[bks[r]][:, 0:n], AF.Identity, [("ps", bks[r]), CA.rng()],
                        [Y.rng(ch, ch + 1, c0, c0 + n)], bias=cacol(C_CMB + ch))

            def halo_conv():
                cmw3 = CA.ap[:, 0, C_CMW:C_CMW + 248].rearrange("p (s k) -> p s k", s=8)
                for t_idx, t in enumerate((126, 127)):
                    tt("dve", HPROD.ap[:, :, :], Z.ap[:, :, t - 30:t + 1], cmw3, ALU.mult,
                       [Z.rng(c0=t - 30, c1=t + 1), CA.rng()], [HPROD.rng()])
                    S.add("dve", (lambda o_, i_: lambda h: h.tensor_reduce(out=o_, in_=i_, axis=mybir.AxisListType.X, op=ALU.add))(
                        HY.ap[:, :, t_idx], HPROD.ap[:, :, :]), reads=[HPROD.rng()], writes=[HY.rng()])
                cp("dve", HYH.ap[:, :, :], HY.ap[:, :, :], [HY.rng()], [HYH.rng()])
                cp("dve", HYF.ap[:, :, :], HYH.ap[:, :, :], [HYH.rng()], [HYF.rng()])
                tt("dve", HYL.ap[:, :, :], HY.ap[:, :, :], HYF.ap[:, :, :], ALU.subtract, [HY.rng(), HYF.rng()], [HYL.rng()])
                for h2 in range(2):
                    bks = [nb() for _ in range(4)]
                    for r in range(4):
                        for c in range(4):
                            for part, (src, first) in enumerate(((HYH, True), (HYL, False))):
                                pe_mm(ps[bks[r]][32 * c:32 * c + 32, 0:2], I32B.ap[32 * r:32 * r + 32, 0, :],
                                      src.ap[32 * r:32 * r + 32, 4 * h2 + c, 0:2], first, not first,
                                      [I32B.rng(), src.rng()], bks[r], tp=(32 * r, 32 * c))
                    for r in range(4):
                        ch = 4 * h2 + r
                        act(Y.ap[:, ch, 126:128], ps[bks[r]][:, 0:2], AF.Identity, [("ps", bks[r]), CA.rng()],
                            [Y.rng(ch, ch + 1, 126, 128)], bias=cacol(C_CMB + ch))

            def ln_sq(c0, n):
                act(SQ.ap[:, :, 0:n], Y.ap[:, :, c0:c0 + n], AF.Square, [Y.rng(c0=c0, c1=c0 + n)], [SQ.rng(c0=0, c1=n)])

            def ln_stats(ti, c0, n):
                b = nb()

                def srcfn(s_, k, t0, m, c0=c0):
                    if s_ == 0:
                        return Y.ap[:, k, c0 + t0:c0 + t0 + m], Y.rng(k, k + 1, c0 + t0, c0 + t0 + m)
                    return SQ.ap[:, k, t0:t0 + m], SQ.rng(k, k + 1, t0, t0 + m)

                token_stats(srcfn, 2, n, b)
                nq = (n + 127) // 128
                mm = min(128, n)
                par = sm_ctr[0] % 8
                sm_ctr[0] += 1
                cp("dve", SM.ap[0:mm, par, 0:2 * nq], ps[b][0:mm, 0:2 * nq], [("ps", b)], [SM.rng(par, par + 1, 0, 2 * nq)])
                st2 = SM.ap[0:mm, par, 0:2 * nq].rearrange("p (q s) -> p q s", s=2)
                mean, e2 = st2[:, :, 0], st2[:, :, 1]
                tt("dve", SM.ap[0:mm, par, 8:8 + nq], mean, mean, ALU.mult, [SM.rng(par, par + 1, 0, 8)], [SM.rng(par, par + 1, 8, 8 + nq)])
                tt("dve", SM.ap[0:mm, par, 12:12 + nq], e2, SM.ap[0:mm, par, 8:8 + nq], ALU.subtract,
                   [SM.rng(par, par + 1, 0, 12)], [SM.rng(par, par + 1, 12, 12 + nq)])
                ts("dve", SM.ap[0:mm, par, 12:12 + nq], SM.ap[0:mm, par, 12:12 + nq], 0.0, EPS, ALU.max, ALU.add,
                   [SM.rng(par, par + 1, 12, 12 + nq)], [SM.rng(par, par + 1, 12, 12 + nq)])
                tt("pool", SM.ap[0:mm, par, 16:16 + nq], SM.ap[0:mm, par, 12:12 + nq], MHALF.ap[0:mm, 0, 0:nq], ALU.pow,
                   [SM.rng(par, par + 1, 12, 16), MHALF.rng()], [SM.rng(par, par + 1, 16, 16 + nq)])
                stt("dve", SM.ap[0:mm, par, 20:20 + nq], mean, -1.0, SM.ap[0:mm, par, 16:16 + nq], ALU.mult, ALU.mult,
                    [SM.rng(par, par + 1, 0, 20)], [SM.rng(par, par + 1, 20, 20 + nq)])
                ln_par[ti] = par

            lnap_ctr = [0]

            def ln_apply(tiles):
                for ti, (c0, n) in tiles:
                    s1 = bcast_diag(SM, ln_par[ti], 16, n)
                    s2 = bcast_diag(SM, ln_par[ti], 20, n)
                    b1 = bcast_mm(s1, n)
                    b2 = bcast_mm(s2, n)
                    if n == 2:
                        r1, r2 = SMH.ap[:, 0, 0:2], SMH.ap[:, 1, 0:2]
                        rr1, rr2 = SMH.rng(0, 1), SMH.rng(1, 2)
                    else:
                        i1, i2 = 2 * (ti % 2), 2 * (ti % 2) + 1
                        r1, r2 = R12.ap[:, i1, 0:n], R12.ap[:, i2, 0:n]
                        rr1, rr2 = R12.rng(i1, i1 + 1, 0, n), R12.rng(i2, i2 + 1, 0, n)
                    act(r1, ps[b1][:, 0:n], AF.Copy, [("ps", b1)], [rr1])
                    act(r2, ps[b2][:, 0:n], AF.Copy, [("ps", b2)], [rr2])
                    for c in range(8):
                        q = lnap_ctr[0] % 4
                        lnap_ctr[0] += 1
                        tt("dve", TN2.ap[:, q, 0:n], Y.ap[:, c, c0:c0 + n], r1, ALU.mult,
                           [Y.rng(c, c + 1, c0, c0 + n), rr1], [TN2.rng(q, q + 1, 0, n)])
                        tt("dve" if c % 4 == 3 else "pool", TN2.ap[:, q, 0:n], TN2.ap[:, q, 0:n], r2, ALU.add,
                           [TN2.rng(q, q + 1, 0, n), rr2], [TN2.rng(q, q + 1, 0, n)])
                        act(Z.ap[:, c, c0:c0 + n], TN2.ap[:, q, 0:n], AF.Silu, [TN2.rng(q, q + 1, 0, n), CA.rng()],
                            [Z.rng(c, c + 1, c0, c0 + n)], scale=cacol(C_CMG + c), bias=cacol(C_CMBT + c))

            ln_pending = None
            for ti, (c0, n) in enumerate(ET):
                if n == 2:
                    halo_conv()
                    ln_sq(c0, n)
                    ln_pending = (ti, c0, n)
                    continue
                conv_unit(0, c0, n)
                if ln_pending is not None:
                    ln_stats(*ln_pending)
                    ln_pending = None
                conv_unit(1, c0, n)
                ln_sq(c0, n)
                ln_pending = (ti, c0, n)
            ln_apply(list(enumerate(ET))[:-1])

            for g in range(2):
                slot = ring_next()
                w4 = ring4(slot)
                for i in range(4):
                    uc = 4 * g + i
                    for (c0, n) in ET:
                        b = nb()
                        for k in range(8):
                            pe_mm(ps[b][:, 0:n], w4[:, i, k, :], HT.ap[:, k, c0:c0 + n], k == 0, k == 7,
                                  [slot.rng(), HT.rng(k, k + 1, c0, c0 + n)], b)
                        act(U.ap[:, uc, c0:c0 + n], ps[b][:, 0:n], AF.Gelu, [("ps", b)], [U.rng(uc, uc + 1, c0, c0 + n)])
                    if g == 0 and i == 0:
                        ln_stats(*ln_pending)
                    if g == 0 and i == 1:
                        ln_apply(list(enumerate(ET))[-1:])

            slot0 = ring_next(prefetch=RING - 1)
            slot1 = ring_next(prefetch=RING - 2)
            wv = [ring4(slot0), ring4(slot1)]
            wslots = [slot0, slot1]
            sgu_q = []
            blocks = ([0] if p == 0 else []) + list(range(1, 9))
            nslot_of = {}
            for bi, blk in enumerate(blocks):
                cb = 128 * blk
                par = bi % 2
                ns = bi % 8
                nslot_of[blk] = ns
                for hh in range(2):
                    b = nb()
                    for k in range(8):
                        pe_mm(ps[b][:, :].rearrange("p (c m) -> p c m", c=4), HT.ap[:, k, cb:cb + 128], wv[hh][:, :, k, :],
                              k == 0, k == 7, [wslots[hh].rng(), HT.rng(k, k + 1, cb, cb + 128)], b)
                    act(GV.ap[:, par, hh * 512:(hh + 1) * 512], ps[b][:, :], AF.Gelu, [("ps", b)],
                        [GV.rng(par, par + 1, hh * 512, (hh + 1) * 512)])
                st3 = STS.ap[:, par, :].rearrange("p (a b) -> p a b", a=2)
                for hh in range(2):
                    S.add("dve", (lambda o_, i_: lambda h: h.bn_stats(out=o_, in_=i_))(st3[:, hh, :], GV.ap[:, par, hh * 512:(hh + 1) * 512]),
                          reads=[GV.rng(par, par + 1, hh * 512, (hh + 1) * 512)], writes=[STS.rng(par, par + 1, hh * 6, hh * 6 + 6)])
                S.add("dve", (lambda o_, i_: lambda h: h.bn_aggr(out=o_, in_=i_))(MV.ap[:, par, 0:2], st3),
                      reads=[STS.rng(par, par + 1)], writes=[MV.rng(par, par + 1, 0, 2)])
                ts("dve", MV.ap[:, par, 2:3], MV.ap[:, par, 1:2], EPS, None, ALU.add, None,
                   [MV.rng(par, par + 1, 0, 2)], [MV.rng(par, par + 1, 2, 3)])
                tt("pool", MV.ap[:, par, 3:4], MV.ap[:, par, 2:3], MHALF.ap[:, 0, 0:1], ALU.pow,
                   [MV.rng(par, par + 1, 2, 3), MHALF.rng()], [MV.rng(par, par + 1, 3, 4)])
                ts("dve", NBLK.ap[:, ns, :], GV.ap[:, par, :], MV.ap[:, par, 0:1], MV.ap[:, par, 3:4], ALU.subtract, ALU.mult,
                   [GV.rng(par, par + 1), MV.rng(par, par + 1)], [NBLK.rng(ns, ns + 1)])
                tile = None
                if p == 0 and blk == 0:
                    tile = (126, 2, [0])
                elif blk in (4, 8):
                    c0 = 128 if blk == 4 else 640
                    tile = (c0, 512, [blk - 3, blk - 2, blk - 1, blk])
                if tile is not None:
                    c0, n, tb = tile
                    for g in range(8):
                        def sgu_item(g=g, c0=c0, n=n, tb=tb, nsl=dict(nslot_of)):
                            b = nb()
                            tq = g % 4
                            if n == 2:
                                pe_mm(ps[b][:, 0:2], NBLK.ap[:, nsl[0], g * 128:(g + 1) * 128], WGT.ap[:, g, 126:128], True, True,
                                      [NBLK.rng(nsl[0], nsl[0] + 1), WGT.rng(g, g + 1)], b)
                                tin1 = TT.ap[:, g, 126:128]
                                pin, tout = ps[b][:, 0:2], T12.ap[:, tq, 0:2]
                                uin, uout = U.ap[:, g, 126:128], U.ap[:, g, 126:128]
                            else:
                                for q, bq in enumerate(tb):
                                    pe_mm(ps[b][:, q * 128:(q + 1) * 128], NBLK.ap[:, nsl[bq], g * 128:(g + 1) * 128], WGT.ap[:, g, :],
                                          True, True, [NBLK.rng(nsl[bq], nsl[bq] + 1), WGT.rng(g, g + 1)], b)
                                tin1 = TT.ap[:, g, :].unsqueeze(1).to_broadcast([128, 4, 128])
                                pin = ps[b][:, :].rearrange("p (a b) -> p a b", a=4)
                                tout = T12.ap[:, tq, :].rearrange("p (a b) -> p a b", a=4)
                                uin = U.ap[:, g, c0:c0 + n].rearrange("p (a b) -> p a b", a=4)
                                uout = uin
                            stt("dve", tout, pin, cacol(C_GAMA + g), tin1, ALU.mult, ALU.add,
                                [("ps", b), CA.rng(), TT.rng(g, g + 1)], [T12.rng(tq, tq + 1, 0, n)])
                            tt("pool", uout, tout, uin, ALU.mult,
                               [T12.rng(tq, tq + 1, 0, n), U.rng(g, g + 1, c0, c0 + n)], [U.rng(g, g + 1, c0, c0 + n)])
                        sgu_q.append(sgu_item)
                    if n == 2:
                        while sgu_q:
                            sgu_q.pop(0)()
                else:
                    for _ in range(2):
                        if sgu_q:
                            sgu_q.pop(0)()
            for _ in range(4):
                if sgu_q:
                    sgu_q.pop(0)()

            sp_ctr = 0
            for dc in range(8):
                slot = ring_next()
                w4 = ring4(slot)
                for (c0, n) in ET:
                    if dc == 0 and (c0, n) == ET[-1]:
                        while sgu_q:
                            sgu_q.pop(0)()
                        load_x()
                    srcs = [HT, HT, U, Z]
                    banks = []
                    for i in range(4):
                        b = nb()
                        banks.append(b)
                        for k in range(8):
                            pe_mm(ps[b][:, 0:n], w4[:, i, k, :], srcs[i].ap[:, k, c0:c0 + n], k == 0, k == 7,
                                  [slot.rng(), srcs[i].rng(k, k + 1, c0, c0 + n)], b)
                        if i < 2:
                            q = (sp_ctr * 2 + i) % 4
                            act(SGT.ap[:, q, 0:n], ps[b][:, 0:n], AF.Sigmoid, [("ps", b)], [SGT.rng(q, q + 1, 0, n)])
                        else:
                            q = (sp_ctr * 2 + i - 2) % 4
                            tq = (sp_ctr * 2 + i - 2) % 4
                            tt("dve", T12.ap[:, tq, 0:n], ps[b][:, 0:n], SGT.ap[:, q, 0:n], ALU.mult,
                               [("ps", b), SGT.rng(q, q + 1, 0, n)], [T12.rng(tq, tq + 1, 0, n)])
                    t1q = (sp_ctr * 2) % 4
                    t2q = (sp_ctr * 2 + 1) % 4
                    tt("pool", Y.ap[:, dc, c0:c0 + n], T12.ap[:, t1q, 0:n], T12.ap[:, t2q, 0:n], ALU.add,
                       [T12.rng(t1q, t1q + 1, 0, n), T12.rng(t2q, t2q + 1, 0, n)], [Y.rng(dc, dc + 1, c0, c0 + n)])
                    sp_ctr += 1

            slots4 = [ring_next(), ring_next(prefetch=RING - 2)]
            pend_a2 = None
            pend_b = None
            for (c0, n) in ET:
                for dc in range(8):
                    slot = slots4[dc // 4]
                    w4 = ring4(slot)
                    b = nb()
                    for k in range(8):
                        pe_mm(ps[b][:, 0:n], w4[:, dc % 4, k, :], Y.ap[:, k, c0:c0 + n], k == 0, k == 7,
                              [slot.rng(), Y.rng(k, k + 1, c0, c0 + n)], b)
                    tt("dve", A.ap[:, dc, c0:c0 + n], ps[b][:, 0:n], A.ap[:, dc, c0:c0 + n], ALU.add,
                       [("ps", b), A.rng(dc, dc + 1, c0, c0 + n)], [A.rng(dc, dc + 1, c0, c0 + n)])
                    if dc == 1 and pend_a2 is not None:
                        pc0, pn = pend_a2
                        pend_b = (pc0, pn, rms_a2(pc0, pn))
                        pend_a2 = None
                    if dc == 5 and pend_b is not None:
                        pc0, pn, sl = pend_b
                        rms_b(C_G2, HT, pc0, pn, sl)
                        pend_b = None
                rms_a1(c0, n)
                pend_a2 = (c0, n)

            for G in range(6):
                ncg = NCH_G[G]
                slots = [ring_next(), ring_next(prefetch=RING - 2)]
                for half in range(2):
                    dgs = DIAGF.ap[:, half * 12:(half + 1) * 12, :]
                    i32b = CA.ap[:, 0, C_I32:C_I32 + 32].unsqueeze(1).to_broadcast([128, 12, 32])
                    t0_ = C_FDW + (half * 6 + G) * 12
                    wbc = CA.ap[:, 0, t0_:t0_ + 12].unsqueeze(2).to_broadcast([128, 12, 32])
                    tt("pool", dgs, i32b, wbc, ALU.mult, [CA.rng()], [DIAGF.rng(half * 12, (half + 1) * 12)])
                for half in range(2):
                    slot = slots[half]
                    w4 = ring4(slot)

                    def up_unit(c, c0, n, half=half, slot=slot, w4=w4):
                        ui = half * 4 + c
                        b = nb()
                        for k in range(8):
                            pe_mm(ps[b][:, 0:n], w4[:, c, k, :], HT.ap[:, k, c0:c0 + n], k == 0, k == 7,
                                  [slot.rng(), HT.rng(k, k + 1, c0, c0 + n)], b)
                        u0 = c0 - 126
                        if n == 2:
                            act(UPB.ap[:, ui, u0:u0 + n], ps[b][:, 0:n], AF.Copy, [("ps", b), CA.rng()],
                                [UPB.rng(ui, ui + 1, u0, u0 + n)], scale=cacol(C_FLAG))
                        elif c % 2 == 1:
                            cp("dve", UPB.ap[:, ui, u0:u0 + n], ps[b][:, 0:n], [("ps", b)],
                               [UPB.rng(ui, ui + 1, u0, u0 + n)])
                        else:
                            act(UPB.ap[:, ui, u0:u0 + n], ps[b][:, 0:n], AF.Copy, [("ps", b)],
                                [UPB.rng(ui, ui + 1, u0, u0 + n)])

                    if G == 0 and half == 0:
                        lc0, ln_ = pend_a2
                        for (c0, n) in ET[:-1]:
                            for c in range(4):
                                up_unit(c, c0, n)
                                if (c0, n) == ET[-2] and c == 0:
                                    pend_b = (lc0, ln_, rms_a2(lc0, ln_))
                                if (c0, n) == ET[-2] and c == 2:
                                    rms_b(C_G2, HT, lc0, ln_, pend_b[2])
                        for c in range(4):
                            up_unit(c, lc0, ln_)
                    else:
                        for c in range(4):
                            for (c0, n) in ET:
                                up_unit(c, c0, n)
                    for c in range(4):
                        ui = half * 4 + c
                        cidx = half * 24 + G * 4 + c
                        if p == 0:
                            cp("pool", UPC.ap[:, cidx, :], UPB.ap[:, ui, 1024:1026], [UPB.rng(ui, ui + 1, 1024, 1026)],
                               [UPC.rng(cidx, cidx + 1)])
                        else:
                            cp("pool", UPB.ap[:, ui, 0:2], UPC.ap[:, cidx, :], [UPC.rng(cidx, cidx + 1)],
                               [UPB.rng(ui, ui + 1, 0, 2)])
                for (c0, n) in MT:
                    u0 = c0 - 126
                    for half in range(2):
                        bks = [nb() for _ in range(ncg)]
                        for k in range(3):
                            s0 = u0 - 2 + k
                            for r in range(ncg):
                                for c in range(4):
                                    di = half * 12 + c * 3 + k
                                    ui = half * 4 + c
                                    pe_mm(ps[bks[r]][32 * c:32 * c + 32, 0:n], DIAGF.ap[32 * r:32 * r + 32, di, :],
                                          UPB.ap[32 * r:32 * r + 32, ui, s0:s0 + n], k == 0, k == 2,
                                          [DIAGF.rng(di, di + 1), UPB.rng(ui, ui + 1, s0, s0 + n)], bks[r],
                                          tp=(32 * r, 32 * c))
                        for r in range(ncg):
                            j = 4 * G + r
                            if half == 0:
                                act(SGT.ap[:, r, 0:n], ps[bks[r]][:, 0:n], AF.Silu, [("ps", bks[r]), CA.rng()],
                                    [SGT.rng(r, r + 1, 0, n)], bias=cacol(C_FDB + j))
                            else:
                                stt("dve", HH.ap[:, j, c0 - 128:c0 - 128 + n], ps[bks[r]][:, 0:n], cacol(C_FDB + 22 + j),
                                    SGT.ap[:, r, 0:n], ALU.add, ALU.mult, [("ps", bks[r]), CA.rng(), SGT.rng(r, r + 1, 0, n)],
                                    [HH.rng(j, j + 1, c0 - 128, c0 - 128 + n)])

            nxt = []
            if p == 0:
                for (c0, n) in MT:
                    def st_load(c0=c0, n=n):
                        S.add("sp", lambda h: h.dma_start(out=XS.ap[:, :, 0:n], in_=xT_v[:, :, c0 + 1024:c0 + 1024 + n]),
                              writes=[XS.rng(c0=0, c1=n)], dma_key="xs")
                        rms_a1(c0, n, src=XS, sc0=0)
                    st = {}

                    def st_a2(c0=c0, n=n, st=st):
                        st["slot"] = rms_a2(c0, n)

                    def st_b(c0=c0, n=n, st=st):
                        rms_b(C_G1, HT, c0, n, st["slot"], src=XS, sc0=0)
                    nxt += [st_load, st_a2, st_b]
            def wdown_unit(slot, dc, c0, n):
                wd3 = slot.ap[:, 0, 0:DFF].rearrange("p (k m) -> p k m", k=22)
                b = nb()
                for k in range(22):
                    pe_mm(ps[b][:, 0:n], wd3[:, k, :], HH.ap[:, k, c0 - 128:c0 - 128 + n], k == 0, k == 21,
                          [slot.rng(), HH.rng(k, k + 1, c0 - 128, c0 - 128 + n)], b)
                tt("dve", A.ap[:, dc, c0:c0 + n], ps[b][:, 0:n], A.ap[:, dc, c0:c0 + n], ALU.add,
                   [("ps", b), A.rng(dc, dc + 1, c0, c0 + n)], [A.rng(dc, dc + 1, c0, c0 + n)])

            if p == 0:
                for dc in range(8):
                    slot = ring_next()
                    for (c0, n) in MT:
                        wdown_unit(slot, dc, c0, n)
                        if nxt and dc >= 1 and (c0, n) == MT[0]:
                            nxt.pop(0)()

            def fin_out(c0, n, doff=doff):
                d0 = c0 - 128 + doff
                S.add("sp", (lambda c0, n, d0: lambda h: h.dma_start(out=out_v[:, :, d0:d0 + n], in_=A.ap[:, :, c0:c0 + n]))(c0, n, d0),
                      reads=[A.rng(c0=c0, c1=c0 + n)], writes=[("dram", "out")], dma_key="out")

            if p == 0:
                for (c0, n) in MT:
                    st = {}

                    def f_a(c0=c0, n=n, st=st):
                        rms_a1(c0, n)
                        st["slot"] = rms_a2(c0, n)

                    def f_b(c0=c0, n=n, st=st):
                        rms_b(C_GF, A, c0, n, st["slot"])
                        fin_out(c0, n)
                    deferred += [f_a, f_b]
                flush_deferred()
            else:
                (ca, na), (cb_, nb_) = MT
                for dc in range(8):
                    wdown_unit(ring_next(), dc, ca, na)
                rms_a1(ca, na)
                st0 = {}
                for dc in range(8):
                    wdown_unit(ring_next(), dc, cb_, nb_)
                    if dc == 1:
                        st0["slot"] = rms_a2(ca, na)
                    if dc == 3:
                        rms_b(C_GF, A, ca, na, st0["slot"])
                        fin_out(ca, na)
                rmsnorm_tile(C_GF, A, cb_, nb_)
                fin_out(cb_, nb_)

        S.add("sp", None, reads=[("dram", "out")])
        S.finalize()
        S.emit(block, eng_sems, dma_sems)
    return nc


_CACHE = {}


def kernel(**inputs):
    inp = {k: np.asarray(v) for k, v in inputs.items()}
    x = inp["x"].astype(np.float32, copy=False)
    wst, wdn, cA, cB = host_layout(inp)
    in_maps = []
    for core in range(NCORES):
        b, s = core // 2, core % 2
        t0 = s * SEQ_CORE
        xT = np.zeros((D, NTOK), np.float32)
        xT[:, HALO:] = x[b, t0:t0 + SEQ_CORE, :].T
        cAc = cA.copy()
        if s == 1:
            xT[:, :HALO] = x[b, t0 - HALO:t0, :].T
            cAc[:, C_FLAG] = 1.0
        in_maps.append({"xT": xT, "wst": wst, "wdn": wdn, "cA": cAc, "cB": cB})
    if "nc" not in _CACHE:
        _CACHE["nc"] = build_program()
    nc = _CACHE["nc"]
    res = run_bass_kernel_spmd(nc, in_maps, core_ids=list(range(NCORES)))
    out = np.empty((4, 4096, D), np.float32)
    for core in range(NCORES):
        b, s = core // 2, core % 2
        t0 = s * SEQ_CORE
        out[b, t0:t0 + SEQ_CORE, :] = res.results[core]["outT"].T
    return out
```
